# Optimizing a Trainium2 kernel written in Bass

```python
import jax, jax.numpy as jnp
from jax import lax
import numpy as np

D_MODEL = 1024
BATCH = 2
SEQ = 8192
DEPTH = 4

M_HEADS = 4
M_HEAD_DIM = 128
M_WIDTH = M_HEADS * M_HEAD_DIM
M_CONV = 5
M_CHUNK = 64
B_HEADS = 8
B_HEAD_DIM = 64
B_WIDTH = B_HEADS * B_HEAD_DIM
DILATED_PAIRS = ((128, 1), (512, 4), (2048, 16))
AB_IN = 2 * M_WIDTH + M_WIDTH + M_WIDTH + 4 * M_HEADS + 3 * B_WIDTH
AB_MIX = M_WIDTH + B_WIDTH
C_HEADS = 16
C_KV_HEADS = 4
C_HEAD_DIM = 64
C_RADIUS = 128
C_IN = (C_HEADS + 2 * C_KV_HEADS) * C_HEAD_DIM
C_MIX = C_HEADS * C_HEAD_DIM
D_FF = 2816
N_EVEN = (DEPTH + 1) // 2
N_ODD = DEPTH // 2
EPS = 1e-6

kernel_name = "hybrid_mlstm_dilated_swa_macaron_encoder"


def rmsnorm(x, g):
    xf = x.astype(jnp.float32)
    y = xf * lax.rsqrt(jnp.mean(xf * xf, axis=-1, keepdims=True) + EPS) * g.astype(jnp.float32)
    return y.astype(x.dtype)


def swiglu(h, w1, w3, w2):
    return (jax.nn.silu(h @ w1) * (h @ w3)) @ w2


def alibi_slopes(n):
    return jnp.exp2(-8.0 * jnp.arange(1, n + 1, dtype=jnp.float32) / n)


def banded_attention(q, k, v, radius, dist_unit, slopes, sink):
    N, L, K, G, E = q.shape
    Q = radius
    n = -(-L // Q)
    Lp = n * Q
    qb = jnp.pad(q, ((0, 0), (0, Lp - L), (0, 0), (0, 0), (0, 0))).reshape(N, n, Q, K, G, E)
    qb = jnp.moveaxis(qb, 1, 0)
    kp = jnp.pad(k, ((0, 0), (Q, Lp - L + Q), (0, 0), (0, 0)))
    vp = jnp.pad(v, ((0, 0), (Q, Lp - L + Q), (0, 0), (0, 0)))
    rel = jnp.arange(3 * Q)[None, :] - Q - jnp.arange(Q)[:, None]
    bias = -slopes[:, :, None, None] * (dist_unit * jnp.abs(rel)).astype(jnp.float32)
    in_window = jnp.abs(rel) <= radius
    scale = E ** -0.5

    def block(args):
        i, qi = args
        ki = lax.dynamic_slice_in_dim(kp, i * Q, 3 * Q, axis=1).astype(jnp.float32)
        vi = lax.dynamic_slice_in_dim(vp, i * Q, 3 * Q, axis=1).astype(jnp.float32)
        s = jnp.einsum('zqhge,zshe->zhgqs', qi.astype(jnp.float32), ki) * scale + bias
        kpos = i * Q - Q + jnp.arange(3 * Q)
        valid = in_window & ((kpos >= 0) & (kpos < L))[None, :]
        s = jnp.where(valid, s, -jnp.inf)
        m = s.max(-1)
        if sink is not None:
            m = jnp.maximum(m, sink[None, :, :, None])
        p = jnp.exp(s - m[..., None])
        den = p.sum(-1)
        if sink is not None:
            den = den + jnp.exp(sink[None, :, :, None] - m)
        o = jnp.einsum('zhgqs,zshe->zqhge', p, vi) / jnp.moveaxis(den, 3, 1)[..., None]
        lse = jnp.moveaxis(m + jnp.log(den), 3, 1)
        return o, lse

    o, lse = lax.map(block, (jnp.arange(n), qb))
    o = jnp.moveaxis(o, 0, 1).reshape(N, Lp, K, G, E)[:, :L]
    lse = jnp.moveaxis(lse, 0, 1).reshape(N, Lp, K, G)[:, :L]
    return o, lse


def to_residue(t, d):
    Bn, S = t.shape[:2]
    t = jnp.moveaxis(t.reshape((Bn, S // d, d) + t.shape[2:]), 2, 1)
    return t.reshape((Bn * d, S // d) + t.shape[3:])


def from_residue(t, Bn, d):
    L = t.shape[1]
    t = jnp.moveaxis(t.reshape((Bn, d, L) + t.shape[2:]), 1, 2)
    return t.reshape((Bn, L * d) + t.shape[3:])


def mlstm_chunkwise(q, k, v, i_pre, logf):
    Bn, H, S, E = q.shape
    L = M_CHUNK
    NC = S // L
    q, k, v = (t.reshape(Bn, H, NC, L, E) for t in (q, k, v))
    i_pre = i_pre.reshape(Bn, H, NC, L)
    b = jnp.cumsum(logf.reshape(Bn, H, NC, L), axis=-1)
    b_last = b[..., -1]
    a = b_last[..., None] - b + i_pre
    m_loc = a.max(-1)
    wa = jnp.exp(a - m_loc[..., None])
    C_chunk = jnp.einsum('bhcs,bhcsd,bhcse->bhcde', wa, v, k)
    n_chunk = jnp.einsum('bhcs,bhcse->bhce', wa, k)

    def step(carry, inp):
        C, n, m = carry
        Cc, nc, mc, bc = inp
        m_new = jnp.maximum(bc + m, mc)
        s_old = jnp.exp(bc + m - m_new)
        s_new = jnp.exp(mc - m_new)
        C_new = s_old[..., None, None] * C + s_new[..., None, None] * Cc
        n_new = s_old[..., None] * n + s_new[..., None] * nc
        return (C_new, n_new, m_new), (C, n, m)

    init = (jnp.zeros((Bn, H, E, E), q.dtype), jnp.zeros((Bn, H, E), q.dtype), jnp.zeros((Bn, H), q.dtype))
    cf = lambda t: jnp.moveaxis(t, 2, 0)
    _, (C_prev, n_prev, m_prev) = lax.scan(step, init, (cf(C_chunk), cf(n_chunk), cf(m_loc), cf(b_last)))
    C_prev, n_prev, m_prev = (jnp.moveaxis(t, 0, 2) for t in (C_prev, n_prev, m_prev))

    lower = jnp.tril(jnp.ones((L, L), dtype=bool))
    dmat = jnp.where(lower, b[..., :, None] - b[..., None, :] + i_pre[..., None, :], -jnp.inf)
    m_inter = b + m_prev[..., None]
    m_t = jnp.maximum(m_inter, dmat.max(-1))
    w = jnp.exp(dmat - m_t[..., None]) * jnp.einsum('bhcte,bhcse->bhcts', q, k)
    s_inter = jnp.exp(m_inter - m_t)
    num = jnp.einsum('bhcts,bhcsd->bhctd', w, v) + s_inter[..., None] * jnp.einsum('bhcde,bhcte->bhctd', C_prev, q)
    den = w.sum(-1) + s_inter * jnp.einsum('bhce,bhcte->bhct', n_prev, q)
    h = num / jnp.maximum(jnp.abs(den), jnp.exp(-m_t))[..., None]
    return h.reshape(Bn, H, S, E)


def mixer_ab(h, w_in, conv_w, conv_b, gate_b, hnorm_g, w_out):
    Bn, S, _ = h.shape
    f32 = jnp.float32
    proj = h @ w_in
    cuts = [2 * M_WIDTH, 3 * M_WIDTH, 4 * M_WIDTH, 4 * M_WIDTH + 4 * M_HEADS,
            4 * M_WIDTH + 4 * M_HEADS + B_WIDTH, 4 * M_WIDTH + 4 * M_HEADS + 2 * B_WIDTH]
    qk_m, v_m, o_m, g_m, q_b, k_b, v_b = jnp.split(proj, cuts, axis=-1)

    qk_m = lax.conv_general_dilated(qk_m, conv_w[:, None, :], (1,), 'SAME',
                                    dimension_numbers=('NWC', 'WIO', 'NWC'),
                                    feature_group_count=2 * M_WIDTH)
    qk_m = jax.nn.silu(qk_m + conv_b)
    q_m, k_m = jnp.split(qk_m, 2, axis=-1)
    heads = lambda t: t.reshape(Bn, S, M_HEADS, M_HEAD_DIM).transpose(0, 2, 1, 3).astype(f32)
    q_m, k_m, v_m = heads(q_m), heads(k_m) * (M_HEAD_DIM ** -0.5), heads(v_m)
    g = g_m.astype(f32).reshape(Bn, S, 2, 2, M_HEADS) + gate_b.astype(f32)
    g = g.transpose(2, 3, 0, 4, 1)
    i_pre, logf = g[0], jax.nn.log_sigmoid(g[1])
    fwd = mlstm_chunkwise(q_m, k_m, v_m, i_pre[0], logf[0])
    rev = lambda t: jnp.flip(t, axis=2)
    bwd = rev(mlstm_chunkwise(rev(q_m), rev(k_m), rev(v_m), rev(i_pre[1]), rev(logf[1])))
    hm = fwd + bwd
    hm = (hm * lax.rsqrt(jnp.mean(hm * hm, axis=-1, keepdims=True) + EPS)
          * hnorm_g.astype(f32).reshape(M_HEADS, 1, M_HEAD_DIM))
    hm = hm.transpose(0, 2, 1, 3).reshape(Bn, S, M_WIDTH) * jax.nn.sigmoid(o_m.astype(f32))

    qb = q_b.reshape(Bn, S, B_HEADS, 1, B_HEAD_DIM)
    kb = k_b.reshape(Bn, S, B_HEADS, B_HEAD_DIM)
    vb = v_b.reshape(Bn, S, B_HEADS, B_HEAD_DIM)
    slopes = alibi_slopes(B_HEADS).reshape(B_HEADS, 1)
    outs, lses = [], []
    for window, dil in DILATED_PAIRS:
        o, lse = banded_attention(to_residue(qb, dil), to_residue(kb, dil), to_residue(vb, dil),
                                  window // (2 * dil), dil, slopes, None)
        outs.append(from_residue(o, Bn, dil))
        lses.append(from_residue(lse, Bn, dil))
    wts = jax.nn.softmax(jnp.stack(lses), axis=0)
    ob = jnp.sum(wts[..., None] * jnp.stack(outs), axis=0).reshape(Bn, S, B_WIDTH)

    mixed = jnp.concatenate([hm, ob], axis=-1).astype(h.dtype)
    return mixed @ w_out


def mixer_c(h, w_in, sink, w_out):
    Bn, S, _ = h.shape
    G = C_HEADS // C_KV_HEADS
    proj = h @ w_in
    q, k, v = jnp.split(proj, [C_HEADS * C_HEAD_DIM, (C_HEADS + C_KV_HEADS) * C_HEAD_DIM], axis=-1)
    q = q.reshape(Bn, S, C_KV_HEADS, G, C_HEAD_DIM)
    k = k.reshape(Bn, S, C_KV_HEADS, C_HEAD_DIM)
    v = v.reshape(Bn, S, C_KV_HEADS, C_HEAD_DIM)
    o, _ = banded_attention(q, k, v, C_RADIUS, 1, alibi_slopes(C_HEADS).reshape(C_KV_HEADS, G),
                            sink.astype(jnp.float32).reshape(C_KV_HEADS, G))
    return o.reshape(Bn, S, C_MIX).astype(h.dtype) @ w_out


def setup_inputs(seed: int = 0) -> dict:
    key = jax.random.key(seed)
    ks = jax.random.split(key, 16)
    nrm = jax.random.normal
    D, F = D_MODEL, D_FF
    x = nrm(ks[0], (BATCH, SEQ, D), jnp.float32)
    norm_g = 1.0 + 0.02 * nrm(ks[1], (DEPTH, 3, D), jnp.float32)
    ffn_w1 = nrm(ks[2], (DEPTH, 2, D, F), jnp.float32) * D ** -0.5
    ffn_w3 = nrm(ks[3], (DEPTH, 2, D, F), jnp.float32) * D ** -0.5
    ffn_w2 = nrm(ks[4], (DEPTH, 2, F, D), jnp.float32) * F ** -0.5
    ab_w_in = nrm(ks[5], (N_EVEN, D, AB_IN), jnp.float32) * D ** -0.5
    ab_conv_w = nrm(ks[6], (N_EVEN, M_CONV, 2 * M_WIDTH), jnp.float32) * M_CONV ** -0.5
    ab_conv_b = 0.02 * nrm(ks[7], (N_EVEN, 2 * M_WIDTH), jnp.float32)
    ig_b = 0.1 * nrm(ks[8], (N_EVEN, 1, 2, M_HEADS), jnp.float32)
    fg_b = jnp.linspace(3.0, 6.0, M_HEADS, dtype=jnp.float32) + 0.1 * nrm(ks[9], (N_EVEN, 1, 2, M_HEADS), jnp.float32)
    ab_gate_b = jnp.concatenate([ig_b, fg_b], axis=1)
    ab_hnorm_g = 1.0 + 0.02 * nrm(ks[10], (N_EVEN, M_WIDTH), jnp.float32)
    ab_w_out = nrm(ks[11], (N_EVEN, AB_MIX, D), jnp.float32) * AB_MIX ** -0.5
    c_w_in = nrm(ks[12], (N_ODD, D, C_IN), jnp.float32) * D ** -0.5
    c_sink = 0.5 * nrm(ks[13], (N_ODD, C_HEADS), jnp.float32)
    c_w_out = nrm(ks[14], (N_ODD, C_MIX, D), jnp.float32) * C_MIX ** -0.5
    final_g = 1.0 + 0.02 * nrm(ks[15], (D,), jnp.float32)
    return {"x": x, "norm_g": norm_g, "ffn_w1": ffn_w1, "ffn_w3": ffn_w3, "ffn_w2": ffn_w2,
            "ab_w_in": ab_w_in, "ab_conv_w": ab_conv_w, "ab_conv_b": ab_conv_b, "ab_gate_b": ab_gate_b,
            "ab_hnorm_g": ab_hnorm_g, "ab_w_out": ab_w_out, "c_w_in": c_w_in, "c_sink": c_sink,
            "c_w_out": c_w_out, "final_g": final_g}


def reference(x, norm_g, ffn_w1, ffn_w3, ffn_w2, ab_w_in, ab_conv_w, ab_conv_b, ab_gate_b,
              ab_hnorm_g, ab_w_out, c_w_in, c_sink, c_w_out, final_g):
    for l in range(DEPTH):
        x = x + 0.5 * swiglu(rmsnorm(x, norm_g[l, 0]), ffn_w1[l, 0], ffn_w3[l, 0], ffn_w2[l, 0])
        h = rmsnorm(x, norm_g[l, 1])
        j = l // 2
        if l % 2 == 0:
            x = x + mixer_ab(h, ab_w_in[j], ab_conv_w[j], ab_conv_b[j], ab_gate_b[j], ab_hnorm_g[j], ab_w_out[j])
        else:
            x = x + mixer_c(h, c_w_in[j], c_sink[j], c_w_out[j])
        x = x + 0.5 * swiglu(rmsnorm(x, norm_g[l, 2]), ffn_w1[l, 1], ffn_w3[l, 1], ffn_w2[l, 1])
    return rmsnorm(x, final_g)
```

```python
import contextlib
import numpy as np
import concourse.bass as bass
import concourse.mybir as mybir
from concourse.bass_utils import run_bass_kernel_spmd

F32 = mybir.dt.float32
BF16 = mybir.dt.bfloat16
AF = mybir.ActivationFunctionType
ALU = mybir.AluOpType
AX = mybir.AxisListType

NCORES = 8
D = 1024
SEQ = 8192
BATCH = 2
TPC = 2048
DFF = 2816
EPS = 1e-6


class Tok:
    __slots__ = ("w", "r")

    def __init__(self):
        self.w = None
        self.r = []


class Sched:
    ENG = ("pe", "dve", "act", "pool", "sp")

    def __init__(self, nc, stack, n_dma_sems=6):
        self.nc = nc
        self.eng = {"pe": nc.tensor, "dve": nc.vector, "act": nc.scalar, "pool": nc.gpsimd, "sp": nc.sync}
        self.ops = {e: [] for e in self.ENG}
        self.cnt = {e: 0 for e in self.ENG}
        self.sem = {e: stack.enter_context(nc.semaphore("s_" + e)) for e in self.ENG}
        self.semobj = dict(self.sem)
        self.waited = {}
        self.dma_sems = {}
        self.dma_cnt = {}
        self.dma_rr = {}
        for q in ("sp", "pool", "act"):
            self.dma_sems[q] = []
            for i in range(n_dma_sems):
                key = "d_%s%d" % (q, i)
                self.semobj[key] = stack.enter_context(nc.semaphore(key))
                self.dma_sems[q].append(key)
                self.dma_cnt[key] = 0
            self.dma_rr[q] = 0
        self.toks = {}

    def tok(self, *key):
        t = self.toks.get(key)
        if t is None:
            t = self.toks[key] = Tok()
        return t

    def _need(self, eng, semkey, val):
        k = (eng, semkey)
        if self.waited.get(k, 0) >= val:
            return
        self.waited[k] = val
        so = self.semobj[semkey]
        self.ops[eng].append(lambda E, so=so, val=val: E.wait_ge(so, val))

    def _deps(self, eng, reads, writes):
        deps = {}
        def add(d):
            if d is None:
                return
            if deps.get(d[0], 0) < d[1]:
                deps[d[0]] = d[1]
        for t in reads:
            add(t.w)
        for t in writes:
            add(t.w)
            for r in t.r:
                add(r)
        for semkey, val in deps.items():
            if eng == "pe" and semkey == "pe":
                continue
            self._need(eng, semkey, val)

    def _mark(self, stamp, reads, writes):
        for t in reads:
            t.r.append(stamp)
        for t in writes:
            t.w = stamp
            t.r = []

    def op(self, eng, fn, reads=(), writes=()):
        self._deps(eng, reads, writes)
        self.cnt[eng] += 1
        so = self.sem[eng]
        self.ops[eng].append(lambda E, fn=fn, so=so: fn(E).then_inc(so, 1))
        self._mark((eng, self.cnt[eng]), reads, writes)

    def dma(self, q, out, in_, reads=(), writes=(), **kw):
        self._deps(q, reads, writes)
        pool = self.dma_sems[q]
        key = pool[self.dma_rr[q] % len(pool)]
        self.dma_rr[q] += 1
        prev = self.dma_cnt[key]
        if prev:
            self._need(q, key, prev)
        self.dma_cnt[key] = prev + 16
        so = self.semobj[key]
        self.ops[q].append(lambda E, so=so, out=out, in_=in_, kw=kw: E.dma_start(out=out, in_=in_, **kw).then_inc(so, 16))
        self._mark((key, prev + 16), reads, writes)

    def finish(self, final_toks):
        for t in final_toks:
            if t.w is not None:
                self._need("sp", t.w[0], t.w[1])
        nc = self.nc
        with nc.Block() as block:
            for e, deco in (("sp", block.sync), ("pe", block.tensor), ("dve", block.vector),
                            ("act", block.scalar), ("pool", block.gpsimd)):
                ops = self.ops[e]
                if not ops:
                    continue

                def body(E, ops=ops):
                    for o in ops:
                        o(E)
                deco(body)


class Ctx:
    def __init__(self, nc):
        self.nc = nc
        self.stack = contextlib.ExitStack()
        self.n = 0

    def sb(self, shape, dt, name=None):
        self.n += 1
        return self.stack.enter_context(self.nc.sbuf_tensor((name or "t") + "_s%d" % self.n, list(shape), dt))

    def ps(self, shape, dt=F32, name=None):
        self.n += 1
        return self.stack.enter_context(self.nc.psum_tensor(name or ("p%d" % self.n), list(shape), dt))


def emit_rmsnorm(S, cx, xT, xT_tok, ntok0, ntok, gcol, xn, xn_tok, ones_bf, ps_ss, ps_tok, scratch):
    sq, rs = scratch["sq"], scratch["rs"]
    sl = slice(ntok0, ntok0 + ntok)
    for k in range(8):
        S.op("act", lambda E, k=k: E.activation(out=sq[:, k, 0:ntok], in_=xT[:, k, sl], func=AF.Square),
             reads=[xT_tok], writes=[scratch["sq_tok"]])
    for k in range(8):
        S.op("pe", lambda E, k=k: E.matmul(ps_ss[:, 0:ntok], ones_bf[:, :], sq[:, k, 0:ntok],
                                           start=(k == 0), stop=(k == 7)),
             reads=[scratch["sq_tok"]], writes=[ps_tok])
    S.op("act", lambda E: E.activation(out=rs[:, 0:ntok], in_=ps_ss[:, 0:ntok], func=AF.Sqrt,
                                       scale=1.0 / D, bias=scratch["eps"][:, 0:1]),
         reads=[ps_tok], writes=[scratch["rs_tok"]])
    S.op("dve", lambda E: E.reciprocal(out=rs[:, 0:ntok], in_=rs[:, 0:ntok]),
         reads=[scratch["rs_tok"]], writes=[scratch["rs_tok"]])
    for k in range(8):
        S.op("dve", lambda E, k=k: E.scalar_tensor_tensor(out=xn[:, k, 0:ntok], in0=xT[:, k, sl],
                                                          scalar=gcol[:, k:k + 1], in1=rs[:, 0:ntok],
                                                          op0=ALU.mult, op1=ALU.mult),
             reads=[xT_tok, scratch["rs_tok"]], writes=[xn_tok])


def build_ffn(final_norm=False):
    nc = bass.Bass("TRN2", target_bir_lowering=False)
    xin = nc.dram_tensor("xT", [D, TPC], F32, kind="ExternalInput").ap()
    g_in = nc.dram_tensor("g", [D], F32, kind="ExternalInput").ap()
    w1 = nc.dram_tensor("w1", [D, DFF], F32, kind="ExternalInput").ap()
    w3 = nc.dram_tensor("w3", [D, DFF], F32, kind="ExternalInput").ap()
    w2 = nc.dram_tensor("w2", [DFF, D], F32, kind="ExternalInput").ap()
    if final_norm:
        gf_in = nc.dram_tensor("gf", [D], F32, kind="ExternalInput").ap()
    xout = nc.dram_tensor("yT", [D, TPC], F32, kind="ExternalOutput").ap()
    NF = DFF // 128
    TP = 1024
    cx = Ctx(nc)
    with cx.stack:
        S = Sched(nc, cx.stack)
        xT = cx.sb([128, 8, TPC], F32, "xT_sb")
        xn = cx.sb([128, 8, TP], BF16, "xn")
        gb = cx.sb([128, NF, TP], BF16, "gb")
        gcol = cx.sb([128, 8], F32, "gcol")
        ones_bf = cx.sb([128, 128], BF16, "ones")
        epsb = cx.sb([128, 1], F32, "eps")
        sq = cx.sb([128, 8, 512], BF16, "sq")
        rs = cx.sb([128, 512], F32, "rs")
        w13 = [cx.sb([128, 2, 8, 128], BF16, "w13_%d" % i) for i in range(3)]
        w2b = [cx.sb([128, NF, 128], BF16, "w2b_%d" % i) for i in range(2)]
        sil = [cx.sb([128, 512], F32, "sil%d" % i) for i in range(2)]
        ps_h = [[cx.ps([128, 512]) for _ in range(2)] for _ in range(2)]
        ps_y = [cx.ps([128, 512]) for _ in range(2)]
        ps_ss = cx.ps([128, 512])
        scratch = {"sq": sq, "rs": rs, "eps": epsb, "sq_tok": S.tok("sq"), "rs_tok": S.tok("rs")}

        S.op("pool", lambda E: E.memset(ones_bf[:, :], 1.0), writes=[S.tok("ones")])
        S.op("pool", lambda E: E.memset(epsb[:, :], EPS), writes=[S.tok("eps")])
        S.dma("sp", gcol[:, :], g_in.rearrange("(k p) -> p k", p=128), writes=[S.tok("gcol")],
              allow_slow_non_contiguous=True)
        xin_v = xin.rearrange("(k p) t -> p k t", p=128)
        xout_v = xout.rearrange("(k p) t -> p k t", p=128)
        xtoks = [S.tok("x", i) for i in range(4)]
        for i in range(4):
            for k in range(0, 8, 4):
                S.dma("sp", xT[:, k:k + 4, i * 512:(i + 1) * 512], xin_v[:, k:k + 4, i * 512:(i + 1) * 512],
                      writes=[xtoks[i]])
        scratch_r = [S.tok("ones"), S.tok("eps"), S.tok("gcol")]
        w1v = w1.rearrange("(k p) f -> p k f", p=128)
        w3v = w3.rearrange("(k p) f -> p k f", p=128)
        w2v = w2.rearrange("(f p) d -> p f d", p=128)
        wq = ["pool", "act"]
        nload = 0
        for pas in range(TPC // TP):
            for gi in range(TP // 512):
                grp = pas * (TP // 512) + gi
                for t in scratch_r:
                    S._deps("act", [t], [])
                    S._deps("dve", [t], [])
                    S._deps("pe", [t], [])
                emit_rmsnorm(S, cx, xT, xtoks[grp], grp * 512, 512, gcol,
                             xn[:, :, gi * 512:(gi + 1) * 512], S.tok("xn", gi), ones_bf, ps_ss,
                             S.tok("ps_ss"), scratch)
            for f in range(NF):
                wb = w13[f % 3]
                wt = S.tok("w13", f % 3)
                S.dma("pool", wb[:, 0, :, :], w1v[:, :, f * 128:(f + 1) * 128], writes=[wt])
                S.dma("pool", wb[:, 1, :, :], w3v[:, :, f * 128:(f + 1) * 128], writes=[wt])
                for gi in range(TP // 512):
                    pb = ps_h[(f * 2 + gi) % 2]
                    pt = [S.tok("ps_h", (f * 2 + gi) % 2, j) for j in range(2)]
                    tsl = slice(gi * 512, (gi + 1) * 512)
                    for j in range(2):
                        for k in range(8):
                            S.op("pe", lambda E, j=j, k=k, pb=pb, wb=wb, tsl=tsl: E.matmul(
                                pb[j][:, :], wb[:, j, k, :], xn[:, k, tsl], start=(k == 0), stop=(k == 7)),
                                reads=[wt, S.tok("xn", gi)], writes=[pt[j]])
                    sb_ = sil[(f * 2 + gi) % 2]
                    st = S.tok("sil", (f * 2 + gi) % 2)
                    S.op("act", lambda E, pb=pb, sb_=sb_: E.activation(out=sb_[:, :], in_=pb[0][:, :], func=AF.Silu),
                         reads=[pt[0]], writes=[st])
                    S.op("dve", lambda E, pb=pb, sb_=sb_, f=f, tsl=tsl: E.tensor_tensor(
                        out=gb[:, f, tsl], in0=sb_[:, :], in1=pb[1][:, :], op=ALU.mult),
                        reads=[st, pt[1]], writes=[S.tok("gb", gi)])
            for d in range(8):
                wb = w2b[d % 2]
                wt = S.tok("w2b", d % 2)
                S.dma("pool", wb[:, 0:11, :], w2v[:, 0:11, d * 128:(d + 1) * 128], writes=[wt])
                S.dma("pool", wb[:, 11:22, :], w2v[:, 11:22, d * 128:(d + 1) * 128], writes=[wt])
                for gi in range(TP // 512):
                    grp = pas * (TP // 512) + gi
                    py = ps_y[(d * 2 + gi) % 2]
                    pyt = S.tok("ps_y", (d * 2 + gi) % 2)
                    tsl = slice(gi * 512, (gi + 1) * 512)
                    for f in range(NF):
                        S.op("pe", lambda E, f=f, py=py, wb=wb, tsl=tsl: E.matmul(
                            py[:, :], wb[:, f, :], gb[:, f, tsl], start=(f == 0), stop=(f == NF - 1)),
                            reads=[wt, S.tok("gb", gi)], writes=[pyt])
                    xs = xT[:, d, grp * 512:(grp + 1) * 512]
                    S.op("dve", lambda E, py=py, xs=xs: E.scalar_tensor_tensor(
                        out=xs, in0=py[:, :], scalar=0.5, in1=xs, op0=ALU.mult, op1=ALU.add),
                        reads=[pyt, xtoks[grp]], writes=[xtoks[grp]])
        outt = S.tok("out")
        for i in range(4):
            for k in range(0, 8, 4):
                S.dma("sp", xout_v[:, k:k + 4, i * 512:(i + 1) * 512], xT[:, k:k + 4, i * 512:(i + 1) * 512],
                      reads=[xtoks[i]], writes=[outt, S.tok("out", i, k)])
        S.finish([S.tok("out", i, k) for i in range(4) for k in (0, 4)])
    return nc


_CACHE = {}


def _get(name, fn, *a):
    key = (name,) + a
    if key not in _CACHE:
        _CACHE[key] = fn(*a)
    return _CACHE[key]


def run_ffn(xT_shards, g, w1, w3, w2):
    nc = _get("ffn", build_ffn)
    in_maps = [{"xT": xT_shards[c], "g": g, "w1": w1, "w3": w3, "w2": w2} for c in range(NCORES)]
    res = run_bass_kernel_spmd(nc, in_maps, core_ids=list(range(NCORES)))
    return [r["yT"] for r in res.results]


PROJ_SPECS = {
    "C": (1536, [("qT", 0, 1024, "fm64", BF16), ("kT", 1024, 256, "fm64", BF16), ("V", 1280, 256, "tm", BF16)]),
    "AB": (3600, [("qkT", 0, 1024, "fm128", F32), ("Vm", 1024, 512, "tm", BF16), ("Om", 1536, 512, "tm", F32),
                  ("Gm", 2048, 16, "tm", F32), ("qbT", 2064, 512, "fm64", BF16), ("kbT", 2576, 512, "fm64", BF16),
                  ("Vb", 3088, 512, "tm", BF16)]),
}


def build_proj(kind):
    NC_, specs = PROJ_SPECS[kind]
    nc = bass.Bass("TRN2", target_bir_lowering=False)
    xin = nc.dram_tensor("xT", [D, TPC], F32, kind="ExternalInput").ap()
    g_in = nc.dram_tensor("g", [D], F32, kind="ExternalInput").ap()
    w = nc.dram_tensor("w", [D, NC_], F32, kind="ExternalInput").ap()
    outs = {}
    for name, c0, ncol, lay, dt in specs:
        shp = [TPC, ncol] if lay == "tm" else [ncol, TPC]
        outs[name] = nc.dram_tensor(name, shp, dt, kind="ExternalOutput").ap()
    cx = Ctx(nc)
    with cx.stack:
        S = Sched(nc, cx.stack)
        xT = cx.sb([128, 8, TPC], F32, "xT_sb")
        xn = cx.sb([128, 8, TPC], BF16, "xn")
        wsb = cx.sb([128, 8, NC_], BF16, "wsb")
        gcol = cx.sb([128, 8], F32, "gcol")
        ones_bf = cx.sb([128, 128], BF16, "ones")
        epsb = cx.sb([128, 1], F32, "eps")
        sq = cx.sb([128, 8, 512], BF16, "sq")
        rs = cx.sb([128, 512], F32, "rs")
        stg_fm = [cx.sb([128, TPC], F32, "stgfm%d" % i) for i in range(2)]
        stg_tm = [cx.sb([128, 512], F32, "stgtm%d" % i) for i in range(3)]
        ps = [cx.ps([128, 512]) for _ in range(6)]
        ps_ss = cx.ps([128, 512])
        scratch = {"sq": sq, "rs": rs, "eps": epsb, "sq_tok": S.tok("sq"), "rs_tok": S.tok("rs")}
        S.op("pool", lambda E: E.memset(ones_bf[:, :], 1.0), writes=[S.tok("ones")])
        S.op("pool", lambda E: E.memset(epsb[:, :], EPS), writes=[S.tok("eps")])
        S.dma("sp", gcol[:, :], g_in.rearrange("(k p) -> p k", p=128), writes=[S.tok("gcol")],
              allow_slow_non_contiguous=True)
        xin_v = xin.rearrange("(k p) t -> p k t", p=128)
        xtoks = [S.tok("x", i) for i in range(4)]
        for i in range(4):
            for k in range(0, 8, 4):
                S.dma("sp", xT[:, k:k + 4, i * 512:(i + 1) * 512], xin_v[:, k:k + 4, i * 512:(i + 1) * 512],
                      writes=[xtoks[i]])
        wv = w.rearrange("(k p) f -> p k f", p=128)
        wtok = S.tok("w")
        for k in range(8):
            S.dma("pool", wsb[:, k, :], wv[:, k, :], writes=[wtok])
        for t in (S.tok("ones"), S.tok("eps"), S.tok("gcol")):
            for e in ("act", "dve", "pe"):
                S._deps(e, [t], [])
        for gi in range(4):
            emit_rmsnorm(S, cx, xT, xtoks[gi], gi * 512, 512, gcol, xn[:, :, gi * 512:(gi + 1) * 512],
                         S.tok("xn", gi), ones_bf, ps_ss, S.tok("ps_ss"), scratch)
        pi = 0
        ei = 0
        si = 0
        finals = []
        for name, c0, ncol, lay, dt in specs:
            if lay == "tm":
                continue
            M = 64 if lay == "fm64" else 128
            for ch in range(ncol // M):
                stg = stg_fm[si % 2]
                stt = S.tok("stgfm", si % 2)
                si += 1
                for gi in range(4):
                    p = ps[pi % 6]
                    pt = S.tok("ps", pi % 6)
                    pi += 1
                    for k in range(8):
                        S.op("pe", lambda E, p=p, k=k, cc=c0 + ch * M, M=M, gi=gi: E.matmul(
                            p[0:M, :], wsb[:, k, cc:cc + M], xn[:, k, gi * 512:(gi + 1) * 512],
                            start=(k == 0), stop=(k == 7)), reads=[wtok, S.tok("xn", gi)], writes=[pt])
                    dst = _stg_view(stg, dt, M, TPC)[:, gi * 512:(gi + 1) * 512]
                    if ei % 2 == 0:
                        S.op("act", lambda E, p=p, dst=dst, M=M: E.copy(out=dst, in_=p[0:M, :]), reads=[pt], writes=[stt])
                    else:
                        S.op("dve", lambda E, p=p, dst=dst, M=M: E.tensor_copy(out=dst, in_=p[0:M, :]), reads=[pt], writes=[stt])
                    ei += 1
                ft = S.tok("fin", name, ch)
                finals.append(ft)
                S.dma("sp", outs[name][ch * M:(ch + 1) * M, :], _stg_view(stg, dt, M, TPC), reads=[stt], writes=[ft])
        ti = 0
        for tt in range(16):
            for name, c0, ncol, lay, dt in specs:
                if lay != "tm":
                    continue
                p = ps[pi % 6]
                pt = S.tok("ps", pi % 6)
                pi += 1
                for k in range(8):
                    S.op("pe", lambda E, p=p, k=k, c0=c0, ncol=ncol, tt=tt: E.matmul(
                        p[:, 0:ncol], xn[:, k, tt * 128:(tt + 1) * 128], wsb[:, k, c0:c0 + ncol],
                        start=(k == 0), stop=(k == 7)), reads=[wtok, S.tok("xn", tt // 4)], writes=[pt])
                stg = stg_tm[ti % 3]
                stt = S.tok("stgtm", ti % 3)
                ti += 1
                dst = _stg_view(stg, dt, 128, ncol)
                if ei % 2 == 0:
                    S.op("act", lambda E, p=p, dst=dst, ncol=ncol: E.copy(out=dst, in_=p[:, 0:ncol]), reads=[pt], writes=[stt])
                else:
                    S.op("dve", lambda E, p=p, dst=dst, ncol=ncol: E.tensor_copy(out=dst, in_=p[:, 0:ncol]), reads=[pt], writes=[stt])
                ei += 1
                ft = S.tok("fin", name, "t", tt)
                finals.append(ft)
                S.dma("sp", outs[name][tt * 128:(tt + 1) * 128, :], dst, reads=[stt], writes=[ft],
                      allow_slow_non_contiguous=(ncol < 64))
        S.finish(finals)
    return nc


def _stg_view(stg, dt, M, n):
    if dt == F32:
        return stg[0:M, 0:n]
    return stg[0:M, :].bitcast(BF16)[:, 0:n]


def run_proj(kind, xT_shards, g, w):
    nc = _get("proj", build_proj, kind)
    in_maps = [{"xT": xT_shards[c], "g": g, "w": w} for c in range(NCORES)]
    res = run_bass_kernel_spmd(nc, in_maps, core_ids=list(range(NCORES)))
    return res.results


def emit_absrel(S, cx, nchunks, halo, radius, name):
    A = cx.sb([128, nchunks, 128], F32, name)
    Ai = cx.sb([128, nchunks, 128], mybir.dt.int32, name + "_i")
    B = cx.sb([128, nchunks, 128], F32, name + "_b")
    t = S.tok(name)
    for c in range(nchunks):
        S.op("pool", lambda E, c=c: E.iota(Ai[:, c, :], [[-1, 128]], base=c * 128 - halo, channel_multiplier=1),
             writes=[t])
    S.op("dve", lambda E: E.tensor_copy(out=A[:, :, :], in_=Ai[:, :, :]), reads=[t], writes=[t])
    S.op("dve", lambda E: E.tensor_scalar(out=B[:, :, :], in0=A[:, :, :], scalar1=-1.0, scalar2=None, op0=ALU.mult),
         reads=[t], writes=[t])
    S.op("dve", lambda E: E.tensor_tensor(out=A[:, :, :], in0=A[:, :, :], in1=B[:, :, :], op=ALU.max),
         reads=[t], writes=[t])
    S.op("dve", lambda E: E.tensor_scalar(out=B[:, :, :], in0=A[:, :, :], scalar1=float(radius) + 0.5, scalar2=1e9,
                                          op0=ALU.is_gt, op1=ALU.mult), reads=[t], writes=[t])
    S.op("dve", lambda E: E.tensor_tensor(out=A[:, :, :], in0=A[:, :, :], in1=B[:, :, :], op=ALU.add),
         reads=[t], writes=[t])
    return A, t


def alibi_slopes(n):
    return [float(2.0 ** (-8.0 * (i + 1) / n)) for i in range(n)]


def build_attn_c():
    nc = bass.Bass("TRN2", target_bir_lowering=False)
    KH = TPC + 256
    q_in = nc.dram_tensor("qT", [1024, TPC], BF16, kind="ExternalInput").ap()
    k_in = nc.dram_tensor("kT", [256, KH], BF16, kind="ExternalInput").ap()
    v_in = nc.dram_tensor("V", [KH, 256], BF16, kind="ExternalInput").ap()
    kb_in = nc.dram_tensor("kb", [128, 18], F32, kind="ExternalInput").ap()
    sink_in = nc.dram_tensor("sink", [16], F32, kind="ExternalInput").ap()
    o_out = nc.dram_tensor("oT", [1024, TPC], BF16, kind="ExternalOutput").ap()
    cx = Ctx(nc)
    slopes = alibi_slopes(16)
    with cx.stack:
        S = Sched(nc, cx.stack)
        qT = cx.sb([64, 16, TPC], BF16, "qT_sb")
        kT = cx.sb([64, 4, KH], BF16, "kT_sb")
        V = cx.sb([128, 18, 256], BF16, "V_sb")
        kb = cx.sb([128, 18], F32, "kb_sb")
        esink = cx.sb([64, 16], F32, "esink")
        ones_bf = cx.sb([128, 64], BF16, "ones")
        bias = cx.sb([128, 3, 16, 128], F32, "bias")
        tmp = [cx.sb([128, 512], F32, "tmp%d" % i) for i in range(3)]
        P = [cx.sb([128, 512], BF16, "P%d" % i) for i in range(3)]
        rden = [cx.sb([64, 512], F32, "rden%d" % i) for i in range(2)]
        ostg = [cx.sb([64, 4, TPC], BF16, "ostg%d" % i) for i in range(2)]
        ps_s = [cx.ps([128, 512]) for _ in range(4)]
        ps_n = [cx.ps([64, 512]) for _ in range(2)]
        ps_d = [cx.ps([64, 512]) for _ in range(2)]
        S.dma("sp", qT[:, :, :], q_in.rearrange("(h e) t -> e h t", e=64), writes=[S.tok("q")])
        S.dma("sp", kT[:, :, :], k_in.rearrange("(h e) t -> e h t", e=64), writes=[S.tok("k")])
        S.dma("sp", V[:, :, :], v_in.rearrange("(c p) f -> p c f", p=128), writes=[S.tok("v")])
        S.dma("sp", kb[:, :], kb_in, writes=[S.tok("kb")])
        S.dma("sp", esink[:, :], sink_in.partition_broadcast(64), writes=[S.tok("esink")])
        S.op("act", lambda E: E.activation(out=esink[:, :], in_=esink[:, :], func=AF.Exp),
             reads=[S.tok("esink")], writes=[S.tok("esink")])
        S.op("pool", lambda E: E.memset(ones_bf[:, :], 1.0), writes=[S.tok("ones")])
        A, At = emit_absrel(S, cx, 3, 128, 128, "absrel")
        bt = S.tok("bias")
        for h in range(16):
            S.op("dve", lambda E, h=h: E.tensor_scalar(out=bias[:, :, h, :], in0=A[:, :, :], scalar1=-slopes[h],
                                                       scalar2=None, op0=ALU.mult), reads=[At], writes=[bt])
        scale = 64 ** -0.5
        it = 0
        finals = []
        for kvh in range(4):
            og = ostg[kvh % 2]
            ogt = S.tok("ostg", kvh % 2)
            for b in range(16):
                pn, pd = ps_n[it % 2], ps_d[it % 2]
                pnt, pdt = S.tok("psn", it % 2), S.tok("psd", it % 2)
                for c in range(3):
                    j = it * 3 + c
                    pss = ps_s[j % 4]
                    pst = S.tok("pss", j % 4)
                    S.op("pe", lambda E, pss=pss, kvh=kvh, b=b, c=c: E.matmul(
                        pss[:, :], kT[:, kvh, (b + c) * 128:(b + c + 1) * 128],
                        qT[:, kvh * 4:(kvh + 1) * 4, b * 128:(b + 1) * 128], start=True, stop=True),
                        reads=[S.tok("q"), S.tok("k")], writes=[pst])
                    tm_, tmt = tmp[j % 3], S.tok("tmp", j % 3)
                    S.op("dve", lambda E, pss=pss, tm_=tm_, kvh=kvh, c=c: E.scalar_tensor_tensor(
                        out=tm_[:, :], in0=pss[:, :], scalar=scale, in1=bias[:, c, kvh * 4:(kvh + 1) * 4, :],
                        op0=ALU.mult, op1=ALU.add), reads=[pst, bt], writes=[tmt])
                    P_, Pt = P[j % 3], S.tok("P", j % 3)
                    S.op("act", lambda E, tm_=tm_, P_=P_, b=b, c=c: E.activation(
                        out=P_[:, :], in_=tm_[:, :], func=AF.Exp, bias=kb[:, b + c:b + c + 1]),
                        reads=[tmt, S.tok("kb")], writes=[Pt])
                    S.op("pe", lambda E, pn=pn, P_=P_, kvh=kvh, b=b, c=c: E.matmul(
                        pn[:, :], V[:, b + c, kvh * 64:(kvh + 1) * 64], P_[:, :], start=(c == 0), stop=(c == 2)),
                        reads=[Pt, S.tok("v")], writes=[pnt])
                    S.op("pe", lambda E, pd=pd, P_=P_, c=c: E.matmul(
                        pd[:, :], ones_bf[:, :], P_[:, :], start=(c == 0), stop=(c == 2)),
                        reads=[Pt, S.tok("ones")], writes=[pdt])
                rd, rdt = rden[it % 2], S.tok("rden", it % 2)
                for g in range(4):
                    h = kvh * 4 + g
                    S.op("dve", lambda E, rd=rd, pd=pd, g=g, h=h: E.tensor_scalar(
                        out=rd[:, g * 128:(g + 1) * 128], in0=pd[:, g * 128:(g + 1) * 128],
                        scalar1=esink[:, h:h + 1], scalar2=None, op0=ALU.add),
                        reads=[pdt, S.tok("esink")], writes=[rdt])
                S.op("dve", lambda E, rd=rd: E.reciprocal(out=rd[:, :], in_=rd[:, :]), reads=[rdt], writes=[rdt])
                S.op("dve", lambda E, rd=rd, pn=pn, og=og, b=b: E.tensor_tensor(
                    out=og[:, :, b * 128:(b + 1) * 128], in0=pn[:, :].rearrange("p (g q) -> p g q", g=4),
                    in1=rd[:, :].rearrange("p (g q) -> p g q", g=4), op=ALU.mult),
                    reads=[pnt, rdt], writes=[ogt])
                it += 1
            ft = S.tok("fin", kvh)
            finals.append(ft)
            S.dma("sp", o_out[kvh * 256:(kvh + 1) * 256, :].rearrange("(g e) t -> e g t", e=64), og[:, :, :],
                  reads=[ogt], writes=[ft])
        S.finish(finals)
    return nc


def run_attn_c(ins):
    nc = _get("attn_c", build_attn_c)
    res = run_bass_kernel_spmd(nc, ins, core_ids=list(range(NCORES)))
    return [r["oT"] for r in res.results]


def _halo(arr, c, halo):
    b, qd = c // 4, c % 4
    t0 = qd * TPC
    out = np.zeros((TPC + 2 * halo, arr.shape[2]), arr.dtype)
    lo, hi = max(0, t0 - halo), min(SEQ, t0 + TPC + halo)
    out[lo - (t0 - halo):hi - (t0 - halo)] = arr[b, lo:hi]
    return out


def _kvalid(c, halo):
    qd = c % 4
    t0 = qd * TPC
    pos = np.arange(t0 - halo, t0 + TPC + halo)
    return np.where((pos >= 0) & (pos < SEQ), 0.0, -30000.0).astype(np.float32)


def prep_attn_c(q, k, v, sink):
    ins = []
    for c in range(NCORES):
        b, qd = c // 4, c % 4
        kh = _halo(k, c, 128)
        vh = _halo(v, c, 128)
        ins.append({"qT": np.ascontiguousarray(q[b, qd * TPC:(qd + 1) * TPC].T),
                    "kT": np.ascontiguousarray(kh.T), "V": vh,
                    "kb": np.ascontiguousarray(_kvalid(c, 128).reshape(18, 128).T), "sink": sink})
    return ins


def build_out():
    nc = bass.Bass("TRN2", target_bir_lowering=False)
    xin = nc.dram_tensor("xT", [D, TPC], F32, kind="ExternalInput").ap()
    m_in = nc.dram_tensor("mT", [D, TPC], BF16, kind="ExternalInput").ap()
    w = nc.dram_tensor("w", [D, D], F32, kind="ExternalInput").ap()
    xout = nc.dram_tensor("yT", [D, TPC], F32, kind="ExternalOutput").ap()
    cx = Ctx(nc)
    with cx.stack:
        S = Sched(nc, cx.stack)
        xT = cx.sb([128, 8, TPC], F32, "xT_sb")
        mT = cx.sb([128, 8, TPC], BF16, "mT_sb")
        wsb = cx.sb([128, 8, D], BF16, "wsb")
        ps = [cx.ps([128, 512]) for _ in range(4)]
        xin_v = xin.rearrange("(k p) t -> p k t", p=128)
        xout_v = xout.rearrange("(k p) t -> p k t", p=128)
        m_v = m_in.rearrange("(k p) t -> p k t", p=128)
        wv = w.rearrange("(k p) f -> p k f", p=128)
        for k in range(8):
            S.dma("pool", wsb[:, k, :], wv[:, k, :], writes=[S.tok("w")])
        for k in range(0, 8, 4):
            S.dma("sp", mT[:, k:k + 4, :], m_v[:, k:k + 4, :], writes=[S.tok("m")])
        xt = [S.tok("x", d) for d in range(8)]
        for d in range(8):
            S.dma("sp", xT[:, d, :], xin_v[:, d, :], writes=[xt[d]])
        finals = []
        i = 0
        for d in range(8):
            for gi in range(4):
                p, pt = ps[i % 4], S.tok("ps", i % 4)
                i += 1
                for k in range(8):
                    S.op("pe", lambda E, p=p, k=k, d=d, gi=gi: E.matmul(
                        p[:, :], wsb[:, k, d * 128:(d + 1) * 128], mT[:, k, gi * 512:(gi + 1) * 512],
                        start=(k == 0), stop=(k == 7)), reads=[S.tok("w"), S.tok("m")], writes=[pt])
                xs = xT[:, d, gi * 512:(gi + 1) * 512]
                S.op("dve", lambda E, p=p, xs=xs: E.tensor_tensor(out=xs, in0=p[:, :], in1=xs, op=ALU.add),
                     reads=[pt, xt[d]], writes=[xt[d]])
            ft = S.tok("fin", d)
            finals.append(ft)
            S.dma("sp", xout_v[:, d, :], xT[:, d, :], reads=[xt[d]], writes=[ft])
        S.finish(finals)
    return nc


def run_out(xT_shards, mT_shards, w):
    nc = _get("out", build_out)
    in_maps = [{"xT": xT_shards[c], "mT": mT_shards[c], "w": w} for c in range(NCORES)]
    res = run_bass_kernel_spmd(nc, in_maps, core_ids=list(range(NCORES)))
    return [r["yT"] for r in res.results]


B_DILS = (1, 4, 16)
B_HALO = 1024


def b_chunks():
    tab = {}
    for d in B_DILS:
        for r in range(d):
            for m in range(16 // d + 1):
                tab[(d, r, m)] = len(tab)
    return tab


def build_attn_b():
    nc = bass.Bass("TRN2", target_bir_lowering=False)
    KH = TPC + 2 * B_HALO
    q_in = nc.dram_tensor("qT", [512, TPC], BF16, kind="ExternalInput").ap()
    k_in = nc.dram_tensor("kT", [512, KH], BF16, kind="ExternalInput").ap()
    v_in = nc.dram_tensor("V", [KH, 512], BF16, kind="ExternalInput").ap()
    kb_in = nc.dram_tensor("kbt", [128, 69], F32, kind="ExternalInput").ap()
    o_out = nc.dram_tensor("oT", [512, TPC], BF16, kind="ExternalOutput").ap()
    cx = Ctx(nc)
    slopes = alibi_slopes(8)
    tab = b_chunks()
    NCH = len(tab)
    with cx.stack:
        S = Sched(nc, cx.stack)
        qT = cx.sb([128, 4, TPC], BF16, "qT_sb")
        kT = cx.sb([128, 4, KH], BF16, "kT_sb")
        V = cx.sb([128, NCH, 512], BF16, "V_sb")
        kb = cx.sb([128, NCH], F32, "kb_sb")
        ones_bf = cx.sb([128, 64], BF16, "ones")
        bias = cx.sb([128, 8, 3, 2, 128], F32, "bias")
        tmp = [cx.sb([128, 2, 128], F32, "tmp%d" % i) for i in range(3)]
        P = [cx.sb([128, 2, 128], BF16, "P%d" % i) for i in range(3)]
        accn = [cx.sb([64, TPC], F32, "accn%d" % i) for i in range(2)]
        accd = [cx.sb([64, TPC], F32, "accd%d" % i) for i in range(2)]
        ostg = [cx.sb([64, TPC], BF16, "ostg%d" % i) for i in range(2)]
        ps_s = [cx.ps([128, 2, 128]) for _ in range(3)]
        ps_n = [cx.ps([64, 128]) for _ in range(2)]
        ps_d = [cx.ps([64, 128]) for _ in range(2)]
        S.dma("sp", qT[:, :, :], q_in.rearrange("(k p) t -> p k t", p=128), writes=[S.tok("q")])
        for k in range(4):
            S.dma("sp", kT[:, k, :], k_in[k * 128:(k + 1) * 128, :], writes=[S.tok("k")])
        S.dma("sp", kb[:, :], kb_in, writes=[S.tok("kb")])
        for d in B_DILS:
            for r in range(d):
                M = 16 // d + 1
                c0 = tab[(d, r, 0)]
                start = B_HALO + r - 64 * d
                src = bass.AP(v_in.tensor, start * 512, [[d * 512, 128], [d * 128 * 512, M], [1, 512]])
                S.dma("sp" if (r % 2 == 0) else "act", V[:, c0:c0 + M, :], src, writes=[S.tok("v")])
        S.op("pool", lambda E: E.memset(ones_bf[:, :], 1.0), writes=[S.tok("ones")])
        A, At = emit_absrel(S, cx, 2, 64, 64, "absrel")
        bt = S.tok("bias")
        for h in range(8):
            for di, d in enumerate(B_DILS):
                S.op("dve", lambda E, h=h, di=di, d=d: E.tensor_scalar(
                    out=bias[:, h, di, :, :], in0=A[:, :, :], scalar1=-slopes[h] * d, scalar2=None, op0=ALU.mult),
                    reads=[At], writes=[bt])
        scale = 64 ** -0.5
        it = 0
        finals = []
        for h in range(8):
            pb, hp = (h % 2) * 64, h // 2
            an, ad = accn[h % 2], accd[h % 2]
            ant, adt = S.tok("accn", h % 2), S.tok("accd", h % 2)
            for di, d in enumerate(B_DILS):
                for r in range(d):
                    for blk in range(16 // d):
                        q0 = r + d * 128 * blk
                        qsl = slice(q0, q0 + d * 127 + 1, d)
                        pss, pst = ps_s[it % 3], S.tok("pss", it % 3)
                        for c in range(2):
                            u0 = B_HALO + r + d * (128 * (blk + c) - 64)
                            S.op("pe", lambda E, pss=pss, c=c, u0=u0, d=d, qsl=qsl, pb=pb, hp=hp: E.matmul(
                                pss[:, c, :], kT[pb:pb + 64, hp, u0:u0 + d * 127 + 1:d], qT[pb:pb + 64, hp, qsl],
                                start=True, stop=True), reads=[S.tok("q"), S.tok("k")], writes=[pst])
                        tm_, tmt = tmp[it % 3], S.tok("tmp", it % 3)
                        S.op("dve", lambda E, pss=pss, tm_=tm_, h=h, di=di: E.scalar_tensor_tensor(
                            out=tm_[:, :, :], in0=pss[:, :, :], scalar=scale, in1=bias[:, h, di, :, :],
                            op0=ALU.mult, op1=ALU.add), reads=[pst, bt], writes=[tmt])
                        P_, Pt = P[it % 3], S.tok("P", it % 3)
                        for c in range(2):
                            ci = tab[(d, r, blk + c)]
                            S.op("act", lambda E, tm_=tm_, P_=P_, c=c, ci=ci: E.activation(
                                out=P_[:, c, :], in_=tm_[:, c, :], func=AF.Exp, bias=kb[:, ci:ci + 1]),
                                reads=[tmt, S.tok("kb")], writes=[Pt])
                        pn, pd = ps_n[it % 2], ps_d[it % 2]
                        pnt, pdt = S.tok("psn", it % 2), S.tok("psd", it % 2)
                        for c in range(2):
                            ci = tab[(d, r, blk + c)]
                            S.op("pe", lambda E, pn=pn, P_=P_, c=c, ci=ci, h=h: E.matmul(
                                pn[:, :], V[:, ci, h * 64:(h + 1) * 64], P_[:, c, :], start=(c == 0), stop=(c == 1)),
                                reads=[Pt, S.tok("v")], writes=[pnt])
                        for c in range(2):
                            S.op("pe", lambda E, pd=pd, P_=P_, c=c: E.matmul(
                                pd[:, :], ones_bf[:, :], P_[:, c, :], start=(c == 0), stop=(c == 1)),
                                reads=[Pt, S.tok("ones")], writes=[pdt])
                        if di == 0:
                            S.op("dve", lambda E, an=an, pn=pn, qsl=qsl: E.tensor_copy(out=an[:, qsl], in_=pn[:, :]),
                                 reads=[pnt], writes=[ant])
                            S.op("act", lambda E, ad=ad, pd=pd, qsl=qsl: E.copy(out=ad[:, qsl], in_=pd[:, :]),
                                 reads=[pdt], writes=[adt])
                        else:
                            S.op("dve", lambda E, an=an, pn=pn, qsl=qsl: E.tensor_tensor(
                                out=an[:, qsl], in0=pn[:, :], in1=an[:, qsl], op=ALU.add), reads=[pnt, ant], writes=[ant])
                            S.op("dve", lambda E, ad=ad, pd=pd, qsl=qsl: E.tensor_tensor(
                                out=ad[:, qsl], in0=pd[:, :], in1=ad[:, qsl], op=ALU.add), reads=[pdt, adt], writes=[adt])
                        it += 1
            og, ogt = ostg[h % 2], S.tok("ostg", h % 2)
            S.op("dve", lambda E, ad=ad: E.reciprocal(out=ad[:, :], in_=ad[:, :]), reads=[adt], writes=[adt])
            S.op("dve", lambda E, an=an, ad=ad, og=og: E.tensor_tensor(out=og[:, :], in0=an[:, :], in1=ad[:, :], op=ALU.mult),
                 reads=[ant, adt], writes=[ogt])
            ft = S.tok("fin", h)
            finals.append(ft)
            S.dma("sp", o_out[h * 64:(h + 1) * 64, :], og[:, :], reads=[ogt], writes=[ft])
        S.finish(finals)
    return nc


def prep_attn_b(q, k, v):
    tab = b_chunks()
    ins = []
    for c in range(NCORES):
        b, qd = c // 4, c % 4
        kh = _halo(k, c, B_HALO)
        vh = _halo(v, c, B_HALO)
        valid = _kvalid(c, B_HALO)
        kbt = np.zeros((128, len(tab)), np.float32)
        for (d, r, m), idx in tab.items():
            u = B_HALO + r + d * (128 * m - 64 + np.arange(128))
            kbt[:, idx] = valid[u]
        ins.append({"qT": np.ascontiguousarray(q[b, qd * TPC:(qd + 1) * TPC].T),
                    "kT": np.ascontiguousarray(kh.T), "V": vh, "kbt": kbt})
    return ins


def run_attn_b(ins):
    nc = _get("attn_b", build_attn_b)
    res = run_bass_kernel_spmd(nc, ins, core_ids=list(range(NCORES)))
    return [r["oT"] for r in res.results]


def build_mlstm():
    nc = bass.Bass("TRN2", target_bir_lowering=False)
    NT = SEQ // 128
    qk_in = nc.dram_tensor("qkT", [256, SEQ], F32, kind="ExternalInput").ap()
    cw_in = nc.dram_tensor("cw", [128, 2, 5], F32, kind="ExternalInput").ap()
    cb_in = nc.dram_tensor("cb", [128, 2], F32, kind="ExternalInput").ap()
    v_in = nc.dram_tensor("Vm", [SEQ, 128], BF16, kind="ExternalInput").ap()
    o_in = nc.dram_tensor("Om", [SEQ, 128], F32, kind="ExternalInput").ap()
    g_in = nc.dram_tensor("G", [SEQ, 4], F32, kind="ExternalInput").ap()
    gb_in = nc.dram_tensor("gb", [4], F32, kind="ExternalInput").ap()
    hg_in = nc.dram_tensor("hg", [128], F32, kind="ExternalInput").ap()
    h_out = nc.dram_tensor("hm", [SEQ, 128], BF16, kind="ExternalOutput").ap()
    I32 = mybir.dt.int32
    cx = Ctx(nc)
    with cx.stack:
        S = Sched(nc, cx.stack)
        qkb = cx.sb([128, 2, SEQ], BF16, "qkb")
        vaug = cx.sb([128, NT, 132], BF16, "vaug")
        SG = cx.sb([128, NT, 128], F32, "SG")
        Hf = cx.sb([128, NT, 128], F32, "Hf")
        ostg = cx.sb([128, NT, 128], BF16, "ostg")
        xin = [cx.sb([128, 2, 1028], F32, "xin%d" % i) for i in range(2)]
        cacc = [cx.sb([128, 2, 1024], F32, "cacc%d" % i) for i in range(2)]
        cw = cx.sb([128, 2, 5], F32, "cw")
        cb = cx.sb([128, 2], F32, "cb")
        G = cx.sb([128, NT, 4], F32, "G")
        gb = cx.sb([128, 4], F32, "gb")
        hg = cx.sb([128, 128], F32, "hg")
        L = cx.sb([128, 2, NT], F32, "L")
        CB = cx.sb([128, 2, NT], F32, "CB")
        TOT = cx.sb([128, 2, NT], F32, "TOT")
        EB = cx.sb([128, 2, NT], F32, "EB")
        NEB = cx.sb([128, 2, NT], F32, "NEB")
        RS = cx.sb([128, 2, NT], F32, "RS")
        EA = cx.sb([128, 2, NT], F32, "EA")
        FF = cx.sb([128, 2, NT], F32, "FF")
        tri_i = cx.sb([128, 128], I32, "tri_i")
        tri = cx.sb([128, 2, 128], F32, "tri")
        onesf = cx.sb([128, 128], F32, "onesf")
        ident = cx.sb([128, 128], BF16, "ident")
        one1 = cx.sb([128, 1], F32, "one1")
        lns = cx.sb([128, 1], F32, "lns")
        epsb = cx.sb([128, 1], F32, "epsb")
        Cst = cx.sb([128, 132], F32, "Cst")
        Cbf = cx.sb([128, 132], BF16, "Cbf")
        WT = [cx.sb([128, 128], BF16, "WT%d" % i) for i in range(2)]
        kpp = [cx.sb([128, 128], BF16, "kpp%d" % i) for i in range(2)]
        sm = [cx.sb([128, 8], F32, "sm%d" % i) for i in range(2)]
        hs = [cx.sb([128, 128], F32, "hs%d" % i) for i in range(2)]
        sqj = cx.sb([128, 128], F32, "sqj")
        ps_s = [cx.ps([128, 128]) for _ in range(2)]
        ps_o = [cx.ps([128, 132]) for _ in range(2)]
        ps_t = [cx.ps([128, 128], BF16) for _ in range(2)]
        ps_c = [cx.ps([128, 132]) for _ in range(2)]

        ct = S.tok("const")
        S.op("pool", lambda E: E.iota(tri_i[:, :], [[1, 128]], base=0, channel_multiplier=-1), writes=[ct])
        S.op("dve", lambda E: E.tensor_copy(out=tri[:, 0, :], in_=tri_i[:, :]), reads=[ct], writes=[ct])
        S.op("dve", lambda E: E.tensor_scalar(out=sqj[:, :], in0=tri[:, 0, :], scalar1=0.0, scalar2=None, op0=ALU.is_equal),
             reads=[ct], writes=[ct])
        S.op("dve", lambda E: E.tensor_copy(out=ident[:, :], in_=sqj[:, :]), reads=[ct], writes=[ct])
        S.op("dve", lambda E: E.tensor_scalar(out=tri[:, 1, :], in0=tri[:, 0, :], scalar1=0.0, scalar2=None, op0=ALU.is_le),
             reads=[ct], writes=[ct])
        S.op("dve", lambda E: E.tensor_scalar(out=tri[:, 0, :], in0=tri[:, 0, :], scalar1=0.0, scalar2=None, op0=ALU.is_ge),
             reads=[ct], writes=[ct])
        S.op("pool", lambda E: E.memset(onesf[:, :], 1.0), writes=[ct])
        S.op("pool", lambda E: E.memset(one1[:, :], 1.0), writes=[ct])
        S.op("pool", lambda E: E.memset(lns[:, :], float(np.log(128.0 ** -0.5))), writes=[ct])
        S.op("pool", lambda E: E.memset(epsb[:, :], EPS), writes=[ct])
        S.op("pool", lambda E: E.memset(Cst[:, :], 0.0), writes=[S.tok("Cst")])
        S.op("pool", lambda E: E.memset(Cbf[:, :], 0.0), writes=[S.tok("Cbf")])
        S.op("pool", lambda E: E.memset(vaug[:, :, 128:132], 1.0), writes=[S.tok("vaug1")])
        S.dma("sp", cw[:, :, :], cw_in, writes=[S.tok("cw")])
        S.dma("sp", cb[:, :], cb_in, writes=[S.tok("cw")])
        S.dma("sp", G[:, :, :], g_in.rearrange("(n p) g -> p n g", p=128), writes=[S.tok("G")], allow_slow_non_contiguous=True)
        S.dma("sp", gb[:, :], gb_in.partition_broadcast(128), writes=[S.tok("gb")])
        S.dma("sp", hg[:, :], hg_in.partition_broadcast(128), writes=[S.tok("hg")])
        vt = S.tok("vaug")
        for i in range(4):
            S.dma("act", vaug[:, i * 16:(i + 1) * 16, 0:128],
                  v_in[i * 2048:(i + 1) * 2048, :].rearrange("(n p) e -> p n e", p=128), writes=[vt])
        sgt = S.tok("SG")
        for i in range(4):
            S.dma("act", SG[:, i * 16:(i + 1) * 16, :],
                  o_in[i * 2048:(i + 1) * 2048, :].rearrange("(n p) e -> p n e", p=128), writes=[sgt])
        gt = S.tok("gates")
        for j in range(4):
            S.op("dve", lambda E, j=j: E.tensor_scalar(out=G[:, :, j], in0=G[:, :, j], scalar1=gb[:, j:j + 1], scalar2=None,
                                                       op0=ALU.add), reads=[S.tok("G"), S.tok("gb")], writes=[S.tok("G")])
        for dr in range(2):
            S.op("act", lambda E, dr=dr: E.activation(out=L[:, dr, :], in_=G[:, :, 2 + dr], func=AF.Exp, scale=-1.0),
                 reads=[S.tok("G")], writes=[gt])
        S.op("act", lambda E: E.activation(out=L[:, :, :], in_=L[:, :, :], func=AF.Ln, bias=one1[:, 0:1]),
             reads=[gt, ct], writes=[gt])
        pcs = ps_o[0]
        for dr in range(2):
            S.op("pe", lambda E, dr=dr: E.matmul(pcs[:, dr * 64:(dr + 1) * 64], tri[:, dr, :], L[:, dr, :], start=True, stop=True),
                 reads=[gt, ct], writes=[S.tok("pso", 0)])
        S.op("dve", lambda E: E.tensor_copy(out=CB[:, :, :], in_=pcs[:, 0:128].rearrange("p (d n) -> p d n", d=2)),
             reads=[S.tok("pso", 0)], writes=[gt])
        pcs1 = ps_o[1]
        S.op("pe", lambda E: E.matmul(pcs1[:, 0:128], onesf[:, :], L[:, :, :], start=True, stop=True),
             reads=[gt, ct], writes=[S.tok("pso", 1)])
        S.op("dve", lambda E: E.tensor_copy(out=TOT[:, :, :], in_=pcs1[:, 0:128].rearrange("p (d n) -> p d n", d=2)),
             reads=[S.tok("pso", 1)], writes=[gt])
        S.op("act", lambda E: E.activation(out=EB[:, :, :], in_=CB[:, :, :], func=AF.Exp, scale=-1.0), reads=[gt], writes=[gt])
        S.op("act", lambda E: E.activation(out=FF[:, :, :], in_=TOT[:, :, :], func=AF.Exp, scale=-1.0), reads=[gt], writes=[gt])
        for dr in range(2):
            S.op("dve", lambda E, dr=dr: E.tensor_tensor(out=RS[:, dr, :], in0=CB[:, dr, :], in1=G[:, :, dr], op=ALU.add),
                 reads=[gt, S.tok("G")], writes=[gt])
        S.op("dve", lambda E: E.tensor_tensor(out=EA[:, :, :], in0=RS[:, :, :], in1=TOT[:, :, :], op=ALU.subtract),
             reads=[gt], writes=[gt])
        S.op("act", lambda E: E.activation(out=RS[:, :, :], in_=RS[:, :, :], func=AF.Exp, bias=lns[:, 0:1]), reads=[gt], writes=[gt])
        S.op("act", lambda E: E.activation(out=EA[:, :, :], in_=EA[:, :, :], func=AF.Exp, bias=lns[:, 0:1]), reads=[gt], writes=[gt])
        qkt = S.tok("qkb")
        NP = SEQ // 1024
        for pc in range(NP):
            xb, xbt = xin[pc % 2], S.tok("xin", pc % 2)
            ac, act_ = cacc[pc % 2], S.tok("cacc", pc % 2)
            t0 = pc * 1024
            lo, hi = max(0, t0 - 2), min(SEQ, t0 + 1026)
            if pc == 0 or pc == NP - 1:
                S.op("pool", lambda E, xb=xb: E.memset(xb[:, :, :], 0.0), writes=[xbt])
            S.dma("sp", xb[:, :, lo - (t0 - 2):hi - (t0 - 2)],
                  qk_in[:, lo:hi].rearrange("(j p) t -> p j t", p=128), writes=[xbt])
            for j in range(2):
                S.op("dve", lambda E, j=j, xb=xb, ac=ac: E.tensor_scalar(
                    out=ac[:, j, :], in0=xb[:, j, 0:1024], scalar1=cw[:, j, 0:1], scalar2=None, op0=ALU.mult),
                    reads=[xbt, S.tok("cw")], writes=[act_])
                for tap in range(1, 5):
                    S.op("dve", lambda E, j=j, xb=xb, ac=ac, tap=tap: E.scalar_tensor_tensor(
                        out=ac[:, j, :], in0=xb[:, j, tap:tap + 1024], scalar=cw[:, j, tap:tap + 1], in1=ac[:, j, :],
                        op0=ALU.mult, op1=ALU.add), reads=[xbt, act_], writes=[act_])
                S.op("act", lambda E, j=j, ac=ac, t0=t0: E.activation(
                    out=qkb[:, j, t0:t0 + 1024], in_=ac[:, j, :], func=AF.Silu, bias=cb[:, j:j + 1]),
                    reads=[act_, S.tok("cw")], writes=[qkt])
        for i in range(4):
            S.op("act", lambda E, i=i: E.activation(out=SG[:, i * 16:(i + 1) * 16, :], in_=SG[:, i * 16:(i + 1) * 16, :],
                                                   func=AF.Sigmoid), reads=[sgt], writes=[sgt])
        it = 0
        vts = [vt, S.tok("vaug1")]
        for dr in range(2):
            order = range(NT) if dr == 0 else range(NT - 1, -1, -1)
            if dr == 1:
                S.op("pool", lambda E: E.memset(Cst[:, :], 0.0), reads=[], writes=[S.tok("Cst")])
                S.op("pool", lambda E: E.memset(Cbf[:, :], 0.0), reads=[], writes=[S.tok("Cbf")])
            for n in order:
                tsl = slice(n * 128, (n + 1) * 128)
                pss, pst = ps_s[it % 2], S.tok("pss", it % 2)
                pso, pot = ps_o[it % 2], S.tok("pso", it % 2)
                ptt, ptk = ps_t[it % 2], S.tok("pst", it % 2)
                psc, pct = ps_c[it % 2], S.tok("psc", it % 2)
                W_, Wt = WT[it % 2], S.tok("WT", it % 2)
                kp, kpt = kpp[it % 2], S.tok("kpp", it % 2)
                s_, st = sm[it % 2], S.tok("sm", it % 2)
                S.op("pe", lambda E, pss=pss, tsl=tsl: E.matmul(pss[:, :], qkb[:, 1, tsl], qkb[:, 0, tsl], start=True, stop=True),
                     reads=[qkt], writes=[pst])
                S.op("dve", lambda E, pss=pss, W_=W_, dr=dr, n=n: E.scalar_tensor_tensor(
                    out=W_[:, :], in0=pss[:, :], scalar=RS[:, dr, n:n + 1], in1=tri[:, dr, :], op0=ALU.mult, op1=ALU.mult),
                    reads=[pst, gt, ct], writes=[Wt])
                S.op("pe", lambda E, pso=pso, W_=W_, n=n: E.matmul(pso[:, 0:130], W_[:, :], vaug[:, n, 0:130], start=True, stop=False),
                     reads=[Wt] + vts, writes=[pot])
                S.op("pe", lambda E, pso=pso, tsl=tsl: E.matmul(pso[:, 0:130], qkb[:, 0, tsl], Cbf[:, 0:130], start=False, stop=True),
                     reads=[qkt, S.tok("Cbf")], writes=[pot])
                S.op("act", lambda E, pso=pso, s_=s_, dr=dr, n=n: E.activation(
                    out=s_[:, 0:1], in_=pso[:, 128:129], func=AF.Abs, scale=EB[:, dr, n:n + 1]), reads=[pot, gt], writes=[st])
                S.op("dve", lambda E, s_=s_: E.tensor_scalar(out=s_[:, 1:2], in0=s_[:, 0:1], scalar1=1.0, scalar2=None, op0=ALU.max),
                     reads=[st], writes=[st])
                S.op("dve", lambda E, s_=s_: E.reciprocal(out=s_[:, 2:3], in_=s_[:, 1:2]), reads=[st], writes=[st])
                S.op("dve", lambda E, s_=s_, dr=dr, n=n: E.tensor_tensor(out=s_[:, 3:4], in0=s_[:, 2:3], in1=EB[:, dr, n:n + 1], op=ALU.mult),
                     reads=[st, gt], writes=[st])
                if dr == 0:
                    S.op("act", lambda E, pso=pso, s_=s_, n=n: E.activation(
                        out=Hf[:, n, :], in_=pso[:, 0:128], func=AF.Copy, scale=s_[:, 3:4]), reads=[pot, st], writes=[S.tok("Hf", n)])
                else:
                    h_, ht = hs[it % 2], S.tok("hs", it % 2)
                    S.op("dve", lambda E, pso=pso, s_=s_, n=n, h_=h_: E.scalar_tensor_tensor(
                        out=h_[:, :], in0=pso[:, 0:128], scalar=s_[:, 3:4], in1=Hf[:, n, :], op0=ALU.mult, op1=ALU.add),
                        reads=[pot, st, S.tok("Hf", n)], writes=[ht])
                    S.op("act", lambda E, h_=h_, s_=s_: E.activation(out=sqj[:, :], in_=h_[:, :], func=AF.Square, accum_out=s_[:, 4:5]),
                         reads=[ht, st], writes=[st, S.tok("sqj")])
                    S.op("act", lambda E, s_=s_: E.activation(out=s_[:, 5:6], in_=s_[:, 4:5], func=AF.Sqrt, scale=1.0 / 128,
                                                            bias=epsb[:, 0:1]), reads=[st, ct], writes=[st])
                    S.op("dve", lambda E, s_=s_: E.reciprocal(out=s_[:, 6:7], in_=s_[:, 5:6]), reads=[st], writes=[st])
                    S.op("dve", lambda E, h_=h_, s_=s_: E.scalar_tensor_tensor(
                        out=h_[:, :], in0=h_[:, :], scalar=s_[:, 6:7], in1=hg[:, :], op0=ALU.mult, op1=ALU.mult),
                        reads=[ht, st, S.tok("hg")], writes=[ht])
                    S.op("dve", lambda E, h_=h_, n=n: E.tensor_tensor(out=ostg[:, n, :], in0=h_[:, :], in1=SG[:, n, :], op=ALU.mult),
                         reads=[ht, sgt], writes=[S.tok("ostg")])
                S.op("pe", lambda E, ptt=ptt, tsl=tsl: E.transpose(ptt[:, :], qkb[:, 1, tsl], ident[:, :]),
                     reads=[qkt, ct], writes=[ptk])
                S.op("act", lambda E, ptt=ptt, kp=kp, dr=dr, n=n: E.activation(
                    out=kp[:, :], in_=ptt[:, :], func=AF.Copy, scale=EA[:, dr, n:n + 1]), reads=[ptk, gt], writes=[kpt])
                S.op("pe", lambda E, psc=psc, kp=kp, n=n: E.matmul(psc[:, 0:130], kp[:, :], vaug[:, n, 0:130], start=True, stop=True),
                     reads=[kpt] + vts, writes=[pct])
                S.op("dve", lambda E, psc=psc, dr=dr, n=n: E.scalar_tensor_tensor(
                    out=Cst[:, 0:130], in0=Cst[:, 0:130], scalar=FF[:, dr, n:n + 1], in1=psc[:, 0:130], op0=ALU.mult, op1=ALU.add),
                    reads=[pct, gt, S.tok("Cst")], writes=[S.tok("Cst")])
                S.op("act", lambda E: E.copy(out=Cbf[:, 0:130], in_=Cst[:, 0:130]), reads=[S.tok("Cst")], writes=[S.tok("Cbf")])
                it += 1
        finals = []
        for i in range(4):
            ft = S.tok("fin", i)
            finals.append(ft)
            S.dma("sp", h_out[i * 2048:(i + 1) * 2048, :].rearrange("(n p) e -> p n e", p=128), ostg[:, i * 16:(i + 1) * 16, :],
                  reads=[S.tok("ostg")], writes=[ft])
        S.finish(finals)
    return nc


def run_mlstm(ins):
    nc = _get("mlstm", build_mlstm)
    res = run_bass_kernel_spmd(nc, ins, core_ids=list(range(NCORES)))
    return [r["hm"] for r in res.results]


def build_final():
    nc = bass.Bass("TRN2", target_bir_lowering=False)
    xin = nc.dram_tensor("xT", [D, TPC], F32, kind="ExternalInput").ap()
    g_in = nc.dram_tensor("g", [D], F32, kind="ExternalInput").ap()
    xout = nc.dram_tensor("yT", [D, TPC], F32, kind="ExternalOutput").ap()
    cx = Ctx(nc)
    with cx.stack:
        S = Sched(nc, cx.stack)
        xT = cx.sb([128, 8, TPC], F32, "xT_sb")
        xo = [cx.sb([128, 8, 512], F32, "xo%d" % i) for i in range(2)]
        gcol = cx.sb([128, 8], F32, "gcol")
        ones_bf = cx.sb([128, 128], BF16, "ones")
        epsb = cx.sb([128, 1], F32, "eps")
        sq = cx.sb([128, 8, 512], BF16, "sq")
        rs = cx.sb([128, 512], F32, "rs")
        ps_ss = cx.ps([128, 512])
        scratch = {"sq": sq, "rs": rs, "eps": epsb, "sq_tok": S.tok("sq"), "rs_tok": S.tok("rs")}
        S.op("pool", lambda E: E.memset(ones_bf[:, :], 1.0), writes=[S.tok("ones")])
        S.op("pool", lambda E: E.memset(epsb[:, :], EPS), writes=[S.tok("eps")])
        S.dma("sp", gcol[:, :], g_in.rearrange("(k p) -> p k", p=128), writes=[S.tok("gcol")],
              allow_slow_non_contiguous=True)
        xin_v = xin.rearrange("(k p) t -> p k t", p=128)
        xout_v = xout.rearrange("(k p) t -> p k t", p=128)
        xtoks = [S.tok("x", i) for i in range(4)]
        for i in range(4):
            for k in range(0, 8, 4):
                S.dma("sp", xT[:, k:k + 4, i * 512:(i + 1) * 512], xin_v[:, k:k + 4, i * 512:(i + 1) * 512],
                      writes=[xtoks[i]])
        for t in (S.tok("ones"), S.tok("eps"), S.tok("gcol")):
            for e in ("act", "dve", "pe"):
                S._deps(e, [t], [])
        finals = []
        for gi in range(4):
            emit_rmsnorm(S, cx, xT, xtoks[gi], gi * 512, 512, gcol, xo[gi % 2], S.tok("xo", gi % 2), ones_bf, ps_ss,
                         S.tok("ps_ss"), scratch)
            ft = S.tok("fin", gi)
            finals.append(ft)
            S.dma("sp", xout_v[:, :, gi * 512:(gi + 1) * 512], xo[gi % 2][:, :, :], reads=[S.tok("xo", gi % 2)], writes=[ft])
        S.finish(finals)
    return nc


def run_final(xT_shards, g):
    nc = _get("final", build_final)
    in_maps = [{"xT": xT_shards[c], "g": g} for c in range(NCORES)]
    res = run_bass_kernel_spmd(nc, in_maps, core_ids=list(range(NCORES)))
    return [r["yT"] for r in res.results]


def _f32c(a):
    return np.ascontiguousarray(np.asarray(a, dtype=np.float32))


def _gather_tm(res, name):
    a = np.stack([np.asarray(res[c][name]) for c in range(NCORES)])
    return a.reshape(BATCH, SEQ, a.shape[-1])


def _gather_fm(res, name):
    a = np.stack([np.asarray(res[c][name]).T for c in range(NCORES)])
    return a.reshape(BATCH, SEQ, a.shape[-1])


def _layer_ab(xT, g, w_in, conv_w, conv_b, gate_b, hnorm_g, w_out):
    res = run_proj("AB", xT, g, w_in)
    qk = _gather_fm(res, "qkT")
    Vm = _gather_tm(res, "Vm")
    Om = _gather_tm(res, "Om")
    Gm = _gather_tm(res, "Gm")
    gate_b = _f32c(gate_b).reshape(16)
    ins = []
    for c in range(NCORES):
        b, h = c // 4, c % 4
        hs = slice(h * 128, (h + 1) * 128)
        ks = slice(512 + h * 128, 512 + (h + 1) * 128)
        qkT = np.ascontiguousarray(np.concatenate([qk[b, :, hs], qk[b, :, ks]], axis=1).T)
        cw = np.ascontiguousarray(np.stack([conv_w[:, hs].T, conv_w[:, ks].T], axis=1))
        cb = np.ascontiguousarray(np.stack([conv_b[hs], conv_b[ks]], axis=1))
        gi = [h, 4 + h, 8 + h, 12 + h]
        ins.append({"qkT": qkT, "cw": _f32c(cw), "cb": _f32c(cb), "Vm": np.ascontiguousarray(Vm[b, :, hs]),
                    "Om": np.ascontiguousarray(Om[b, :, hs]), "G": np.ascontiguousarray(Gm[b][:, gi]),
                    "gb": np.ascontiguousarray(gate_b[gi]), "hg": _f32c(hnorm_g[hs])})
    hm = run_mlstm(ins)
    qb = _gather_fm(res, "qbT")
    kb = _gather_fm(res, "kbT")
    Vb = _gather_tm(res, "Vb")
    ob = run_attn_b(prep_attn_b(qb, kb, Vb))
    mT = []
    for c in range(NCORES):
        b, qd = c // 4, c % 4
        parts = [np.asarray(hm[b * 4 + h])[qd * TPC:(qd + 1) * TPC].T for h in range(4)]
        parts.append(np.asarray(ob[c]))
        mT.append(np.ascontiguousarray(np.concatenate(parts, axis=0)))
    return run_out(xT, mT, w_out)


def _layer_c(xT, g, w_in, sink, w_out):
    res = run_proj("C", xT, g, w_in)
    q = _gather_fm(res, "qT")
    k = _gather_fm(res, "kT")
    v = _gather_tm(res, "V")
    oT = run_attn_c(prep_attn_c(q, k, v, _f32c(sink)))
    return run_out(xT, [np.asarray(o) for o in oT], w_out)


def kernel(x, norm_g, ffn_w1, ffn_w3, ffn_w2, ab_w_in, ab_conv_w, ab_conv_b, ab_gate_b, ab_hnorm_g, ab_w_out,
           c_w_in, c_sink, c_w_out, final_g):
    x = np.asarray(x, dtype=np.float32)
    norm_g, ffn_w1, ffn_w3, ffn_w2 = (np.asarray(a, np.float32) for a in (norm_g, ffn_w1, ffn_w3, ffn_w2))
    ab_w_in, ab_conv_w, ab_conv_b, ab_gate_b, ab_hnorm_g, ab_w_out = (
        np.asarray(a, np.float32) for a in (ab_w_in, ab_conv_w, ab_conv_b, ab_gate_b, ab_hnorm_g, ab_w_out))
    c_w_in, c_sink, c_w_out, final_g = (np.asarray(a, np.float32) for a in (c_w_in, c_sink, c_w_out, final_g))
    xT = [np.ascontiguousarray(x[c // 4, (c % 4) * TPC:(c % 4 + 1) * TPC].T) for c in range(NCORES)]
    for l in range(4):
        j = l // 2
        xT = run_ffn(xT, _f32c(norm_g[l, 0]), _f32c(ffn_w1[l, 0]), _f32c(ffn_w3[l, 0]), _f32c(ffn_w2[l, 0]))
        if l % 2 == 0:
            xT = _layer_ab(xT, _f32c(norm_g[l, 1]), _f32c(ab_w_in[j]), ab_conv_w[j], ab_conv_b[j], ab_gate_b[j],
                           ab_hnorm_g[j], _f32c(ab_w_out[j]))
        else:
            xT = _layer_c(xT, _f32c(norm_g[l, 1]), _f32c(c_w_in[j]), c_sink[j], _f32c(c_w_out[j]))
        xT = run_ffn(xT, _f32c(norm_g[l, 2]), _f32c(ffn_w1[l, 1]), _f32c(ffn_w3[l, 1]), _f32c(ffn_w2[l, 1]))
    yT = run_final(xT, _f32c(final_g))
    out = np.empty((BATCH, SEQ, D), np.float32)
    for c in range(NCORES):
        out[c // 4, (c % 4) * TPC:(c % 4 + 1) * TPC] = np.asarray(yT[c]).T
    return out
```

```python
import contextlib
import numpy as np
import concourse.bass as bass
import concourse.mybir as mybir
from concourse.bass_utils import run_bass_kernel_spmd

F32 = mybir.dt.float32
BF16 = mybir.dt.bfloat16
AF = mybir.ActivationFunctionType
ALU = mybir.AluOpType
AX = mybir.AxisListType

NCORES = 8
D = 1024
SEQ = 8192
BATCH = 2
TPC = 2048
DFF = 2816
EPS = 1e-6


class Tok:
    __slots__ = ("w", "r")

    def __init__(self):
        self.w = None
        self.r = []


class Sched:
    ENG = ("pe", "dve", "act", "pool", "sp")

    def __init__(self, nc, stack, n_dma_sems=6):
        self.nc = nc
        self.eng = {"pe": nc.tensor, "dve": nc.vector, "act": nc.scalar, "pool": nc.gpsimd, "sp": nc.sync}
        self.ops = {e: [] for e in self.ENG}
        self.cnt = {e: 0 for e in self.ENG}
        self.sem = {e: stack.enter_context(nc.semaphore("s_" + e)) for e in self.ENG}
        self.semobj = dict(self.sem)
        self.waited = {}
        self.dma_sems = {}
        self.dma_cnt = {}
        self.dma_rr = {}
        for q in ("sp", "pool", "act"):
            self.dma_sems[q] = []
            for i in range(n_dma_sems):
                key = "d_%s%d" % (q, i)
                self.semobj[key] = stack.enter_context(nc.semaphore(key))
                self.dma_sems[q].append(key)
                self.dma_cnt[key] = 0
            self.dma_rr[q] = 0
        self.toks = {}
        self.semobj["cc"] = stack.enter_context(nc.semaphore("s_cc"))
        self.cc_cnt = 0
        self.block = None

    def cc(self, kind, ins, outs, groups, reads=(), writes=()):
        self._deps("pool", reads, writes)
        self.cc_cnt += 1
        so = self.semobj["cc"]
        self.ops["pool"].append(lambda E, so=so: E.collective_compute(
            kind, ALU.bypass, replica_groups=groups, ins=ins, outs=outs).then_inc(so, 1))
        self._mark(("cc", self.cc_cnt), reads, writes)
        self._need("pool", "cc", self.cc_cnt)

    def barrier(self):
        vals = {e: self.cnt[e] for e in self.ENG}
        vals.update(self.dma_cnt)
        vals["cc"] = self.cc_cnt
        for e in self.ENG:
            for k, v in vals.items():
                if v > 0 and not (e == "pe" and k == "pe"):
                    self._need(e, k, v)
        self.toks = {}

    def flush(self):
        block = self.block
        for e, deco in (("sp", block.sync), ("pe", block.tensor), ("dve", block.vector),
                        ("act", block.scalar), ("pool", block.gpsimd)):
            ops = self.ops[e]
            if not ops:
                continue

            def body(E, ops=ops):
                for o in ops:
                    o(E)
            deco(body)
            self.ops[e] = []

    def end_stage(self):
        self.barrier()
        self.flush()

    def tok(self, *key):
        t = self.toks.get(key)
        if t is None:
            t = self.toks[key] = Tok()
        return t

    def _need(self, eng, semkey, val):
        k = (eng, semkey)
        if self.waited.get(k, 0) >= val:
            return
        self.waited[k] = val
        so = self.semobj[semkey]
        self.ops[eng].append(lambda E, so=so, val=val: E.wait_ge(so, val))

    def _deps(self, eng, reads, writes):
        deps = {}
        def add(d):
            if d is None:
                return
            if deps.get(d[0], 0) < d[1]:
                deps[d[0]] = d[1]
        for t in reads:
            add(t.w)
        for t in writes:
            add(t.w)
            for r in t.r:
                add(r)
        for semkey, val in deps.items():
            if eng == "pe" and semkey == "pe":
                continue
            self._need(eng, semkey, val)

    def _mark(self, stamp, reads, writes):
        for t in reads:
            t.r.append(stamp)
        for t in writes:
            t.w = stamp
            t.r = []

    def op(self, eng, fn, reads=(), writes=()):
        self._deps(eng, reads, writes)
        self.cnt[eng] += 1
        so = self.sem[eng]
        self.ops[eng].append(lambda E, fn=fn, so=so: fn(E).then_inc(so, 1))
        self._mark((eng, self.cnt[eng]), reads, writes)

    def dma(self, q, out, in_, reads=(), writes=(), **kw):
        self._deps(q, reads, writes)
        pool = self.dma_sems[q]
        key = pool[self.dma_rr[q] % len(pool)]
        self.dma_rr[q] += 1
        prev = self.dma_cnt[key]
        if prev:
            self._need(q, key, prev)
        self.dma_cnt[key] = prev + 16
        so = self.semobj[key]
        self.ops[q].append(lambda E, so=so, out=out, in_=in_, kw=kw: E.dma_start(out=out, in_=in_, **kw).then_inc(so, 16))
        self._mark((key, prev + 16), reads, writes)

    def finish(self, final_toks):
        for t in final_toks:
            if t.w is not None:
                self._need("sp", t.w[0], t.w[1])
        nc = self.nc
        with nc.Block() as block:
            for e, deco in (("sp", block.sync), ("pe", block.tensor), ("dve", block.vector),
                            ("act", block.scalar), ("pool", block.gpsimd)):
                ops = self.ops[e]
                if not ops:
                    continue

                def body(E, ops=ops):
                    for o in ops:
                        o(E)
                deco(body)


class Ctx:
    uid = 0

    def __init__(self, nc):
        self.nc = nc
        self.stack = contextlib.ExitStack()
        self.n = 0

    def sb(self, shape, dt, name=None):
        Ctx.uid += 1
        return self.stack.enter_context(self.nc.sbuf_tensor((name or "t") + "_s%d" % Ctx.uid, list(shape), dt))

    def ps(self, shape, dt=F32, name=None):
        Ctx.uid += 1
        return self.stack.enter_context(self.nc.psum_tensor((name or "p") + "_p%d" % Ctx.uid, list(shape), dt))


def emit_rmsnorm(S, cx, xT, xT_tok, ntok0, ntok, gcol, xn, xn_tok, ones_bf, ps_ss, ps_tok, scratch):
    sq, rs = scratch["sq"], scratch["rs"]
    sl = slice(ntok0, ntok0 + ntok)
    for k in range(8):
        S.op("act", lambda E, k=k: E.activation(out=sq[:, k, 0:ntok], in_=xT[:, k, sl], func=AF.Square),
             reads=[xT_tok], writes=[scratch["sq_tok"]])
    for k in range(8):
        S.op("pe", lambda E, k=k: E.matmul(ps_ss[:, 0:ntok], ones_bf[:, :], sq[:, k, 0:ntok],
                                           start=(k == 0), stop=(k == 7)),
             reads=[scratch["sq_tok"]], writes=[ps_tok])
    S.op("act", lambda E: E.activation(out=rs[:, 0:ntok], in_=ps_ss[:, 0:ntok], func=AF.Sqrt,
                                       scale=1.0 / D, bias=scratch["eps"][:, 0:1]),
         reads=[ps_tok], writes=[scratch["rs_tok"]])
    S.op("dve", lambda E: E.reciprocal(out=rs[:, 0:ntok], in_=rs[:, 0:ntok]),
         reads=[scratch["rs_tok"]], writes=[scratch["rs_tok"]])
    for k in range(8):
        S.op("dve", lambda E, k=k: E.scalar_tensor_tensor(out=xn[:, k, 0:ntok], in0=xT[:, k, sl],
                                                          scalar=gcol[:, k:k + 1], in1=rs[:, 0:ntok],
                                                          op0=ALU.mult, op1=ALU.mult),
             reads=[xT_tok, scratch["rs_tok"]], writes=[xn_tok])


def build_ffn(final_norm=False):
    nc = bass.Bass("TRN2", target_bir_lowering=False)
    xin = nc.dram_tensor("xT", [D, TPC], F32, kind="ExternalInput").ap()
    g_in = nc.dram_tensor("g", [D], F32, kind="ExternalInput").ap()
    w1 = nc.dram_tensor("w1", [D, DFF], F32, kind="ExternalInput").ap()
    w3 = nc.dram_tensor("w3", [D, DFF], F32, kind="ExternalInput").ap()
    w2 = nc.dram_tensor("w2", [DFF, D], F32, kind="ExternalInput").ap()
    if final_norm:
        gf_in = nc.dram_tensor("gf", [D], F32, kind="ExternalInput").ap()
    xout = nc.dram_tensor("yT", [D, TPC], F32, kind="ExternalOutput").ap()
    NF = DFF // 128
    TP = 1024
    cx = Ctx(nc)
    with cx.stack:
        S = Sched(nc, cx.stack)
        xT = cx.sb([128, 8, TPC], F32, "xT_sb")
        xn = cx.sb([128, 8, TP], BF16, "xn")
        gb = cx.sb([128, NF, TP], BF16, "gb")
        gcol = cx.sb([128, 8], F32, "gcol")
        ones_bf = cx.sb([128, 128], BF16, "ones")
        epsb = cx.sb([128, 1], F32, "eps")
        sq = cx.sb([128, 8, 512], BF16, "sq")
        rs = cx.sb([128, 512], F32, "rs")
        w13 = [cx.sb([128, 2, 8, 128], BF16, "w13_%d" % i) for i in range(3)]
        w2b = [cx.sb([128, NF, 128], BF16, "w2b_%d" % i) for i in range(2)]
        sil = [cx.sb([128, 512], F32, "sil%d" % i) for i in range(2)]
        ps_h = [[cx.ps([128, 512]) for _ in range(2)] for _ in range(2)]
        ps_y = [cx.ps([128, 512]) for _ in range(2)]
        ps_ss = cx.ps([128, 512])
        scratch = {"sq": sq, "rs": rs, "eps": epsb, "sq_tok": S.tok("sq"), "rs_tok": S.tok("rs")}

        S.op("pool", lambda E: E.memset(ones_bf[:, :], 1.0), writes=[S.tok("ones")])
        S.op("pool", lambda E: E.memset(epsb[:, :], EPS), writes=[S.tok("eps")])
        S.dma("sp", gcol[:, :], g_in.rearrange("(k p) -> p k", p=128), writes=[S.tok("gcol")],
              allow_slow_non_contiguous=True)
        xin_v = xin.rearrange("(k p) t -> p k t", p=128)
        xout_v = xout.rearrange("(k p) t -> p k t", p=128)
        xtoks = [S.tok("x", i) for i in range(4)]
        for i in range(4):
            for k in range(0, 8, 4):
                S.dma("sp", xT[:, k:k + 4, i * 512:(i + 1) * 512], xin_v[:, k:k + 4, i * 512:(i + 1) * 512],
                      writes=[xtoks[i]])
        scratch_r = [S.tok("ones"), S.tok("eps"), S.tok("gcol")]
        w1v = w1.rearrange("(k p) f -> p k f", p=128)
        w3v = w3.rearrange("(k p) f -> p k f", p=128)
        w2v = w2.rearrange("(f p) d -> p f d", p=128)
        wq = ["pool", "act"]
        nload = 0
        for pas in range(TPC // TP):
            for gi in range(TP // 512):
                grp = pas * (TP // 512) + gi
                for t in scratch_r:
                    S._deps("act", [t], [])
                    S._deps("dve", [t], [])
                    S._deps("pe", [t], [])
                emit_rmsnorm(S, cx, xT, xtoks[grp], grp * 512, 512, gcol,
                             xn[:, :, gi * 512:(gi + 1) * 512], S.tok("xn", gi), ones_bf, ps_ss,
                             S.tok("ps_ss"), scratch)
            for f in range(NF):
                wb = w13[f % 3]
                wt = S.tok("w13", f % 3)
                S.dma("pool", wb[:, 0, :, :], w1v[:, :, f * 128:(f + 1) * 128], writes=[wt])
                S.dma("pool", wb[:, 1, :, :], w3v[:, :, f * 128:(f + 1) * 128], writes=[wt])
                for gi in range(TP // 512):
                    pb = ps_h[(f * 2 + gi) % 2]
                    pt = [S.tok("ps_h", (f * 2 + gi) % 2, j) for j in range(2)]
                    tsl = slice(gi * 512, (gi + 1) * 512)
                    for j in range(2):
                        for k in range(8):
                            S.op("pe", lambda E, j=j, k=k, pb=pb, wb=wb, tsl=tsl: E.matmul(
                                pb[j][:, :], wb[:, j, k, :], xn[:, k, tsl], start=(k == 0), stop=(k == 7)),
                                reads=[wt, S.tok("xn", gi)], writes=[pt[j]])
                    sb_ = sil[(f * 2 + gi) % 2]
                    st = S.tok("sil", (f * 2 + gi) % 2)
                    S.op("act", lambda E, pb=pb, sb_=sb_: E.activation(out=sb_[:, :], in_=pb[0][:, :], func=AF.Silu),
                         reads=[pt[0]], writes=[st])
                    S.op("dve", lambda E, pb=pb, sb_=sb_, f=f, tsl=tsl: E.tensor_tensor(
                        out=gb[:, f, tsl], in0=sb_[:, :], in1=pb[1][:, :], op=ALU.mult),
                        reads=[st, pt[1]], writes=[S.tok("gb", gi)])
            for d in range(8):
                wb = w2b[d % 2]
                wt = S.tok("w2b", d % 2)
                S.dma("pool", wb[:, 0:11, :], w2v[:, 0:11, d * 128:(d + 1) * 128], writes=[wt])
                S.dma("pool", wb[:, 11:22, :], w2v[:, 11:22, d * 128:(d + 1) * 128], writes=[wt])
                for gi in range(TP // 512):
                    grp = pas * (TP // 512) + gi
                    py = ps_y[(d * 2 + gi) % 2]
                    pyt = S.tok("ps_y", (d * 2 + gi) % 2)
                    tsl = slice(gi * 512, (gi + 1) * 512)
                    for f in range(NF):
                        S.op("pe", lambda E, f=f, py=py, wb=wb, tsl=tsl: E.matmul(
                            py[:, :], wb[:, f, :], gb[:, f, tsl], start=(f == 0), stop=(f == NF - 1)),
                            reads=[wt, S.tok("gb", gi)], writes=[pyt])
                    xs = xT[:, d, grp * 512:(grp + 1) * 512]
                    S.op("dve", lambda E, py=py, xs=xs: E.scalar_tensor_tensor(
                        out=xs, in0=py[:, :], scalar=0.5, in1=xs, op0=ALU.mult, op1=ALU.add),
                        reads=[pyt, xtoks[grp]], writes=[xtoks[grp]])
        outt = S.tok("out")
        for i in range(4):
            for k in range(0, 8, 4):
                S.dma("sp", xout_v[:, k:k + 4, i * 512:(i + 1) * 512], xT[:, k:k + 4, i * 512:(i + 1) * 512],
                      reads=[xtoks[i]], writes=[outt, S.tok("out", i, k)])
        S.finish([S.tok("out", i, k) for i in range(4) for k in (0, 4)])
    return nc


_CACHE = {}


def _get(name, fn, *a):
    key = (name,) + a
    if key not in _CACHE:
        _CACHE[key] = fn(*a)
    return _CACHE[key]


def run_ffn(xT_shards, g, w1, w3, w2):
    nc = _get("ffn", build_ffn)
    in_maps = [{"xT": xT_shards[c], "g": g, "w1": w1, "w3": w3, "w2": w2} for c in range(NCORES)]
    res = run_bass_kernel_spmd(nc, in_maps, core_ids=list(range(NCORES)))
    return [r["yT"] for r in res.results]


PROJ_SPECS = {
    "C": (1536, [("qT", 0, 1024, "fm64", BF16), ("kT", 1024, 256, "fm64", BF16), ("V", 1280, 256, "tm", BF16)]),
    "AB": (3600, [("qkT", 0, 1024, "fm128", F32), ("Vm", 1024, 512, "tm", BF16), ("Om", 1536, 512, "tm", F32),
                  ("Gm", 2048, 16, "tm", F32), ("qbT", 2064, 512, "fm64", BF16), ("kbT", 2576, 512, "fm64", BF16),
                  ("Vb", 3088, 512, "tm", BF16)]),
}


def build_proj(kind):
    NC_, specs = PROJ_SPECS[kind]
    nc = bass.Bass("TRN2", target_bir_lowering=False)
    xin = nc.dram_tensor("xT", [D, TPC], F32, kind="ExternalInput").ap()
    g_in = nc.dram_tensor("g", [D], F32, kind="ExternalInput").ap()
    w = nc.dram_tensor("w", [D, NC_], F32, kind="ExternalInput").ap()
    outs = {}
    for name, c0, ncol, lay, dt in specs:
        shp = [TPC, ncol] if lay == "tm" else [ncol, TPC]
        outs[name] = nc.dram_tensor(name, shp, dt, kind="ExternalOutput").ap()
    cx = Ctx(nc)
    with cx.stack:
        S = Sched(nc, cx.stack)
        xT = cx.sb([128, 8, TPC], F32, "xT_sb")
        xn = cx.sb([128, 8, TPC], BF16, "xn")
        wsb = cx.sb([128, 8, NC_], BF16, "wsb")
        gcol = cx.sb([128, 8], F32, "gcol")
        ones_bf = cx.sb([128, 128], BF16, "ones")
        epsb = cx.sb([128, 1], F32, "eps")
        sq = cx.sb([128, 8, 512], BF16, "sq")
        rs = cx.sb([128, 512], F32, "rs")
        stg_fm = [cx.sb([128, TPC], F32, "stgfm%d" % i) for i in range(2)]
        stg_tm = [cx.sb([128, 512], F32, "stgtm%d" % i) for i in range(3)]
        ps = [cx.ps([128, 512]) for _ in range(6)]
        ps_ss = cx.ps([128, 512])
        scratch = {"sq": sq, "rs": rs, "eps": epsb, "sq_tok": S.tok("sq"), "rs_tok": S.tok("rs")}
        S.op("pool", lambda E: E.memset(ones_bf[:, :], 1.0), writes=[S.tok("ones")])
        S.op("pool", lambda E: E.memset(epsb[:, :], EPS), writes=[S.tok("eps")])
        S.dma("sp", gcol[:, :], g_in.rearrange("(k p) -> p k", p=128), writes=[S.tok("gcol")],
              allow_slow_non_contiguous=True)
        xin_v = xin.rearrange("(k p) t -> p k t", p=128)
        xtoks = [S.tok("x", i) for i in range(4)]
        for i in range(4):
            for k in range(0, 8, 4):
                S.dma("sp", xT[:, k:k + 4, i * 512:(i + 1) * 512], xin_v[:, k:k + 4, i * 512:(i + 1) * 512],
                      writes=[xtoks[i]])
        wv = w.rearrange("(k p) f -> p k f", p=128)
        wtok = S.tok("w")
        for k in range(8):
            S.dma("pool", wsb[:, k, :], wv[:, k, :], writes=[wtok])
        for t in (S.tok("ones"), S.tok("eps"), S.tok("gcol")):
            for e in ("act", "dve", "pe"):
                S._deps(e, [t], [])
        for gi in range(4):
            emit_rmsnorm(S, cx, xT, xtoks[gi], gi * 512, 512, gcol, xn[:, :, gi * 512:(gi + 1) * 512],
                         S.tok("xn", gi), ones_bf, ps_ss, S.tok("ps_ss"), scratch)
        pi = 0
        ei = 0
        si = 0
        finals = []
        for name, c0, ncol, lay, dt in specs:
            if lay == "tm":
                continue
            M = 64 if lay == "fm64" else 128
            for ch in range(ncol // M):
                stg = stg_fm[si % 2]
                stt = S.tok("stgfm", si % 2)
                si += 1
                for gi in range(4):
                    p = ps[pi % 6]
                    pt = S.tok("ps", pi % 6)
                    pi += 1
                    for k in range(8):
                        S.op("pe", lambda E, p=p, k=k, cc=c0 + ch * M, M=M, gi=gi: E.matmul(
                            p[0:M, :], wsb[:, k, cc:cc + M], xn[:, k, gi * 512:(gi + 1) * 512],
                            start=(k == 0), stop=(k == 7)), reads=[wtok, S.tok("xn", gi)], writes=[pt])
                    dst = _stg_view(stg, dt, M, TPC)[:, gi * 512:(gi + 1) * 512]
                    if ei % 2 == 0:
                        S.op("act", lambda E, p=p, dst=dst, M=M: E.copy(out=dst, in_=p[0:M, :]), reads=[pt], writes=[stt])
                    else:
                        S.op("dve", lambda E, p=p, dst=dst, M=M: E.tensor_copy(out=dst, in_=p[0:M, :]), reads=[pt], writes=[stt])
                    ei += 1
                ft = S.tok("fin", name, ch)
                finals.append(ft)
                S.dma("sp", outs[name][ch * M:(ch + 1) * M, :], _stg_view(stg, dt, M, TPC), reads=[stt], writes=[ft])
        ti = 0
        for tt in range(16):
            for name, c0, ncol, lay, dt in specs:
                if lay != "tm":
                    continue
                p = ps[pi % 6]
                pt = S.tok("ps", pi % 6)
                pi += 1
                for k in range(8):
                    S.op("pe", lambda E, p=p, k=k, c0=c0, ncol=ncol, tt=tt: E.matmul(
                        p[:, 0:ncol], xn[:, k, tt * 128:(tt + 1) * 128], wsb[:, k, c0:c0 + ncol],
                        start=(k == 0), stop=(k == 7)), reads=[wtok, S.tok("xn", tt // 4)], writes=[pt])
                stg = stg_tm[ti % 3]
                stt = S.tok("stgtm", ti % 3)
                ti += 1
                dst = _stg_view(stg, dt, 128, ncol)
                if ei % 2 == 0:
                    S.op("act", lambda E, p=p, dst=dst, ncol=ncol: E.copy(out=dst, in_=p[:, 0:ncol]), reads=[pt], writes=[stt])
                else:
                    S.op("dve", lambda E, p=p, dst=dst, ncol=ncol: E.tensor_copy(out=dst, in_=p[:, 0:ncol]), reads=[pt], writes=[stt])
                ei += 1
                ft = S.tok("fin", name, "t", tt)
                finals.append(ft)
                S.dma("sp", outs[name][tt * 128:(tt + 1) * 128, :], dst, reads=[stt], writes=[ft],
                      allow_slow_non_contiguous=(ncol < 64))
        S.finish(finals)
    return nc


def _stg_view(stg, dt, M, n):
    if dt == F32:
        return stg[0:M, 0:n]
    return stg[0:M, :].bitcast(BF16)[:, 0:n]


def run_proj(kind, xT_shards, g, w):
    nc = _get("proj", build_proj, kind)
    in_maps = [{"xT": xT_shards[c], "g": g, "w": w} for c in range(NCORES)]
    res = run_bass_kernel_spmd(nc, in_maps, core_ids=list(range(NCORES)))
    return res.results


def emit_absrel(S, cx, nchunks, halo, radius, name):
    A = cx.sb([128, nchunks, 128], F32, name)
    Ai = cx.sb([128, nchunks, 128], mybir.dt.int32, name + "_i")
    B = cx.sb([128, nchunks, 128], F32, name + "_b")
    t = S.tok(name)
    for c in range(nchunks):
        S.op("pool", lambda E, c=c: E.iota(Ai[:, c, :], [[-1, 128]], base=c * 128 - halo, channel_multiplier=1),
             writes=[t])
    S.op("dve", lambda E: E.tensor_copy(out=A[:, :, :], in_=Ai[:, :, :]), reads=[t], writes=[t])
    S.op("dve", lambda E: E.tensor_scalar(out=B[:, :, :], in0=A[:, :, :], scalar1=-1.0, scalar2=None, op0=ALU.mult),
         reads=[t], writes=[t])
    S.op("dve", lambda E: E.tensor_tensor(out=A[:, :, :], in0=A[:, :, :], in1=B[:, :, :], op=ALU.max),
         reads=[t], writes=[t])
    S.op("dve", lambda E: E.tensor_scalar(out=B[:, :, :], in0=A[:, :, :], scalar1=float(radius) + 0.5, scalar2=1e9,
                                          op0=ALU.is_gt, op1=ALU.mult), reads=[t], writes=[t])
    S.op("dve", lambda E: E.tensor_tensor(out=A[:, :, :], in0=A[:, :, :], in1=B[:, :, :], op=ALU.add),
         reads=[t], writes=[t])
    return A, t


def alibi_slopes(n):
    return [float(2.0 ** (-8.0 * (i + 1) / n)) for i in range(n)]


def build_attn_c():
    nc = bass.Bass("TRN2", target_bir_lowering=False)
    KH = TPC + 256
    q_in = nc.dram_tensor("qT", [1024, TPC], BF16, kind="ExternalInput").ap()
    k_in = nc.dram_tensor("kT", [256, KH], BF16, kind="ExternalInput").ap()
    v_in = nc.dram_tensor("V", [KH, 256], BF16, kind="ExternalInput").ap()
    kb_in = nc.dram_tensor("kb", [128, 18], F32, kind="ExternalInput").ap()
    sink_in = nc.dram_tensor("sink", [16], F32, kind="ExternalInput").ap()
    o_out = nc.dram_tensor("oT", [1024, TPC], BF16, kind="ExternalOutput").ap()
    cx = Ctx(nc)
    slopes = alibi_slopes(16)
    with cx.stack:
        S = Sched(nc, cx.stack)
        qT = cx.sb([64, 16, TPC], BF16, "qT_sb")
        kT = cx.sb([64, 4, KH], BF16, "kT_sb")
        V = cx.sb([128, 18, 256], BF16, "V_sb")
        kb = cx.sb([128, 18], F32, "kb_sb")
        esink = cx.sb([64, 16], F32, "esink")
        ones_bf = cx.sb([128, 64], BF16, "ones")
        bias = cx.sb([128, 3, 16, 128], F32, "bias")
        tmp = [cx.sb([128, 512], F32, "tmp%d" % i) for i in range(3)]
        P = [cx.sb([128, 512], BF16, "P%d" % i) for i in range(3)]
        rden = [cx.sb([64, 512], F32, "rden%d" % i) for i in range(2)]
        ostg = [cx.sb([64, 4, TPC], BF16, "ostg%d" % i) for i in range(2)]
        ps_s = [cx.ps([128, 512]) for _ in range(4)]
        ps_n = [cx.ps([64, 512]) for _ in range(2)]
        ps_d = [cx.ps([64, 512]) for _ in range(2)]
        S.dma("sp", qT[:, :, :], q_in.rearrange("(h e) t -> e h t", e=64), writes=[S.tok("q")])
        S.dma("sp", kT[:, :, :], k_in.rearrange("(h e) t -> e h t", e=64), writes=[S.tok("k")])
        S.dma("sp", V[:, :, :], v_in.rearrange("(c p) f -> p c f", p=128), writes=[S.tok("v")])
        S.dma("sp", kb[:, :], kb_in, writes=[S.tok("kb")])
        S.dma("sp", esink[:, :], sink_in.partition_broadcast(64), writes=[S.tok("esink")])
        S.op("act", lambda E: E.activation(out=esink[:, :], in_=esink[:, :], func=AF.Exp),
             reads=[S.tok("esink")], writes=[S.tok("esink")])
        S.op("pool", lambda E: E.memset(ones_bf[:, :], 1.0), writes=[S.tok("ones")])
        A, At = emit_absrel(S, cx, 3, 128, 128, "absrel")
        bt = S.tok("bias")
        for h in range(16):
            S.op("dve", lambda E, h=h: E.tensor_scalar(out=bias[:, :, h, :], in0=A[:, :, :], scalar1=-slopes[h],
                                                       scalar2=None, op0=ALU.mult), reads=[At], writes=[bt])
        scale = 64 ** -0.5
        finals = []
        P6 = P + [cx.sb([128, 512], BF16, "P%d" % i) for i in range(3, 6)]
        iters = [(kvh, b) for kvh in range(4) for b in range(16)]

        def emitA(it):
            kvh, b = iters[it]
            for c in range(3):
                j = it * 3 + c
                pss = ps_s[j % 4]
                pst = S.tok("pss", j % 4)
                S.op("pe", lambda E, pss=pss, kvh=kvh, b=b, c=c: E.matmul(
                    pss[:, :], kT[:, kvh, (b + c) * 128:(b + c + 1) * 128],
                    qT[:, kvh * 4:(kvh + 1) * 4, b * 128:(b + 1) * 128], start=True, stop=True),
                    reads=[S.tok("q"), S.tok("k")], writes=[pst])
                tm_, tmt = tmp[j % 3], S.tok("tmp", j % 3)
                S.op("dve", lambda E, pss=pss, tm_=tm_, kvh=kvh, c=c: E.scalar_tensor_tensor(
                    out=tm_[:, :], in0=pss[:, :], scalar=scale, in1=bias[:, c, kvh * 4:(kvh + 1) * 4, :],
                    op0=ALU.mult, op1=ALU.add), reads=[pst, bt], writes=[tmt])
                P_, Pt = P6[j % 6], S.tok("P", j % 6)
                S.op("act", lambda E, tm_=tm_, P_=P_, b=b, c=c: E.activation(
                    out=P_[:, :], in_=tm_[:, :], func=AF.Exp, bias=kb[:, b + c:b + c + 1]),
                    reads=[tmt, S.tok("kb")], writes=[Pt])

        def emitB(it):
            kvh, b = iters[it]
            og = ostg[kvh % 2]
            ogt = S.tok("ostg", kvh % 2)
            pn, pd = ps_n[it % 2], ps_d[it % 2]
            pnt, pdt = S.tok("psn", it % 2), S.tok("psd", it % 2)
            for c in range(3):
                j = it * 3 + c
                P_, Pt = P6[j % 6], S.tok("P", j % 6)
                S.op("pe", lambda E, pn=pn, P_=P_, kvh=kvh, b=b, c=c: E.matmul(
                    pn[:, :], V[:, b + c, kvh * 64:(kvh + 1) * 64], P_[:, :], start=(c == 0), stop=(c == 2)),
                    reads=[Pt, S.tok("v")], writes=[pnt])
            for c in range(3):
                j = it * 3 + c
                P_, Pt = P6[j % 6], S.tok("P", j % 6)
                S.op("pe", lambda E, pd=pd, P_=P_, c=c: E.matmul(
                    pd[:, :], ones_bf[:, :], P_[:, :], start=(c == 0), stop=(c == 2)),
                    reads=[Pt, S.tok("ones")], writes=[pdt])
            rd, rdt = rden[it % 2], S.tok("rden", it % 2)
            for g in range(4):
                h = kvh * 4 + g
                S.op("dve", lambda E, rd=rd, pd=pd, g=g, h=h: E.tensor_scalar(
                    out=rd[:, g * 128:(g + 1) * 128], in0=pd[:, g * 128:(g + 1) * 128],
                    scalar1=esink[:, h:h + 1], scalar2=None, op0=ALU.add),
                    reads=[pdt, S.tok("esink")], writes=[rdt])
            S.op("dve", lambda E, rd=rd: E.reciprocal(out=rd[:, :], in_=rd[:, :]), reads=[rdt], writes=[rdt])
            S.op("dve", lambda E, rd=rd, pn=pn, og=og, b=b: E.tensor_tensor(
                out=og[:, :, b * 128:(b + 1) * 128], in0=pn[:, :].rearrange("p (g q) -> p g q", g=4),
                in1=rd[:, :].rearrange("p (g q) -> p g q", g=4), op=ALU.mult),
                reads=[pnt, rdt], writes=[ogt])
            if b == 15:
                ft = S.tok("fin", kvh)
                finals.append(ft)
                S.dma("sp", o_out[kvh * 256:(kvh + 1) * 256, :].rearrange("(g e) t -> e g t", e=64), og[:, :, :],
                      reads=[ogt], writes=[ft])

        LOOK = 1
        for idx in range(len(iters) + LOOK):
            if idx < len(iters):
                emitA(idx)
            if idx - LOOK >= 0:
                emitB(idx - LOOK)
        S.finish(finals)
    return nc


def run_attn_c(ins):
    nc = _get("attn_c", build_attn_c)
    res = run_bass_kernel_spmd(nc, ins, core_ids=list(range(NCORES)))
    return [r["oT"] for r in res.results]


def _halo(arr, c, halo):
    b, qd = c // 4, c % 4
    t0 = qd * TPC
    out = np.zeros((TPC + 2 * halo, arr.shape[2]), arr.dtype)
    lo, hi = max(0, t0 - halo), min(SEQ, t0 + TPC + halo)
    out[lo - (t0 - halo):hi - (t0 - halo)] = arr[b, lo:hi]
    return out


def _kvalid(c, halo):
    qd = c % 4
    t0 = qd * TPC
    pos = np.arange(t0 - halo, t0 + TPC + halo)
    return np.where((pos >= 0) & (pos < SEQ), 0.0, -30000.0).astype(np.float32)


def prep_attn_c(q, k, v, sink):
    ins = []
    for c in range(NCORES):
        b, qd = c // 4, c % 4
        kh = _halo(k, c, 128)
        vh = _halo(v, c, 128)
        ins.append({"qT": np.ascontiguousarray(q[b, qd * TPC:(qd + 1) * TPC].T),
                    "kT": np.ascontiguousarray(kh.T), "V": vh,
                    "kb": np.ascontiguousarray(_kvalid(c, 128).reshape(18, 128).T), "sink": sink})
    return ins


def build_out():
    nc = bass.Bass("TRN2", target_bir_lowering=False)
    xin = nc.dram_tensor("xT", [D, TPC], F32, kind="ExternalInput").ap()
    m_in = nc.dram_tensor("mT", [D, TPC], BF16, kind="ExternalInput").ap()
    w = nc.dram_tensor("w", [D, D], F32, kind="ExternalInput").ap()
    xout = nc.dram_tensor("yT", [D, TPC], F32, kind="ExternalOutput").ap()
    cx = Ctx(nc)
    with cx.stack:
        S = Sched(nc, cx.stack)
        xT = cx.sb([128, 8, TPC], F32, "xT_sb")
        mT = cx.sb([128, 8, TPC], BF16, "mT_sb")
        wsb = cx.sb([128, 8, D], BF16, "wsb")
        ps = [cx.ps([128, 512]) for _ in range(4)]
        xin_v = xin.rearrange("(k p) t -> p k t", p=128)
        xout_v = xout.rearrange("(k p) t -> p k t", p=128)
        m_v = m_in.rearrange("(k p) t -> p k t", p=128)
        wv = w.rearrange("(k p) f -> p k f", p=128)
        for k in range(8):
            S.dma("pool", wsb[:, k, :], wv[:, k, :], writes=[S.tok("w")])
        for k in range(0, 8, 4):
            S.dma("sp", mT[:, k:k + 4, :], m_v[:, k:k + 4, :], writes=[S.tok("m")])
        xt = [S.tok("x", d) for d in range(8)]
        for d in range(8):
            S.dma("sp", xT[:, d, :], xin_v[:, d, :], writes=[xt[d]])
        finals = []
        i = 0
        for d in range(8):
            for gi in range(4):
                p, pt = ps[i % 4], S.tok("ps", i % 4)
                i += 1
                for k in range(8):
                    S.op("pe", lambda E, p=p, k=k, d=d, gi=gi: E.matmul(
                        p[:, :], wsb[:, k, d * 128:(d + 1) * 128], mT[:, k, gi * 512:(gi + 1) * 512],
                        start=(k == 0), stop=(k == 7)), reads=[S.tok("w"), S.tok("m")], writes=[pt])
                xs = xT[:, d, gi * 512:(gi + 1) * 512]
                S.op("dve", lambda E, p=p, xs=xs: E.tensor_tensor(out=xs, in0=p[:, :], in1=xs, op=ALU.add),
                     reads=[pt, xt[d]], writes=[xt[d]])
            ft = S.tok("fin", d)
            finals.append(ft)
            S.dma("sp", xout_v[:, d, :], xT[:, d, :], reads=[xt[d]], writes=[ft])
        S.finish(finals)
    return nc


def run_out(xT_shards, mT_shards, w):
    nc = _get("out", build_out)
    in_maps = [{"xT": xT_shards[c], "mT": mT_shards[c], "w": w} for c in range(NCORES)]
    res = run_bass_kernel_spmd(nc, in_maps, core_ids=list(range(NCORES)))
    return [r["yT"] for r in res.results]


B_DILS = (1, 4, 16)
B_HALO = 1024


def b_chunks():
    tab = {}
    for d in B_DILS:
        for r in range(d):
            for m in range(16 // d + 1):
                tab[(d, r, m)] = len(tab)
    return tab


def build_attn_b():
    nc = bass.Bass("TRN2", target_bir_lowering=False)
    KH = TPC + 2 * B_HALO
    q_in = nc.dram_tensor("qT", [512, TPC], BF16, kind="ExternalInput").ap()
    k_in = nc.dram_tensor("kT", [512, KH], BF16, kind="ExternalInput").ap()
    v_in = nc.dram_tensor("V", [KH, 512], BF16, kind="ExternalInput").ap()
    kb_in = nc.dram_tensor("kbt", [128, 69], F32, kind="ExternalInput").ap()
    o_out = nc.dram_tensor("oT", [512, TPC], BF16, kind="ExternalOutput").ap()
    cx = Ctx(nc)
    slopes = alibi_slopes(8)
    tab = b_chunks()
    NCH = len(tab)
    with cx.stack:
        S = Sched(nc, cx.stack)
        qT = cx.sb([128, 4, TPC], BF16, "qT_sb")
        kT = cx.sb([128, 4, KH], BF16, "kT_sb")
        V = cx.sb([128, NCH, 512], BF16, "V_sb")
        kb = cx.sb([128, NCH], F32, "kb_sb")
        ones_bf = cx.sb([128, 64], BF16, "ones")
        bias = cx.sb([128, 8, 3, 2, 128], F32, "bias")
        tmp = [cx.sb([128, 2, 128], F32, "tmp%d" % i) for i in range(3)]
        P = [cx.sb([128, 2, 128], BF16, "P%d" % i) for i in range(3)]
        accn = [cx.sb([64, TPC], F32, "accn%d" % i) for i in range(2)]
        accd = [cx.sb([64, TPC], F32, "accd%d" % i) for i in range(2)]
        ostg = [cx.sb([64, TPC], BF16, "ostg%d" % i) for i in range(2)]
        ps_s = [cx.ps([128, 2, 128]) for _ in range(3)]
        ps_n = [cx.ps([64, 128]) for _ in range(2)]
        ps_d = [cx.ps([64, 128]) for _ in range(2)]
        S.dma("sp", qT[:, :, :], q_in.rearrange("(k p) t -> p k t", p=128), writes=[S.tok("q")])
        for k in range(4):
            S.dma("sp", kT[:, k, :], k_in[k * 128:(k + 1) * 128, :], writes=[S.tok("k")])
        S.dma("sp", kb[:, :], kb_in, writes=[S.tok("kb")])
        for d in B_DILS:
            for r in range(d):
                M = 16 // d + 1
                c0 = tab[(d, r, 0)]
                start = B_HALO + r - 64 * d
                src = bass.AP(v_in.tensor, start * 512, [[d * 512, 128], [d * 128 * 512, M], [1, 512]])
                S.dma("sp" if (r % 2 == 0) else "act", V[:, c0:c0 + M, :], src, writes=[S.tok("v")])
        S.op("pool", lambda E: E.memset(ones_bf[:, :], 1.0), writes=[S.tok("ones")])
        A, At = emit_absrel(S, cx, 2, 64, 64, "absrel")
        bt = S.tok("bias")
        for h in range(8):
            for di, d in enumerate(B_DILS):
                S.op("dve", lambda E, h=h, di=di, d=d: E.tensor_scalar(
                    out=bias[:, h, di, :, :], in0=A[:, :, :], scalar1=-slopes[h] * d, scalar2=None, op0=ALU.mult),
                    reads=[At], writes=[bt])
        scale = 64 ** -0.5
        finals = []
        blocks = []
        for h in range(8):
            for di, d in enumerate(B_DILS):
                for r in range(d):
                    for blk in range(16 // d):
                        blocks.append((h, di, d, r, blk))
        LOOK = 2
        NP_ = 4
        P4 = P + [cx.sb([128, 2, 128], BF16, "P3")]

        def emitA(it):
            h, di, d, r, blk = blocks[it]
            pb, hp = (h % 2) * 64, h // 2
            q0 = r + d * 128 * blk
            qsl = slice(q0, q0 + d * 127 + 1, d)
            pss, pst = ps_s[it % 3], S.tok("pss", it % 3)
            for c in range(2):
                u0 = B_HALO + r + d * (128 * (blk + c) - 64)
                S.op("pe", lambda E, pss=pss, c=c, u0=u0, d=d, qsl=qsl, pb=pb, hp=hp: E.matmul(
                    pss[:, c, :], kT[pb:pb + 64, hp, u0:u0 + d * 127 + 1:d], qT[pb:pb + 64, hp, qsl],
                    start=True, stop=True), reads=[S.tok("q"), S.tok("k")], writes=[pst])
            tm_, tmt = tmp[it % 3], S.tok("tmp", it % 3)
            S.op("dve", lambda E, pss=pss, tm_=tm_, h=h, di=di: E.scalar_tensor_tensor(
                out=tm_[:, :, :], in0=pss[:, :, :], scalar=scale, in1=bias[:, h, di, :, :],
                op0=ALU.mult, op1=ALU.add), reads=[pst, bt], writes=[tmt])
            P_, Pt = P4[it % NP_], S.tok("P", it % NP_)
            for c in range(2):
                ci = tab[(d, r, blk + c)]
                S.op("act", lambda E, tm_=tm_, P_=P_, c=c, ci=ci: E.activation(
                    out=P_[:, c, :], in_=tm_[:, c, :], func=AF.Exp, bias=kb[:, ci:ci + 1]),
                    reads=[tmt, S.tok("kb")], writes=[Pt])

        def emitB(it):
            h, di, d, r, blk = blocks[it]
            q0 = r + d * 128 * blk
            qsl = slice(q0, q0 + d * 127 + 1, d)
            an, ad = accn[h % 2], accd[h % 2]
            ant, adt = S.tok("accn", h % 2), S.tok("accd", h % 2)
            P_, Pt = P4[it % NP_], S.tok("P", it % NP_)
            pn, pd = ps_n[it % 2], ps_d[it % 2]
            pnt, pdt = S.tok("psn", it % 2), S.tok("psd", it % 2)
            for c in range(2):
                ci = tab[(d, r, blk + c)]
                S.op("pe", lambda E, pn=pn, P_=P_, c=c, ci=ci, h=h: E.matmul(
                    pn[:, :], V[:, ci, h * 64:(h + 1) * 64], P_[:, c, :], start=(c == 0), stop=(c == 1)),
                    reads=[Pt, S.tok("v")], writes=[pnt])
            for c in range(2):
                S.op("pe", lambda E, pd=pd, P_=P_, c=c: E.matmul(
                    pd[:, :], ones_bf[:, :], P_[:, c, :], start=(c == 0), stop=(c == 1)),
                    reads=[Pt, S.tok("ones")], writes=[pdt])
            if di == 0:
                S.op("dve", lambda E, an=an, pn=pn, qsl=qsl: E.tensor_copy(out=an[:, qsl], in_=pn[:, :]),
                     reads=[pnt], writes=[ant])
                S.op("act", lambda E, ad=ad, pd=pd, qsl=qsl: E.copy(out=ad[:, qsl], in_=pd[:, :]),
                     reads=[pdt], writes=[adt])
            else:
                S.op("dve", lambda E, an=an, pn=pn, qsl=qsl: E.tensor_tensor(
                    out=an[:, qsl], in0=pn[:, :], in1=an[:, qsl], op=ALU.add), reads=[pnt, ant], writes=[ant])
                S.op("dve", lambda E, ad=ad, pd=pd, qsl=qsl: E.tensor_tensor(
                    out=ad[:, qsl], in0=pd[:, :], in1=ad[:, qsl], op=ALU.add), reads=[pdt, adt], writes=[adt])
            last = (it + 1 == len(blocks)) or blocks[it + 1][0] != h
            if last:
                og, ogt = ostg[h % 2], S.tok("ostg", h % 2)
                S.op("dve", lambda E, ad=ad: E.reciprocal(out=ad[:, :], in_=ad[:, :]), reads=[adt], writes=[adt])
                S.op("dve", lambda E, an=an, ad=ad, og=og: E.tensor_tensor(out=og[:, :], in0=an[:, :], in1=ad[:, :], op=ALU.mult),
                     reads=[ant, adt], writes=[ogt])
                ft = S.tok("fin", h)
                finals.append(ft)
                S.dma("sp", o_out[h * 64:(h + 1) * 64, :], og[:, :], reads=[ogt], writes=[ft])

        for idx in range(len(blocks) + LOOK):
            if idx < len(blocks):
                emitA(idx)
            if idx - LOOK >= 0:
                emitB(idx - LOOK)
        S.finish(finals)
    return nc


def prep_attn_b(q, k, v):
    tab = b_chunks()
    ins = []
    for c in range(NCORES):
        b, qd = c // 4, c % 4
        kh = _halo(k, c, B_HALO)
        vh = _halo(v, c, B_HALO)
        valid = _kvalid(c, B_HALO)
        kbt = np.zeros((128, len(tab)), np.float32)
        for (d, r, m), idx in tab.items():
            u = B_HALO + r + d * (128 * m - 64 + np.arange(128))
            kbt[:, idx] = valid[u]
        ins.append({"qT": np.ascontiguousarray(q[b, qd * TPC:(qd + 1) * TPC].T),
                    "kT": np.ascontiguousarray(kh.T), "V": vh, "kbt": kbt})
    return ins


def run_attn_b(ins):
    nc = _get("attn_b", build_attn_b)
    res = run_bass_kernel_spmd(nc, ins, core_ids=list(range(NCORES)))
    return [r["oT"] for r in res.results]


def build_mlstm():
    nc = bass.Bass("TRN2", target_bir_lowering=False)
    NT = SEQ // 128
    qk_in = nc.dram_tensor("qkT", [256, SEQ], F32, kind="ExternalInput").ap()
    cw_in = nc.dram_tensor("cw", [128, 2, 5], F32, kind="ExternalInput").ap()
    cb_in = nc.dram_tensor("cb", [128, 2], F32, kind="ExternalInput").ap()
    v_in = nc.dram_tensor("Vm", [SEQ, 128], BF16, kind="ExternalInput").ap()
    o_in = nc.dram_tensor("Om", [SEQ, 128], F32, kind="ExternalInput").ap()
    g_in = nc.dram_tensor("G", [SEQ, 4], F32, kind="ExternalInput").ap()
    gb_in = nc.dram_tensor("gb", [4], F32, kind="ExternalInput").ap()
    hg_in = nc.dram_tensor("hg", [128], F32, kind="ExternalInput").ap()
    h_out = nc.dram_tensor("hm", [SEQ, 128], BF16, kind="ExternalOutput").ap()
    I32 = mybir.dt.int32
    cx = Ctx(nc)
    with cx.stack:
        S = Sched(nc, cx.stack)
        qkb = cx.sb([128, 2, SEQ], BF16, "qkb")
        vaug = cx.sb([128, NT, 132], BF16, "vaug")
        SG = cx.sb([128, NT, 128], F32, "SG")
        Hf = cx.sb([128, NT, 128], F32, "Hf")
        ostg = cx.sb([128, NT, 128], BF16, "ostg")
        xin = [cx.sb([128, 2, 1028], F32, "xin%d" % i) for i in range(2)]
        cacc = [cx.sb([128, 2, 1024], F32, "cacc%d" % i) for i in range(2)]
        cw = cx.sb([128, 2, 5], F32, "cw")
        cb = cx.sb([128, 2], F32, "cb")
        G = cx.sb([128, NT, 4], F32, "G")
        gb = cx.sb([128, 4], F32, "gb")
        hg = cx.sb([128, 128], F32, "hg")
        L = cx.sb([128, 2, NT], F32, "L")
        CB = cx.sb([128, 2, NT], F32, "CB")
        TOT = cx.sb([128, 2, NT], F32, "TOT")
        EB = cx.sb([128, 2, NT], F32, "EB")
        NEB = cx.sb([128, 2, NT], F32, "NEB")
        RS = cx.sb([128, 2, NT], F32, "RS")
        EA = cx.sb([128, 2, NT], F32, "EA")
        FF = cx.sb([128, 2, NT], F32, "FF")
        tri_i = cx.sb([128, 128], I32, "tri_i")
        tri = cx.sb([128, 2, 128], F32, "tri")
        onesf = cx.sb([128, 128], F32, "onesf")
        ident = cx.sb([128, 128], BF16, "ident")
        one1 = cx.sb([128, 1], F32, "one1")
        lns = cx.sb([128, 1], F32, "lns")
        epsb = cx.sb([128, 1], F32, "epsb")
        Cst = cx.sb([128, 132], F32, "Cst")
        Cbf = cx.sb([128, 132], BF16, "Cbf")
        WT = [cx.sb([128, 128], BF16, "WT%d" % i) for i in range(2)]
        kpp = [cx.sb([128, 128], BF16, "kpp%d" % i) for i in range(2)]
        sm = [cx.sb([128, 8], F32, "sm%d" % i) for i in range(2)]
        hs = [cx.sb([128, 128], F32, "hs%d" % i) for i in range(2)]
        sqj = cx.sb([128, 128], F32, "sqj")
        ps_s = [cx.ps([128, 128]) for _ in range(2)]
        ps_o = [cx.ps([128, 132]) for _ in range(2)]
        ps_t = [cx.ps([128, 128], BF16) for _ in range(2)]
        ps_c = [cx.ps([128, 132]) for _ in range(2)]

        ct = S.tok("const")
        S.op("pool", lambda E: E.iota(tri_i[:, :], [[1, 128]], base=0, channel_multiplier=-1), writes=[ct])
        S.op("dve", lambda E: E.tensor_copy(out=tri[:, 0, :], in_=tri_i[:, :]), reads=[ct], writes=[ct])
        S.op("dve", lambda E: E.tensor_scalar(out=sqj[:, :], in0=tri[:, 0, :], scalar1=0.0, scalar2=None, op0=ALU.is_equal),
             reads=[ct], writes=[ct])
        S.op("dve", lambda E: E.tensor_copy(out=ident[:, :], in_=sqj[:, :]), reads=[ct], writes=[ct])
        S.op("dve", lambda E: E.tensor_scalar(out=tri[:, 1, :], in0=tri[:, 0, :], scalar1=0.0, scalar2=None, op0=ALU.is_le),
             reads=[ct], writes=[ct])
        S.op("dve", lambda E: E.tensor_scalar(out=tri[:, 0, :], in0=tri[:, 0, :], scalar1=0.0, scalar2=None, op0=ALU.is_ge),
             reads=[ct], writes=[ct])
        S.op("pool", lambda E: E.memset(onesf[:, :], 1.0), writes=[ct])
        S.op("pool", lambda E: E.memset(one1[:, :], 1.0), writes=[ct])
        S.op("pool", lambda E: E.memset(lns[:, :], float(np.log(128.0 ** -0.5))), writes=[ct])
        S.op("pool", lambda E: E.memset(epsb[:, :], EPS), writes=[ct])
        S.op("pool", lambda E: E.memset(Cst[:, :], 0.0), writes=[S.tok("Cst")])
        S.op("pool", lambda E: E.memset(Cbf[:, :], 0.0), writes=[S.tok("Cbf")])
        S.op("pool", lambda E: E.memset(vaug[:, :, 128:132], 1.0), writes=[S.tok("vaug1")])
        S.dma("sp", cw[:, :, :], cw_in, writes=[S.tok("cw")])
        S.dma("sp", cb[:, :], cb_in, writes=[S.tok("cw")])
        S.dma("sp", G[:, :, :], g_in.rearrange("(n p) g -> p n g", p=128), writes=[S.tok("G")], allow_slow_non_contiguous=True)
        S.dma("sp", gb[:, :], gb_in.partition_broadcast(128), writes=[S.tok("gb")])
        S.dma("sp", hg[:, :], hg_in.partition_broadcast(128), writes=[S.tok("hg")])
        vt = S.tok("vaug")
        for i in range(4):
            S.dma("act", vaug[:, i * 16:(i + 1) * 16, 0:128],
                  v_in[i * 2048:(i + 1) * 2048, :].rearrange("(n p) e -> p n e", p=128), writes=[vt])
        sgt = S.tok("SG")
        for i in range(4):
            S.dma("act", SG[:, i * 16:(i + 1) * 16, :],
                  o_in[i * 2048:(i + 1) * 2048, :].rearrange("(n p) e -> p n e", p=128), writes=[sgt])
        gt = S.tok("gates")
        for j in range(4):
            S.op("dve", lambda E, j=j: E.tensor_scalar(out=G[:, :, j], in0=G[:, :, j], scalar1=gb[:, j:j + 1], scalar2=None,
                                                       op0=ALU.add), reads=[S.tok("G"), S.tok("gb")], writes=[S.tok("G")])
        for dr in range(2):
            S.op("act", lambda E, dr=dr: E.activation(out=L[:, dr, :], in_=G[:, :, 2 + dr], func=AF.Exp, scale=-1.0),
                 reads=[S.tok("G")], writes=[gt])
        S.op("act", lambda E: E.activation(out=L[:, :, :], in_=L[:, :, :], func=AF.Ln, bias=one1[:, 0:1]),
             reads=[gt, ct], writes=[gt])
        pcs = ps_o[0]
        for dr in range(2):
            S.op("pe", lambda E, dr=dr: E.matmul(pcs[:, dr * 64:(dr + 1) * 64], tri[:, dr, :], L[:, dr, :], start=True, stop=True),
                 reads=[gt, ct], writes=[S.tok("pso", 0)])
        S.op("dve", lambda E: E.tensor_copy(out=CB[:, :, :], in_=pcs[:, 0:128].rearrange("p (d n) -> p d n", d=2)),
             reads=[S.tok("pso", 0)], writes=[gt])
        pcs1 = ps_o[1]
        S.op("pe", lambda E: E.matmul(pcs1[:, 0:128], onesf[:, :], L[:, :, :], start=True, stop=True),
             reads=[gt, ct], writes=[S.tok("pso", 1)])
        S.op("dve", lambda E: E.tensor_copy(out=TOT[:, :, :], in_=pcs1[:, 0:128].rearrange("p (d n) -> p d n", d=2)),
             reads=[S.tok("pso", 1)], writes=[gt])
        S.op("act", lambda E: E.activation(out=EB[:, :, :], in_=CB[:, :, :], func=AF.Exp, scale=-1.0), reads=[gt], writes=[gt])
        S.op("act", lambda E: E.activation(out=FF[:, :, :], in_=TOT[:, :, :], func=AF.Exp, scale=-1.0), reads=[gt], writes=[gt])
        for dr in range(2):
            S.op("dve", lambda E, dr=dr: E.tensor_tensor(out=RS[:, dr, :], in0=CB[:, dr, :], in1=G[:, :, dr], op=ALU.add),
                 reads=[gt, S.tok("G")], writes=[gt])
        S.op("dve", lambda E: E.tensor_tensor(out=EA[:, :, :], in0=RS[:, :, :], in1=TOT[:, :, :], op=ALU.subtract),
             reads=[gt], writes=[gt])
        S.op("act", lambda E: E.activation(out=RS[:, :, :], in_=RS[:, :, :], func=AF.Exp, bias=lns[:, 0:1]), reads=[gt], writes=[gt])
        S.op("act", lambda E: E.activation(out=EA[:, :, :], in_=EA[:, :, :], func=AF.Exp, bias=lns[:, 0:1]), reads=[gt], writes=[gt])
        qkt = S.tok("qkb")
        NP = SEQ // 1024
        for pc in range(NP):
            xb, xbt = xin[pc % 2], S.tok("xin", pc % 2)
            ac, act_ = cacc[pc % 2], S.tok("cacc", pc % 2)
            t0 = pc * 1024
            lo, hi = max(0, t0 - 2), min(SEQ, t0 + 1026)
            if pc == 0 or pc == NP - 1:
                S.op("pool", lambda E, xb=xb: E.memset(xb[:, :, :], 0.0), writes=[xbt])
            S.dma("sp", xb[:, :, lo - (t0 - 2):hi - (t0 - 2)],
                  qk_in[:, lo:hi].rearrange("(j p) t -> p j t", p=128), writes=[xbt])
            for j in range(2):
                S.op("dve", lambda E, j=j, xb=xb, ac=ac: E.tensor_scalar(
                    out=ac[:, j, :], in0=xb[:, j, 0:1024], scalar1=cw[:, j, 0:1], scalar2=None, op0=ALU.mult),
                    reads=[xbt, S.tok("cw")], writes=[act_])
                for tap in range(1, 5):
                    S.op("dve", lambda E, j=j, xb=xb, ac=ac, tap=tap: E.scalar_tensor_tensor(
                        out=ac[:, j, :], in0=xb[:, j, tap:tap + 1024], scalar=cw[:, j, tap:tap + 1], in1=ac[:, j, :],
                        op0=ALU.mult, op1=ALU.add), reads=[xbt, act_], writes=[act_])
                S.op("act", lambda E, j=j, ac=ac, t0=t0: E.activation(
                    out=qkb[:, j, t0:t0 + 1024], in_=ac[:, j, :], func=AF.Silu, bias=cb[:, j:j + 1]),
                    reads=[act_, S.tok("cw")], writes=[qkt])
        for i in range(4):
            S.op("act", lambda E, i=i: E.activation(out=SG[:, i * 16:(i + 1) * 16, :], in_=SG[:, i * 16:(i + 1) * 16, :],
                                                   func=AF.Sigmoid), reads=[sgt], writes=[sgt])
        it = 0
        vts = [vt, S.tok("vaug1")]
        for dr in range(2):
            order = range(NT) if dr == 0 else range(NT - 1, -1, -1)
            if dr == 1:
                S.op("pool", lambda E: E.memset(Cst[:, :], 0.0), reads=[], writes=[S.tok("Cst")])
                S.op("pool", lambda E: E.memset(Cbf[:, :], 0.0), reads=[], writes=[S.tok("Cbf")])
            for n in order:
                tsl = slice(n * 128, (n + 1) * 128)
                pss, pst = ps_s[it % 2], S.tok("pss", it % 2)
                pso, pot = ps_o[it % 2], S.tok("pso", it % 2)
                ptt, ptk = ps_t[it % 2], S.tok("pst", it % 2)
                psc, pct = ps_c[it % 2], S.tok("psc", it % 2)
                W_, Wt = WT[it % 2], S.tok("WT", it % 2)
                kp, kpt = kpp[it % 2], S.tok("kpp", it % 2)
                s_, st = sm[it % 2], S.tok("sm", it % 2)
                S.op("pe", lambda E, pss=pss, tsl=tsl: E.matmul(pss[:, :], qkb[:, 1, tsl], qkb[:, 0, tsl], start=True, stop=True),
                     reads=[qkt], writes=[pst])
                S.op("dve", lambda E, pss=pss, W_=W_, dr=dr, n=n: E.scalar_tensor_tensor(
                    out=W_[:, :], in0=pss[:, :], scalar=RS[:, dr, n:n + 1], in1=tri[:, dr, :], op0=ALU.mult, op1=ALU.mult),
                    reads=[pst, gt, ct], writes=[Wt])
                S.op("pe", lambda E, pso=pso, W_=W_, n=n: E.matmul(pso[:, 0:130], W_[:, :], vaug[:, n, 0:130], start=True, stop=False),
                     reads=[Wt] + vts, writes=[pot])
                S.op("pe", lambda E, pso=pso, tsl=tsl: E.matmul(pso[:, 0:130], qkb[:, 0, tsl], Cbf[:, 0:130], start=False, stop=True),
                     reads=[qkt, S.tok("Cbf")], writes=[pot])
                S.op("act", lambda E, pso=pso, s_=s_, dr=dr, n=n: E.activation(
                    out=s_[:, 0:1], in_=pso[:, 128:129], func=AF.Abs, scale=EB[:, dr, n:n + 1]), reads=[pot, gt], writes=[st])
                S.op("dve", lambda E, s_=s_: E.tensor_scalar(out=s_[:, 1:2], in0=s_[:, 0:1], scalar1=1.0, scalar2=None, op0=ALU.max),
                     reads=[st], writes=[st])
                S.op("dve", lambda E, s_=s_: E.reciprocal(out=s_[:, 2:3], in_=s_[:, 1:2]), reads=[st], writes=[st])
                S.op("dve", lambda E, s_=s_, dr=dr, n=n: E.tensor_tensor(out=s_[:, 3:4], in0=s_[:, 2:3], in1=EB[:, dr, n:n + 1], op=ALU.mult),
                     reads=[st, gt], writes=[st])
                if dr == 0:
                    S.op("act", lambda E, pso=pso, s_=s_, n=n: E.activation(
                        out=Hf[:, n, :], in_=pso[:, 0:128], func=AF.Copy, scale=s_[:, 3:4]), reads=[pot, st], writes=[S.tok("Hf", n)])
                else:
                    h_, ht = hs[it % 2], S.tok("hs", it % 2)
                    S.op("dve", lambda E, pso=pso, s_=s_, n=n, h_=h_: E.scalar_tensor_tensor(
                        out=h_[:, :], in0=pso[:, 0:128], scalar=s_[:, 3:4], in1=Hf[:, n, :], op0=ALU.mult, op1=ALU.add),
                        reads=[pot, st, S.tok("Hf", n)], writes=[ht])
                    S.op("act", lambda E, h_=h_, s_=s_: E.activation(out=sqj[:, :], in_=h_[:, :], func=AF.Square, accum_out=s_[:, 4:5]),
                         reads=[ht, st], writes=[st, S.tok("sqj")])
                    S.op("act", lambda E, s_=s_: E.activation(out=s_[:, 5:6], in_=s_[:, 4:5], func=AF.Sqrt, scale=1.0 / 128,
                                                            bias=epsb[:, 0:1]), reads=[st, ct], writes=[st])
                    S.op("dve", lambda E, s_=s_: E.reciprocal(out=s_[:, 6:7], in_=s_[:, 5:6]), reads=[st], writes=[st])
                    S.op("dve", lambda E, h_=h_, s_=s_: E.scalar_tensor_tensor(
                        out=h_[:, :], in0=h_[:, :], scalar=s_[:, 6:7], in1=hg[:, :], op0=ALU.mult, op1=ALU.mult),
                        reads=[ht, st, S.tok("hg")], writes=[ht])
                    S.op("dve", lambda E, h_=h_, n=n: E.tensor_tensor(out=ostg[:, n, :], in0=h_[:, :], in1=SG[:, n, :], op=ALU.mult),
                         reads=[ht, sgt], writes=[S.tok("ostg")])
                S.op("pe", lambda E, ptt=ptt, tsl=tsl: E.transpose(ptt[:, :], qkb[:, 1, tsl], ident[:, :]),
                     reads=[qkt, ct], writes=[ptk])
                S.op("act", lambda E, ptt=ptt, kp=kp, dr=dr, n=n: E.activation(
                    out=kp[:, :], in_=ptt[:, :], func=AF.Copy, scale=EA[:, dr, n:n + 1]), reads=[ptk, gt], writes=[kpt])
                S.op("pe", lambda E, psc=psc, kp=kp, n=n: E.matmul(psc[:, 0:130], kp[:, :], vaug[:, n, 0:130], start=True, stop=True),
                     reads=[kpt] + vts, writes=[pct])
                S.op("dve", lambda E, psc=psc, dr=dr, n=n: E.scalar_tensor_tensor(
                    out=Cst[:, 0:130], in0=Cst[:, 0:130], scalar=FF[:, dr, n:n + 1], in1=psc[:, 0:130], op0=ALU.mult, op1=ALU.add),
                    reads=[pct, gt, S.tok("Cst")], writes=[S.tok("Cst")])
                S.op("act", lambda E: E.copy(out=Cbf[:, 0:130], in_=Cst[:, 0:130]), reads=[S.tok("Cst")], writes=[S.tok("Cbf")])
                it += 1
        finals = []
        for i in range(4):
            ft = S.tok("fin", i)
            finals.append(ft)
            S.dma("sp", h_out[i * 2048:(i + 1) * 2048, :].rearrange("(n p) e -> p n e", p=128), ostg[:, i * 16:(i + 1) * 16, :],
                  reads=[S.tok("ostg")], writes=[ft])
        S.finish(finals)
    return nc


def run_mlstm(ins):
    nc = _get("mlstm", build_mlstm)
    res = run_bass_kernel_spmd(nc, ins, core_ids=list(range(NCORES)))
    return [r["hm"] for r in res.results]


def build_final():
    nc = bass.Bass("TRN2", target_bir_lowering=False)
    xin = nc.dram_tensor("xT", [D, TPC], F32, kind="ExternalInput").ap()
    g_in = nc.dram_tensor("g", [D], F32, kind="ExternalInput").ap()
    xout = nc.dram_tensor("yT", [D, TPC], F32, kind="ExternalOutput").ap()
    cx = Ctx(nc)
    with cx.stack:
        S = Sched(nc, cx.stack)
        xT = cx.sb([128, 8, TPC], F32, "xT_sb")
        xo = [cx.sb([128, 8, 512], F32, "xo%d" % i) for i in range(2)]
        gcol = cx.sb([128, 8], F32, "gcol")
        ones_bf = cx.sb([128, 128], BF16, "ones")
        epsb = cx.sb([128, 1], F32, "eps")
        sq = cx.sb([128, 8, 512], BF16, "sq")
        rs = cx.sb([128, 512], F32, "rs")
        ps_ss = cx.ps([128, 512])
        scratch = {"sq": sq, "rs": rs, "eps": epsb, "sq_tok": S.tok("sq"), "rs_tok": S.tok("rs")}
        S.op("pool", lambda E: E.memset(ones_bf[:, :], 1.0), writes=[S.tok("ones")])
        S.op("pool", lambda E: E.memset(epsb[:, :], EPS), writes=[S.tok("eps")])
        S.dma("sp", gcol[:, :], g_in.rearrange("(k p) -> p k", p=128), writes=[S.tok("gcol")],
              allow_slow_non_contiguous=True)
        xin_v = xin.rearrange("(k p) t -> p k t", p=128)
        xout_v = xout.rearrange("(k p) t -> p k t", p=128)
        xtoks = [S.tok("x", i) for i in range(4)]
        for i in range(4):
            for k in range(0, 8, 4):
                S.dma("sp", xT[:, k:k + 4, i * 512:(i + 1) * 512], xin_v[:, k:k + 4, i * 512:(i + 1) * 512],
                      writes=[xtoks[i]])
        for t in (S.tok("ones"), S.tok("eps"), S.tok("gcol")):
            for e in ("act", "dve", "pe"):
                S._deps(e, [t], [])
        finals = []
        for gi in range(4):
            emit_rmsnorm(S, cx, xT, xtoks[gi], gi * 512, 512, gcol, xo[gi % 2], S.tok("xo", gi % 2), ones_bf, ps_ss,
                         S.tok("ps_ss"), scratch)
            ft = S.tok("fin", gi)
            finals.append(ft)
            S.dma("sp", xout_v[:, :, gi * 512:(gi + 1) * 512], xo[gi % 2][:, :, :], reads=[S.tok("xo", gi % 2)], writes=[ft])
        S.finish(finals)
    return nc


def run_final(xT_shards, g):
    nc = _get("final", build_final)
    in_maps = [{"xT": xT_shards[c], "g": g} for c in range(NCORES)]
    res = run_bass_kernel_spmd(nc, in_maps, core_ids=list(range(NCORES)))
    return [r["yT"] for r in res.results]


def _f32c(a):
    return np.ascontiguousarray(np.asarray(a, dtype=np.float32))


def _gather_tm(res, name):
    a = np.stack([np.asarray(res[c][name]) for c in range(NCORES)])
    return a.reshape(BATCH, SEQ, a.shape[-1])


def _gather_fm(res, name):
    a = np.stack([np.asarray(res[c][name]).T for c in range(NCORES)])
    return a.reshape(BATCH, SEQ, a.shape[-1])


def _layer_ab(xT, g, w_in, conv_w, conv_b, gate_b, hnorm_g, w_out):
    res = run_proj("AB", xT, g, w_in)
    qk = _gather_fm(res, "qkT")
    Vm = _gather_tm(res, "Vm")
    Om = _gather_tm(res, "Om")
    Gm = _gather_tm(res, "Gm")
    gate_b = _f32c(gate_b).reshape(16)
    ins = []
    for c in range(NCORES):
        b, h = c // 4, c % 4
        hs = slice(h * 128, (h + 1) * 128)
        ks = slice(512 + h * 128, 512 + (h + 1) * 128)
        qkT = np.ascontiguousarray(np.concatenate([qk[b, :, hs], qk[b, :, ks]], axis=1).T)
        cw = np.ascontiguousarray(np.stack([conv_w[:, hs].T, conv_w[:, ks].T], axis=1))
        cb = np.ascontiguousarray(np.stack([conv_b[hs], conv_b[ks]], axis=1))
        gi = [h, 4 + h, 8 + h, 12 + h]
        ins.append({"qkT": qkT, "cw": _f32c(cw), "cb": _f32c(cb), "Vm": np.ascontiguousarray(Vm[b, :, hs]),
                    "Om": np.ascontiguousarray(Om[b, :, hs]), "G": np.ascontiguousarray(Gm[b][:, gi]),
                    "gb": np.ascontiguousarray(gate_b[gi]), "hg": _f32c(hnorm_g[hs])})
    hm = run_mlstm(ins)
    qb = _gather_fm(res, "qbT")
    kb = _gather_fm(res, "kbT")
    Vb = _gather_tm(res, "Vb")
    ob = run_attn_b(prep_attn_b(qb, kb, Vb))
    mT = []
    for c in range(NCORES):
        b, qd = c // 4, c % 4
        parts = [np.asarray(hm[b * 4 + h])[qd * TPC:(qd + 1) * TPC].T for h in range(4)]
        parts.append(np.asarray(ob[c]))
        mT.append(np.ascontiguousarray(np.concatenate(parts, axis=0)))
    return run_out(xT, mT, w_out)


def _layer_c(xT, g, w_in, sink, w_out):
    res = run_proj("C", xT, g, w_in)
    q = _gather_fm(res, "qT")
    k = _gather_fm(res, "kT")
    v = _gather_tm(res, "V")
    oT = run_attn_c(prep_attn_c(q, k, v, _f32c(sink)))
    return run_out(xT, [np.asarray(o) for o in oT], w_out)


def kernel_unfused(x, norm_g, ffn_w1, ffn_w3, ffn_w2, ab_w_in, ab_conv_w, ab_conv_b, ab_gate_b, ab_hnorm_g, ab_w_out,
                   c_w_in, c_sink, c_w_out, final_g):
    x = np.asarray(x, dtype=np.float32)
    norm_g, ffn_w1, ffn_w3, ffn_w2 = (np.asarray(a, np.float32) for a in (norm_g, ffn_w1, ffn_w3, ffn_w2))
    ab_w_in, ab_conv_w, ab_conv_b, ab_gate_b, ab_hnorm_g, ab_w_out = (
        np.asarray(a, np.float32) for a in (ab_w_in, ab_conv_w, ab_conv_b, ab_gate_b, ab_hnorm_g, ab_w_out))
    c_w_in, c_sink, c_w_out, final_g = (np.asarray(a, np.float32) for a in (c_w_in, c_sink, c_w_out, final_g))
    xT = [np.ascontiguousarray(x[c // 4, (c % 4) * TPC:(c % 4 + 1) * TPC].T) for c in range(NCORES)]
    for l in range(4):
        j = l // 2
        xT = run_ffn(xT, _f32c(norm_g[l, 0]), _f32c(ffn_w1[l, 0]), _f32c(ffn_w3[l, 0]), _f32c(ffn_w2[l, 0]))
        if l % 2 == 0:
            xT = _layer_ab(xT, _f32c(norm_g[l, 1]), _f32c(ab_w_in[j]), ab_conv_w[j], ab_conv_b[j], ab_gate_b[j],
                           ab_hnorm_g[j], _f32c(ab_w_out[j]))
        else:
            xT = _layer_c(xT, _f32c(norm_g[l, 1]), _f32c(c_w_in[j]), c_sink[j], _f32c(c_w_out[j]))
        xT = run_ffn(xT, _f32c(norm_g[l, 2]), _f32c(ffn_w1[l, 1]), _f32c(ffn_w3[l, 1]), _f32c(ffn_w2[l, 1]))
    yT = run_final(xT, _f32c(final_g))
    out = np.empty((BATCH, SEQ, D), np.float32)
    for c in range(NCORES):
        out[c // 4, (c % 4) * TPC:(c % 4 + 1) * TPC] = np.asarray(yT[c]).T
    return out


def stage_ffn(S, nc, X, g_in, w1, w3, w2):
    NF = DFF // 128
    TP = 1024
    cx = Ctx(nc)
    with cx.stack:
        xT = cx.sb([128, 8, TPC], F32, "xT_sb")
        xn = cx.sb([128, 8, TP], BF16, "xn")
        gb = cx.sb([128, NF, TP], BF16, "gb")
        gcol = cx.sb([128, 8], F32, "gcol")
        ones_bf = cx.sb([128, 128], BF16, "ones")
        epsb = cx.sb([128, 1], F32, "eps")
        sq = cx.sb([128, 8, 512], BF16, "sq")
        rs = cx.sb([128, 512], F32, "rs")
        w13 = [cx.sb([128, 2, 8, 128], BF16, "w13_%d" % i) for i in range(3)]
        w2b = [cx.sb([128, NF, 128], BF16, "w2b_%d" % i) for i in range(2)]
        sil = [cx.sb([128, 512], F32, "sil%d" % i) for i in range(2)]
        ps_h = [[cx.ps([128, 512]) for _ in range(2)] for _ in range(2)]
        ps_y = [cx.ps([128, 512]) for _ in range(2)]
        ps_ss = cx.ps([128, 512])
        scratch = {"sq": sq, "rs": rs, "eps": epsb, "sq_tok": S.tok("sq"), "rs_tok": S.tok("rs")}
        S.op("pool", lambda E: E.memset(ones_bf[:, :], 1.0), writes=[S.tok("ones")])
        S.op("pool", lambda E: E.memset(epsb[:, :], EPS), writes=[S.tok("eps")])
        S.dma("sp", gcol[:, :], g_in.rearrange("(k p) -> p k", p=128), writes=[S.tok("gcol")],
              allow_slow_non_contiguous=True)
        xin_v = X.rearrange("(k p) t -> p k t", p=128)
        xtoks = [S.tok("x", i) for i in range(4)]
        for i in range(4):
            for k in range(0, 8, 4):
                S.dma("sp", xT[:, k:k + 4, i * 512:(i + 1) * 512], xin_v[:, k:k + 4, i * 512:(i + 1) * 512],
                      writes=[xtoks[i]])
        w1v = w1.rearrange("(k p) f -> p k f", p=128)
        w3v = w3.rearrange("(k p) f -> p k f", p=128)
        w2v = w2.rearrange("(f p) d -> p f d", p=128)
        for t in (S.tok("ones"), S.tok("eps"), S.tok("gcol")):
            for e in ("act", "dve", "pe"):
                S._deps(e, [t], [])
        for pas in range(TPC // TP):
            for gi in range(TP // 512):
                grp = pas * (TP // 512) + gi
                emit_rmsnorm(S, cx, xT, xtoks[grp], grp * 512, 512, gcol,
                             xn[:, :, gi * 512:(gi + 1) * 512], S.tok("xn", gi), ones_bf, ps_ss,
                             S.tok("ps_ss"), scratch)
            for f in range(NF):
                wb = w13[f % 3]
                wt = S.tok("w13", f % 3)
                S.dma("pool", wb[:, 0, :, :], w1v[:, :, f * 128:(f + 1) * 128], writes=[wt])
                S.dma("pool", wb[:, 1, :, :], w3v[:, :, f * 128:(f + 1) * 128], writes=[wt])
                for gi in range(TP // 512):
                    pb = ps_h[(f * 2 + gi) % 2]
                    pt = [S.tok("ps_h", (f * 2 + gi) % 2, j) for j in range(2)]
                    tsl = slice(gi * 512, (gi + 1) * 512)
                    for j in range(2):
                        for k in range(8):
                            S.op("pe", lambda E, j=j, k=k, pb=pb, wb=wb, tsl=tsl: E.matmul(
                                pb[j][:, :], wb[:, j, k, :], xn[:, k, tsl], start=(k == 0), stop=(k == 7)),
                                reads=[wt, S.tok("xn", gi)], writes=[pt[j]])
                    sb_ = sil[(f * 2 + gi) % 2]
                    st = S.tok("sil", (f * 2 + gi) % 2)
                    S.op("act", lambda E, pb=pb, sb_=sb_: E.activation(out=sb_[:, :], in_=pb[0][:, :], func=AF.Silu),
                         reads=[pt[0]], writes=[st])
                    S.op("dve", lambda E, pb=pb, sb_=sb_, f=f, tsl=tsl: E.tensor_tensor(
                        out=gb[:, f, tsl], in0=sb_[:, :], in1=pb[1][:, :], op=ALU.mult),
                        reads=[st, pt[1]], writes=[S.tok("gb", gi)])
            for d in range(8):
                wb = w2b[d % 2]
                wt = S.tok("w2b", d % 2)
                S.dma("pool", wb[:, 0:11, :], w2v[:, 0:11, d * 128:(d + 1) * 128], writes=[wt])
                S.dma("pool", wb[:, 11:22, :], w2v[:, 11:22, d * 128:(d + 1) * 128], writes=[wt])
                for gi in range(TP // 512):
                    grp = pas * (TP // 512) + gi
                    py = ps_y[(d * 2 + gi) % 2]
                    pyt = S.tok("ps_y", (d * 2 + gi) % 2)
                    tsl = slice(gi * 512, (gi + 1) * 512)
                    for f in range(NF):
                        S.op("pe", lambda E, f=f, py=py, wb=wb, tsl=tsl: E.matmul(
                            py[:, :], wb[:, f, :], gb[:, f, tsl], start=(f == 0), stop=(f == NF - 1)),
                            reads=[wt, S.tok("gb", gi)], writes=[pyt])
                    xs = xT[:, d, grp * 512:(grp + 1) * 512]
                    S.op("dve", lambda E, py=py, xs=xs: E.scalar_tensor_tensor(
                        out=xs, in0=py[:, :], scalar=0.5, in1=xs, op0=ALU.mult, op1=ALU.add),
                        reads=[pyt, xtoks[grp]], writes=[xtoks[grp]])
        for i in range(4):
            for k in range(0, 8, 4):
                S.dma("sp", xin_v[:, k:k + 4, i * 512:(i + 1) * 512], xT[:, k:k + 4, i * 512:(i + 1) * 512],
                      reads=[xtoks[i]], writes=[S.tok("xout", i, k)])
        S.end_stage()


def stage_proj(S, nc, kind, X, g_in, w, outs):
    NC_, specs = PROJ_SPECS[kind]
    cx = Ctx(nc)
    with cx.stack:
        xT = cx.sb([128, 8, TPC], F32, "xT_sb")
        xn = cx.sb([128, 8, TPC], BF16, "xn")
        wsb = cx.sb([128, 8, NC_], BF16, "wsb")
        gcol = cx.sb([128, 8], F32, "gcol")
        ones_bf = cx.sb([128, 128], BF16, "ones")
        epsb = cx.sb([128, 1], F32, "eps")
        sq = cx.sb([128, 8, 512], BF16, "sq")
        rs = cx.sb([128, 512], F32, "rs")
        stg_fm = [cx.sb([128, TPC], F32, "stgfm%d" % i) for i in range(2)]
        stg_tm = [cx.sb([128, 512], F32, "stgtm%d" % i) for i in range(3)]
        ps = [cx.ps([128, 512]) for _ in range(6)]
        ps_ss = cx.ps([128, 512])
        scratch = {"sq": sq, "rs": rs, "eps": epsb, "sq_tok": S.tok("sq"), "rs_tok": S.tok("rs")}
        S.op("pool", lambda E: E.memset(ones_bf[:, :], 1.0), writes=[S.tok("ones")])
        S.op("pool", lambda E: E.memset(epsb[:, :], EPS), writes=[S.tok("eps")])
        S.dma("sp", gcol[:, :], g_in.rearrange("(k p) -> p k", p=128), writes=[S.tok("gcol")],
              allow_slow_non_contiguous=True)
        xin_v = X.rearrange("(k p) t -> p k t", p=128)
        xtoks = [S.tok("x", i) for i in range(4)]
        for i in range(4):
            for k in range(0, 8, 4):
                S.dma("sp", xT[:, k:k + 4, i * 512:(i + 1) * 512], xin_v[:, k:k + 4, i * 512:(i + 1) * 512],
                      writes=[xtoks[i]])
        wv = w.rearrange("(k p) f -> p k f", p=128)
        wtok = S.tok("w")
        for k in range(8):
            S.dma("pool", wsb[:, k, :], wv[:, k, :], writes=[wtok])
        for t in (S.tok("ones"), S.tok("eps"), S.tok("gcol")):
            for e in ("act", "dve", "pe"):
                S._deps(e, [t], [])
        for gi in range(4):
            emit_rmsnorm(S, cx, xT, xtoks[gi], gi * 512, 512, gcol, xn[:, :, gi * 512:(gi + 1) * 512],
                         S.tok("xn", gi), ones_bf, ps_ss, S.tok("ps_ss"), scratch)
        pi = ei = si = ti = 0
        for name, c0, ncol, lay, dt in specs:
            if lay == "tm":
                continue
            M = 64 if lay == "fm64" else 128
            for ch in range(ncol // M):
                stg = stg_fm[si % 2]
                stt = S.tok("stgfm", si % 2)
                si += 1
                for gi in range(4):
                    p = ps[pi % 6]
                    pt = S.tok("ps", pi % 6)
                    pi += 1
                    for k in range(8):
                        S.op("pe", lambda E, p=p, k=k, cc=c0 + ch * M, M=M, gi=gi: E.matmul(
                            p[0:M, :], wsb[:, k, cc:cc + M], xn[:, k, gi * 512:(gi + 1) * 512],
                            start=(k == 0), stop=(k == 7)), reads=[wtok, S.tok("xn", gi)], writes=[pt])
                    dst = _stg_view(stg, dt, M, TPC)[:, gi * 512:(gi + 1) * 512]
                    if ei % 2 == 0:
                        S.op("act", lambda E, p=p, dst=dst, M=M: E.copy(out=dst, in_=p[0:M, :]), reads=[pt], writes=[stt])
                    else:
                        S.op("dve", lambda E, p=p, dst=dst, M=M: E.tensor_copy(out=dst, in_=p[0:M, :]), reads=[pt], writes=[stt])
                    ei += 1
                S.dma("sp", outs[name][ch * M:(ch + 1) * M, :], _stg_view(stg, dt, M, TPC), reads=[stt],
                      writes=[S.tok("fin", name, ch)])
        for tt in range(16):
            for name, c0, ncol, lay, dt in specs:
                if lay != "tm":
                    continue
                p = ps[pi % 6]
                pt = S.tok("ps", pi % 6)
                pi += 1
                for k in range(8):
                    S.op("pe", lambda E, p=p, k=k, c0=c0, ncol=ncol, tt=tt: E.matmul(
                        p[:, 0:ncol], xn[:, k, tt * 128:(tt + 1) * 128], wsb[:, k, c0:c0 + ncol],
                        start=(k == 0), stop=(k == 7)), reads=[wtok, S.tok("xn", tt // 4)], writes=[pt])
                stg = stg_tm[ti % 3]
                stt = S.tok("stgtm", ti % 3)
                ti += 1
                dst = _stg_view(stg, dt, 128, ncol)
                if ei % 2 == 0:
                    S.op("act", lambda E, p=p, dst=dst, ncol=ncol: E.copy(out=dst, in_=p[:, 0:ncol]), reads=[pt], writes=[stt])
                else:
                    S.op("dve", lambda E, p=p, dst=dst, ncol=ncol: E.tensor_copy(out=dst, in_=p[:, 0:ncol]), reads=[pt], writes=[stt])
                ei += 1
                S.dma("sp", outs[name][tt * 128:(tt + 1) * 128, :], dst, reads=[stt], writes=[S.tok("fin", name, "t", tt)],
                      allow_slow_non_contiguous=(ncol < 64))
        S.end_stage()


def stage_out(S, nc, X, MT, w):
    cx = Ctx(nc)
    with cx.stack:
        xT = cx.sb([128, 8, TPC], F32, "xT_sb")
        mT = cx.sb([128, 8, TPC], BF16, "mT_sb")
        wsb = cx.sb([128, 8, D], BF16, "wsb")
        ps = [cx.ps([128, 512]) for _ in range(4)]
        xin_v = X.rearrange("(k p) t -> p k t", p=128)
        m_v = MT.rearrange("(k p) t -> p k t", p=128)
        wv = w.rearrange("(k p) f -> p k f", p=128)
        for k in range(8):
            S.dma("pool", wsb[:, k, :], wv[:, k, :], writes=[S.tok("w")])
        for k in range(0, 8, 4):
            S.dma("sp", mT[:, k:k + 4, :], m_v[:, k:k + 4, :], writes=[S.tok("m")])
        xt = [S.tok("x", d) for d in range(8)]
        for d in range(8):
            S.dma("sp", xT[:, d, :], xin_v[:, d, :], writes=[xt[d]])
        i = 0
        for d in range(8):
            for gi in range(4):
                p, pt = ps[i % 4], S.tok("ps", i % 4)
                i += 1
                for k in range(8):
                    S.op("pe", lambda E, p=p, k=k, d=d, gi=gi: E.matmul(
                        p[:, :], wsb[:, k, d * 128:(d + 1) * 128], mT[:, k, gi * 512:(gi + 1) * 512],
                        start=(k == 0), stop=(k == 7)), reads=[S.tok("w"), S.tok("m")], writes=[pt])
                xs = xT[:, d, gi * 512:(gi + 1) * 512]
                S.op("dve", lambda E, p=p, xs=xs: E.tensor_tensor(out=xs, in0=p[:, :], in1=xs, op=ALU.add),
                     reads=[pt, xt[d]], writes=[xt[d]])
            S.dma("sp", xin_v[:, d, :], xT[:, d, :], reads=[xt[d]], writes=[S.tok("fin", d)])
        S.end_stage()


def stage_final(S, nc, X, g_in, xout):
    cx = Ctx(nc)
    with cx.stack:
        xT = cx.sb([128, 8, TPC], F32, "xT_sb")
        xo = [cx.sb([128, 8, 512], F32, "xo%d" % i) for i in range(2)]
        gcol = cx.sb([128, 8], F32, "gcol")
        ones_bf = cx.sb([128, 128], BF16, "ones")
        epsb = cx.sb([128, 1], F32, "eps")
        sq = cx.sb([128, 8, 512], BF16, "sq")
        rs = cx.sb([128, 512], F32, "rs")
        ps_ss = cx.ps([128, 512])
        scratch = {"sq": sq, "rs": rs, "eps": epsb, "sq_tok": S.tok("sq"), "rs_tok": S.tok("rs")}
        S.op("pool", lambda E: E.memset(ones_bf[:, :], 1.0), writes=[S.tok("ones")])
        S.op("pool", lambda E: E.memset(epsb[:, :], EPS), writes=[S.tok("eps")])
        S.dma("sp", gcol[:, :], g_in.rearrange("(k p) -> p k", p=128), writes=[S.tok("gcol")],
              allow_slow_non_contiguous=True)
        xin_v = X.rearrange("(k p) t -> p k t", p=128)
        xout_v = xout.rearrange("(k p) t -> p k t", p=128)
        xtoks = [S.tok("x", i) for i in range(4)]
        for i in range(4):
            for k in range(0, 8, 4):
                S.dma("sp", xT[:, k:k + 4, i * 512:(i + 1) * 512], xin_v[:, k:k + 4, i * 512:(i + 1) * 512],
                      writes=[xtoks[i]])
        for t in (S.tok("ones"), S.tok("eps"), S.tok("gcol")):
            for e in ("act", "dve", "pe"):
                S._deps(e, [t], [])
        for gi in range(4):
            emit_rmsnorm(S, cx, xT, xtoks[gi], gi * 512, 512, gcol, xo[gi % 2], S.tok("xo", gi % 2), ones_bf, ps_ss,
                         S.tok("ps_ss"), scratch)
            S.dma("sp", xout_v[:, :, gi * 512:(gi + 1) * 512], xo[gi % 2][:, :, :], reads=[S.tok("xo", gi % 2)],
                  writes=[S.tok("fin", gi)])
        S.end_stage()


def stage_attn_c(S, nc, q_in, k_in, v_in, kb_in, sink_in, o_out):
    KH = TPC + 256
    cx = Ctx(nc)
    slopes = alibi_slopes(16)
    with cx.stack:
        qT = cx.sb([64, 16, TPC], BF16, "qT_sb")
        kT = cx.sb([64, 4, KH], BF16, "kT_sb")
        V = cx.sb([128, 18, 256], BF16, "V_sb")
        kb = cx.sb([128, 18], F32, "kb_sb")
        esink = cx.sb([64, 16], F32, "esink")
        ones_bf = cx.sb([128, 64], BF16, "ones")
        bias = cx.sb([128, 3, 16, 128], F32, "bias")
        tmp = [cx.sb([128, 512], F32, "tmp%d" % i) for i in range(3)]
        P = [cx.sb([128, 512], BF16, "P%d" % i) for i in range(3)]
        rden = [cx.sb([64, 512], F32, "rden%d" % i) for i in range(2)]
        ostg = [cx.sb([64, 4, TPC], BF16, "ostg%d" % i) for i in range(2)]
        ps_s = [cx.ps([128, 512]) for _ in range(4)]
        ps_n = [cx.ps([64, 512]) for _ in range(2)]
        ps_d = [cx.ps([64, 512]) for _ in range(2)]
        S.dma("sp", qT[:, :, :], q_in.rearrange("(h e) t -> e h t", e=64), writes=[S.tok("q")])
        S.dma("sp", kT[:, :, :], k_in.rearrange("(h e) t -> e h t", e=64), writes=[S.tok("k")])
        S.dma("sp", V[:, :, :], v_in.rearrange("(c p) f -> p c f", p=128), writes=[S.tok("v")])
        S.dma("sp", kb[:, :], kb_in, writes=[S.tok("kb")])
        S.dma("sp", esink[:, :], sink_in.partition_broadcast(64), writes=[S.tok("esink")])
        S.op("act", lambda E: E.activation(out=esink[:, :], in_=esink[:, :], func=AF.Exp),
             reads=[S.tok("esink")], writes=[S.tok("esink")])
        S.op("pool", lambda E: E.memset(ones_bf[:, :], 1.0), writes=[S.tok("ones")])
        A, At = emit_absrel(S, cx, 3, 128, 128, "absrel")
        bt = S.tok("bias")
        for h in range(16):
            S.op("dve", lambda E, h=h: E.tensor_scalar(out=bias[:, :, h, :], in0=A[:, :, :], scalar1=-slopes[h],
                                                       scalar2=None, op0=ALU.mult), reads=[At], writes=[bt])
        scale = 64 ** -0.5
        it = 0
        for kvh in range(4):
            og = ostg[kvh % 2]
            ogt = S.tok("ostg", kvh % 2)
            for b in range(16):
                pn, pd = ps_n[it % 2], ps_d[it % 2]
                pnt, pdt = S.tok("psn", it % 2), S.tok("psd", it % 2)
                for c in range(3):
                    j = it * 3 + c
                    pss = ps_s[j % 4]
                    pst = S.tok("pss", j % 4)
                    S.op("pe", lambda E, pss=pss, kvh=kvh, b=b, c=c: E.matmul(
                        pss[:, :], kT[:, kvh, (b + c) * 128:(b + c + 1) * 128],
                        qT[:, kvh * 4:(kvh + 1) * 4, b * 128:(b + 1) * 128], start=True, stop=True),
                        reads=[S.tok("q"), S.tok("k")], writes=[pst])
                    tm_, tmt = tmp[j % 3], S.tok("tmp", j % 3)
                    S.op("dve", lambda E, pss=pss, tm_=tm_, kvh=kvh, c=c: E.scalar_tensor_tensor(
                        out=tm_[:, :], in0=pss[:, :], scalar=scale, in1=bias[:, c, kvh * 4:(kvh + 1) * 4, :],
                        op0=ALU.mult, op1=ALU.add), reads=[pst, bt], writes=[tmt])
                    P_, Pt = P[j % 3], S.tok("P", j % 3)
                    S.op("act", lambda E, tm_=tm_, P_=P_, b=b, c=c: E.activation(
                        out=P_[:, :], in_=tm_[:, :], func=AF.Exp, bias=kb[:, b + c:b + c + 1]),
                        reads=[tmt, S.tok("kb")], writes=[Pt])
                    S.op("pe", lambda E, pn=pn, P_=P_, kvh=kvh, b=b, c=c: E.matmul(
                        pn[:, :], V[:, b + c, kvh * 64:(kvh + 1) * 64], P_[:, :], start=(c == 0), stop=(c == 2)),
                        reads=[Pt, S.tok("v")], writes=[pnt])
                    S.op("pe", lambda E, pd=pd, P_=P_, c=c: E.matmul(
                        pd[:, :], ones_bf[:, :], P_[:, :], start=(c == 0), stop=(c == 2)),
                        reads=[Pt, S.tok("ones")], writes=[pdt])
                rd, rdt = rden[it % 2], S.tok("rden", it % 2)
                for g in range(4):
                    h = kvh * 4 + g
                    S.op("dve", lambda E, rd=rd, pd=pd, g=g, h=h: E.tensor_scalar(
                        out=rd[:, g * 128:(g + 1) * 128], in0=pd[:, g * 128:(g + 1) * 128],
                        scalar1=esink[:, h:h + 1], scalar2=None, op0=ALU.add),
                        reads=[pdt, S.tok("esink")], writes=[rdt])
                S.op("dve", lambda E, rd=rd: E.reciprocal(out=rd[:, :], in_=rd[:, :]), reads=[rdt], writes=[rdt])
                S.op("dve", lambda E, rd=rd, pn=pn, og=og, b=b: E.tensor_tensor(
                    out=og[:, :, b * 128:(b + 1) * 128], in0=pn[:, :].rearrange("p (g q) -> p g q", g=4),
                    in1=rd[:, :].rearrange("p (g q) -> p g q", g=4), op=ALU.mult),
                    reads=[pnt, rdt], writes=[ogt])
                it += 1
            S.dma("sp", o_out[kvh * 256:(kvh + 1) * 256, :].rearrange("(g e) t -> e g t", e=64), og[:, :, :],
                  reads=[ogt], writes=[S.tok("fin", kvh)])
        S.end_stage()


def stage_attn_b(S, nc, q_in, k_in, v_in, kb_in, o_out):
    KH = TPC + 2 * B_HALO
    cx = Ctx(nc)
    slopes = alibi_slopes(8)
    tab = b_chunks()
    NCH = len(tab)
    with cx.stack:
        qT = cx.sb([128, 4, TPC], BF16, "qT_sb")
        kT = cx.sb([128, 4, KH], BF16, "kT_sb")
        V = cx.sb([128, NCH, 512], BF16, "V_sb")
        kb = cx.sb([128, NCH], F32, "kb_sb")
        ones_bf = cx.sb([128, 64], BF16, "ones")
        bias = cx.sb([128, 8, 3, 2, 128], F32, "bias")
        tmp = [cx.sb([128, 2, 128], F32, "tmp%d" % i) for i in range(3)]
        P = [cx.sb([128, 2, 128], BF16, "P%d" % i) for i in range(3)]
        accn = [cx.sb([64, TPC], F32, "accn%d" % i) for i in range(2)]
        accd = [cx.sb([64, TPC], F32, "accd%d" % i) for i in range(2)]
        ostg = [cx.sb([64, TPC], BF16, "ostg%d" % i) for i in range(2)]
        ps_s = [cx.ps([128, 2, 128]) for _ in range(3)]
        ps_n = [cx.ps([64, 128]) for _ in range(2)]
        ps_d = [cx.ps([64, 128]) for _ in range(2)]
        S.dma("sp", qT[:, :, :], q_in.rearrange("(k p) t -> p k t", p=128), writes=[S.tok("q")])
        for k in range(4):
            S.dma("sp", kT[:, k, :], k_in[k * 128:(k + 1) * 128, :], writes=[S.tok("k")])
        S.dma("sp", kb[:, :], kb_in, writes=[S.tok("kb")])
        for d in B_DILS:
            for r in range(d):
                M = 16 // d + 1
                c0 = tab[(d, r, 0)]
                start = B_HALO + r - 64 * d
                src = bass.AP(v_in.tensor, v_in.offset + start * 512, [[d * 512, 128], [d * 128 * 512, M], [1, 512]])
                S.dma("sp" if (r % 2 == 0) else "act", V[:, c0:c0 + M, :], src, writes=[S.tok("v")])
        S.op("pool", lambda E: E.memset(ones_bf[:, :], 1.0), writes=[S.tok("ones")])
        A, At = emit_absrel(S, cx, 2, 64, 64, "absrel")
        bt = S.tok("bias")
        for h in range(8):
            for di, d in enumerate(B_DILS):
                S.op("dve", lambda E, h=h, di=di, d=d: E.tensor_scalar(
                    out=bias[:, h, di, :, :], in0=A[:, :, :], scalar1=-slopes[h] * d, scalar2=None, op0=ALU.mult),
                    reads=[At], writes=[bt])
        scale = 64 ** -0.5
        it = 0
        for h in range(8):
            pb, hp = (h % 2) * 64, h // 2
            an, ad = accn[h % 2], accd[h % 2]
            ant, adt = S.tok("accn", h % 2), S.tok("accd", h % 2)
            for di, d in enumerate(B_DILS):
                for r in range(d):
                    for blk in range(16 // d):
                        q0 = r + d * 128 * blk
                        qsl = slice(q0, q0 + d * 127 + 1, d)
                        pss, pst = ps_s[it % 3], S.tok("pss", it % 3)
                        for c in range(2):
                            u0 = B_HALO + r + d * (128 * (blk + c) - 64)
                            S.op("pe", lambda E, pss=pss, c=c, u0=u0, d=d, qsl=qsl, pb=pb, hp=hp: E.matmul(
                                pss[:, c, :], kT[pb:pb + 64, hp, u0:u0 + d * 127 + 1:d], qT[pb:pb + 64, hp, qsl],
                                start=True, stop=True), reads=[S.tok("q"), S.tok("k")], writes=[pst])
                        tm_, tmt = tmp[it % 3], S.tok("tmp", it % 3)
                        S.op("dve", lambda E, pss=pss, tm_=tm_, h=h, di=di: E.scalar_tensor_tensor(
                            out=tm_[:, :, :], in0=pss[:, :, :], scalar=scale, in1=bias[:, h, di, :, :],
                            op0=ALU.mult, op1=ALU.add), reads=[pst, bt], writes=[tmt])
                        P_, Pt = P[it % 3], S.tok("P", it % 3)
                        for c in range(2):
                            ci = tab[(d, r, blk + c)]
                            S.op("act", lambda E, tm_=tm_, P_=P_, c=c, ci=ci: E.activation(
                                out=P_[:, c, :], in_=tm_[:, c, :], func=AF.Exp, bias=kb[:, ci:ci + 1]),
                                reads=[tmt, S.tok("kb")], writes=[Pt])
                        pn, pd = ps_n[it % 2], ps_d[it % 2]
                        pnt, pdt = S.tok("psn", it % 2), S.tok("psd", it % 2)
                        for c in range(2):
                            ci = tab[(d, r, blk + c)]
                            S.op("pe", lambda E, pn=pn, P_=P_, c=c, ci=ci, h=h: E.matmul(
                                pn[:, :], V[:, ci, h * 64:(h + 1) * 64], P_[:, c, :], start=(c == 0), stop=(c == 1)),
                                reads=[Pt, S.tok("v")], writes=[pnt])
                        for c in range(2):
                            S.op("pe", lambda E, pd=pd, P_=P_, c=c: E.matmul(
                                pd[:, :], ones_bf[:, :], P_[:, c, :], start=(c == 0), stop=(c == 1)),
                                reads=[Pt, S.tok("ones")], writes=[pdt])
                        if di == 0:
                            S.op("dve", lambda E, an=an, pn=pn, qsl=qsl: E.tensor_copy(out=an[:, qsl], in_=pn[:, :]),
                                 reads=[pnt], writes=[ant])
                            S.op("act", lambda E, ad=ad, pd=pd, qsl=qsl: E.copy(out=ad[:, qsl], in_=pd[:, :]),
                                 reads=[pdt], writes=[adt])
                        else:
                            S.op("dve", lambda E, an=an, pn=pn, qsl=qsl: E.tensor_tensor(
                                out=an[:, qsl], in0=pn[:, :], in1=an[:, qsl], op=ALU.add), reads=[pnt, ant], writes=[ant])
                            S.op("dve", lambda E, ad=ad, pd=pd, qsl=qsl: E.tensor_tensor(
                                out=ad[:, qsl], in0=pd[:, :], in1=ad[:, qsl], op=ALU.add), reads=[pdt, adt], writes=[adt])
                        it += 1
            og, ogt = ostg[h % 2], S.tok("ostg", h % 2)
            S.op("dve", lambda E, ad=ad: E.reciprocal(out=ad[:, :], in_=ad[:, :]), reads=[adt], writes=[adt])
            S.op("dve", lambda E, an=an, ad=ad, og=og: E.tensor_tensor(out=og[:, :], in0=an[:, :], in1=ad[:, :], op=ALU.mult),
                 reads=[ant, adt], writes=[ogt])
            S.dma("sp", o_out[h * 64:(h + 1) * 64, :], og[:, :], reads=[ogt], writes=[S.tok("fin", h)])
        S.end_stage()


EX_GROUPS = [[0, 1, 2, 3], [4, 5, 6, 7]]


def exch_rows(items):
    r = 0
    for lay, DST, F, H, dt in items:
        nb = F * H * (4 if dt == F32 else 2)
        r += 2 * ((nb // 4 + 511) // 512)
    return r


def _blockview(rows_ap, lay, F, H, dt):
    v = rows_ap if dt == F32 else rows_ap.bitcast(BF16)
    if lay == "fm":
        return v.rearrange("r (a h) -> (r a) h", h=H)
    return v.rearrange("r (a f) -> (r a) f", f=F)


def exch_alloc(nc, items, tag):
    bufs = []
    for n, (lay, DST, F, H, dt) in enumerate(items):
        rows = (F * H * (4 if dt == F32 else 2) // 4 + 511) // 512
        assert rows <= 512
        for side in range(2):
            bufs.append((nc.dram_tensor("EXS_%s_%d_%d" % (tag, n, side), [rows, 512], F32),
                         nc.dram_tensor("EXO_%s_%d_%d" % (tag, n, side), [4 * rows, 512], F32), rows))
    return bufs


def stage_exch(S, nc, items, sel_in, bufs):
    cx = Ctx(nc)
    with cx.stack:
        sel = cx.sb([128, 6], F32, "sel")
        S.dma("sp", sel[:, :], sel_in, writes=[S.tok("sel")])
        bi = 0
        for lay, DST, F, H, dt in items:
            for side in range(2):
                EXS, EXO, rows = bufs[bi]
                t0 = H if side == 0 else TPC
                src = DST[:, t0:t0 + H] if lay == "fm" else DST[t0:t0 + H, :]
                S.dma("sp" if side == 0 else "act", _blockview(EXS.ap(), lay, F, H, dt), src,
                      writes=[S.tok("exs", bi)], allow_slow_non_contiguous=(lay == "fm" and H < 64))
                bi += 1
        for b in range(len(bufs)):
            EXS, EXO, rows = bufs[b]
            S.cc("AllGather", [EXS.ap().opt()], [EXO.ap().opt()], EX_GROUPS, reads=[S.tok("exs", b)], writes=[S.tok("exo", b)])
        bi = 0
        n = 0
        for lay, DST, F, H, dt in items:
            K = (F if lay == "fm" else H) // 128
            W = H if lay == "fm" else F
            pat = "(k p) h -> p k h" if lay == "fm" else "(c p) f -> p c f"
            for side in range(2):
                b = bi + (1 if side == 0 else 0)
                EXS, EXO, rows = bufs[b]
                exo = EXO.ap()
                cands = [cx.sb([128, K, W], dt, "exc%d_%d_%d" % (n, side, j)) for j in range(3)]
                o = cx.sb([128, K, W], dt, "exo%d_%d" % (n, side))
                tt = S.tok("ext", n, side)
                for j in range(3):
                    slot = j + side
                    S.dma(("sp", "act", "sp")[j], cands[j][:, :, :],
                          _blockview(exo[slot * rows:(slot + 1) * rows, :], lay, F, H, dt).rearrange(pat, p=128),
                          reads=[S.tok("exo", b)], writes=[tt], allow_slow_non_contiguous=(W < 64))
                S.op("dve", lambda E, a=cands[0], o=o, side=side: E.tensor_scalar(
                    out=o[:, :, :], in0=a[:, :, :], scalar1=sel[:, 3 * side:3 * side + 1], scalar2=None, op0=ALU.mult),
                    reads=[tt, S.tok("sel")], writes=[tt])
                for j in (1, 2):
                    S.op("dve", lambda E, b_=cands[j], o=o, side=side, j=j: E.scalar_tensor_tensor(
                        out=o[:, :, :], in0=b_[:, :, :], scalar=sel[:, 3 * side + j:3 * side + j + 1], in1=o[:, :, :],
                        op0=ALU.mult, op1=ALU.add), reads=[tt, S.tok("sel")], writes=[tt])
                t0 = 0 if side == 0 else H + TPC
                dst = DST[:, t0:t0 + H] if lay == "fm" else DST[t0:t0 + H, :]
                S.dma("sp", dst.rearrange(pat, p=128), o[:, :, :], reads=[tt], writes=[S.tok("exd", n, side)],
                      allow_slow_non_contiguous=(W < 64))
            bi += 2
            n += 1
        S.end_stage()


def stage_mlstm(S, nc, QKH, cw_in, cb_in, VM, OM, GM, gb_in, hg_in, mf_in, mb_in, STS, STA, MT_out):
    NT = TPC // 128
    I32 = mybir.dt.int32
    cx = Ctx(nc)
    with cx.stack:
        qkb = cx.sb([128, 8, TPC], BF16, "qkb")
        vaug = cx.sb([128, NT, 4, 132], BF16, "vaug")
        SG = cx.sb([128, NT, 512], BF16, "SG")
        Hf = cx.sb([128, NT, 4, 128], F32, "Hf")
        cw = cx.sb([128, 8, 5], F32, "cw")
        cb = cx.sb([128, 8], F32, "cb")
        G = cx.sb([128, NT, 16], F32, "G")
        gb = cx.sb([128, 16], F32, "gb")
        hg = cx.sb([128, 512], F32, "hg")
        MF = cx.sb([128, 2, 8], F32, "MF")
        L = cx.sb([128, 8, NT], F32, "L")
        CB = cx.sb([128, 8, NT], F32, "CB")
        TOT = cx.sb([128, 8, NT], F32, "TOT")
        EB = cx.sb([128, 8, NT], F32, "EB")
        RS = cx.sb([128, 8, NT], F32, "RS")
        EA = cx.sb([128, 8, NT], F32, "EA")
        FF = cx.sb([128, 8, NT], F32, "FF")
        tri_i = cx.sb([128, 128], I32, "tri_i")
        tri = cx.sb([128, 2, 128], F32, "tri")
        onesf = cx.sb([128, 128], F32, "onesf")
        ident = cx.sb([128, 128], BF16, "ident")
        one1 = cx.sb([128, 1], F32, "one1")
        lns = cx.sb([128, 1], F32, "lns")
        epsb = cx.sb([128, 1], F32, "epsb")
        Cst = cx.sb([128, 8, 132], F32, "Cst")
        Cbf = cx.sb([128, 8, 132], BF16, "Cbf")
        sqj = cx.sb([128, 128], F32, "sqj")
        R = 4
        kpp = [cx.sb([128, 128], BF16, "kpp%d" % i) for i in range(R)]
        pbank = [cx.ps([128, 512]) for _ in range(R)]
        ptb = cx.ps([128, R, 128], BF16)
        ptf = cx.ps([128, 4, 128], BF16)

        def consts():
            ct = S.tok("const")
            S.op("pool", lambda E: E.iota(tri_i[:, :], [[1, 128]], base=0, channel_multiplier=-1), writes=[ct])
            S.op("dve", lambda E: E.tensor_copy(out=tri[:, 0, :], in_=tri_i[:, :]), reads=[ct], writes=[ct])
            S.op("dve", lambda E: E.tensor_scalar(out=sqj[:, :], in0=tri[:, 0, :], scalar1=0.0, scalar2=None, op0=ALU.is_equal),
                 reads=[ct], writes=[ct])
            S.op("dve", lambda E: E.tensor_copy(out=ident[:, :], in_=sqj[:, :]), reads=[ct], writes=[ct])
            S.op("dve", lambda E: E.tensor_scalar(out=tri[:, 1, :], in0=tri[:, 0, :], scalar1=0.0, scalar2=None, op0=ALU.is_le),
                 reads=[ct], writes=[ct])
            S.op("dve", lambda E: E.tensor_scalar(out=tri[:, 0, :], in0=tri[:, 0, :], scalar1=0.0, scalar2=None, op0=ALU.is_ge),
                 reads=[ct], writes=[ct])
            S.op("pool", lambda E: E.memset(onesf[:, :], 1.0), writes=[ct])
            S.op("pool", lambda E: E.memset(one1[:, :], 1.0), writes=[ct])
            S.op("pool", lambda E: E.memset(lns[:, :], float(np.log(128.0 ** -0.5))), writes=[ct])
            S.op("pool", lambda E: E.memset(epsb[:, :], EPS), writes=[ct])
            S.op("pool", lambda E: E.memset(Cst[:, :, :], 0.0), writes=[ct])
            S.op("pool", lambda E: E.memset(vaug[:, :, :, 128:132], 1.0), writes=[ct])
            return ct

        ct = consts()
        for tap in range(5):
            S.dma("sp", cw[:, :, tap], cw_in[tap].rearrange("(k p) -> p k", p=128), writes=[S.tok("cw")],
                  allow_slow_non_contiguous=True)
        S.dma("sp", cb[:, :], cb_in.rearrange("(k p) -> p k", p=128), writes=[S.tok("cw")], allow_slow_non_contiguous=True)
        S.dma("sp", G[:, :, :], GM.rearrange("(n p) g -> p n g", p=128), writes=[S.tok("G")])
        S.dma("sp", gb[:, :], gb_in.partition_broadcast(128), writes=[S.tok("gb")])
        S.dma("sp", hg[:, :], hg_in.partition_broadcast(128), writes=[S.tok("hg")])
        S.dma("sp", MF[:, 0, :], mf_in, writes=[S.tok("MF")])
        S.dma("sp", MF[:, 1, :], mb_in, writes=[S.tok("MF")])
        vt = S.tok("vaug")
        for h in range(4):
            S.dma("act", vaug[:, :, h, 0:128], VM[:, h * 128:(h + 1) * 128].rearrange("(n p) e -> p n e", p=128), writes=[vt])
        gt = S.tok("gates")
        for j in range(16):
            S.op("dve", lambda E, j=j: E.tensor_scalar(out=G[:, :, j], in0=G[:, :, j], scalar1=gb[:, j:j + 1], scalar2=None,
                                                       op0=ALU.add), reads=[S.tok("G"), S.tok("gb")], writes=[S.tok("G")])
        S.op("act", lambda E: E.activation(out=L[:, :, :].rearrange("p c n -> p n c"), in_=G[:, :, 8:16], func=AF.Exp, scale=-1.0),
             reads=[S.tok("G")], writes=[gt])
        S.op("act", lambda E: E.activation(out=L[:, :, :], in_=L[:, :, :], func=AF.Ln, bias=one1[:, 0:1]),
             reads=[gt, ct], writes=[gt])
        pcs = pbank[0]
        for dr in range(2):
            S.op("pe", lambda E, dr=dr: E.matmul(pcs[:, dr * 64:(dr + 1) * 64], tri[:, dr, :], L[:, dr * 4:(dr + 1) * 4, :],
                                                 start=True, stop=True), reads=[gt, ct], writes=[S.tok("pb", 0)])
        S.op("dve", lambda E: E.tensor_copy(out=CB[:, :, :], in_=pcs[:, 0:128].rearrange("p (c n) -> p c n", c=8)),
             reads=[S.tok("pb", 0)], writes=[gt])
        pcs1 = pbank[1]
        S.op("pe", lambda E: E.matmul(pcs1[:, 0:128], onesf[:, :], L[:, :, :], start=True, stop=True),
             reads=[gt, ct], writes=[S.tok("pb", 1)])
        S.op("dve", lambda E: E.tensor_copy(out=TOT[:, :, :], in_=pcs1[:, 0:128].rearrange("p (c n) -> p c n", c=8)),
             reads=[S.tok("pb", 1)], writes=[gt])
        S.op("act", lambda E: E.activation(out=EB[:, :, :], in_=CB[:, :, :], func=AF.Exp, scale=-1.0), reads=[gt], writes=[gt])
        S.op("act", lambda E: E.activation(out=FF[:, :, :], in_=TOT[:, :, :], func=AF.Exp, scale=-1.0), reads=[gt], writes=[gt])
        S.op("dve", lambda E: E.tensor_tensor(out=RS[:, :, :], in0=CB[:, :, :], in1=G[:, :, 0:8].rearrange("p n c -> p c n"), op=ALU.add),
             reads=[gt, S.tok("G")], writes=[gt])
        S.op("dve", lambda E: E.tensor_tensor(out=EA[:, :, :], in0=RS[:, :, :], in1=TOT[:, :, :], op=ALU.subtract),
             reads=[gt], writes=[gt])
        S.op("act", lambda E: E.activation(out=RS[:, :, :], in_=RS[:, :, :], func=AF.Exp, bias=lns[:, 0:1]), reads=[gt, ct], writes=[gt])
        S.op("act", lambda E: E.activation(out=EA[:, :, :], in_=EA[:, :, :], func=AF.Exp, bias=lns[:, 0:1]), reads=[gt, ct], writes=[gt])
        qkt = S.tok("qkb")
        sgt = S.tok("SG")
        cx1 = Ctx(nc)
        with cx1.stack:
            xin = [cx1.sb([128, TPC + 4], F32, "xin%d" % i) for i in range(2)]
            cacc = [cx1.sb([128, TPC], F32, "cacc%d" % i) for i in range(2)]
            ostage = [cx1.sb([128, 4, 512], F32, "ostage%d" % i) for i in range(2)]
            qk_v = QKH.rearrange("(k p) t -> p k t", p=128)
            for k in range(8):
                xb, xbt = xin[k % 2], S.tok("xin", k % 2)
                ac, act_ = cacc[k % 2], S.tok("cacc", k % 2)
                S.dma("sp", xb[:, :], qk_v[:, k, :], writes=[xbt])
                S.op("dve", lambda E, k=k, xb=xb, ac=ac: E.tensor_scalar(
                    out=ac[:, :], in0=xb[:, 0:TPC], scalar1=cw[:, k, 0:1], scalar2=None, op0=ALU.mult),
                    reads=[xbt, S.tok("cw")], writes=[act_])
                for tap in range(1, 5):
                    S.op("dve", lambda E, k=k, xb=xb, ac=ac, tap=tap: E.scalar_tensor_tensor(
                        out=ac[:, :], in0=xb[:, tap:tap + TPC], scalar=cw[:, k, tap:tap + 1], in1=ac[:, :],
                        op0=ALU.mult, op1=ALU.add), reads=[xbt, act_], writes=[act_])
                S.op("act", lambda E, k=k, ac=ac: E.activation(out=qkb[:, k, :], in_=ac[:, :], func=AF.Silu, bias=cb[:, k:k + 1]),
                     reads=[act_, S.tok("cw")], writes=[qkt])
            for i in range(4):
                ob, obt = ostage[i % 2], S.tok("ostage", i % 2)
                S.dma("act", ob[:, :, :], OM[i * 512:(i + 1) * 512, :].rearrange("(n p) e -> p n e", p=128), writes=[obt])
                S.op("act", lambda E, i=i, ob=ob: E.activation(out=SG[:, i * 4:(i + 1) * 4, :], in_=ob[:, :, :], func=AF.Sigmoid),
                     reads=[obt], writes=[sgt])
            S.end_stage()
        ct = S.tok("const")
        gt = S.tok("gates")
        qkt = S.tok("qkb")
        sgt = S.tok("SG")
        vt = S.tok("vaug")

        def state_steps(ch, n, slot):
            h = ch % 4
            tsl = slice(n * 128, (n + 1) * 128)
            pb, pbt = pbank[slot], S.tok("pb", slot)
            kp, kpt = kpp[slot], S.tok("kpp", slot)
            S.op("pe", lambda E: E.transpose(ptb[:, slot, :], qkb[:, 4 + h, tsl], ident[:, :]),
                 reads=[qkt, ct], writes=[S.tok("ptb", slot)])
            S.op("act", lambda E: E.activation(out=kp[:, :], in_=ptb[:, slot, :], func=AF.Copy, scale=EA[:, ch, n:n + 1]),
                 reads=[S.tok("ptb", slot), gt], writes=[kpt])
            S.op("pe", lambda E: E.matmul(pb[:, 260:390], kp[:, :], vaug[:, n, h, 0:130], start=True, stop=True),
                 reads=[kpt, vt, ct], writes=[S.tok("pbc", slot)])
            S.op("dve", lambda E: E.scalar_tensor_tensor(
                out=Cst[:, ch, 0:130], in0=Cst[:, ch, 0:130], scalar=FF[:, ch, n:n + 1], in1=pb[:, 260:390],
                op0=ALU.mult, op1=ALU.add), reads=[S.tok("pbc", slot), gt, S.tok("Cst", ch)], writes=[S.tok("Cst", ch)])

        it = 0
        for i in range(NT):
            for ch in range(8):
                n = i if ch < 4 else NT - 1 - i
                state_steps(ch, n, it % R)
                it += 1
        TS = cx.sb([128, 8], F32, "TS")
        S.op("dve", lambda E: E.tensor_reduce(out=TS[:, :], in_=TOT[:, :, :], axis=AX.X, op=ALU.add), reads=[gt], writes=[S.tok("TS")])
        for ch in range(8):
            S.op("dve", lambda E, ch=ch: E.tensor_copy(out=Cst[:, ch, 130:131], in_=TS[:, ch:ch + 1]),
                 reads=[S.tok("TS"), S.tok("Cst", ch)], writes=[S.tok("Cst", ch)])
        S.dma("sp", STS.ap().rearrange("(c p) x -> p c x", p=128), Cst[:, :, :],
              reads=[S.tok("Cst", ch) for ch in range(8)], writes=[S.tok("sts")])
        S.cc("AllGather", [STS.ap().opt()], [STA.ap().opt()], [list(range(8))], reads=[S.tok("sts")], writes=[S.tok("sta")])
        cx2 = Ctx(nc)
        with cx2.stack:
            ST = cx2.sb([128, 8, 8, 132], F32, "ST")
            MT_ = cx2.sb([128, 8, 8], F32, "MT")
            ACC = cx2.sb([128, 8, 8], F32, "ACC")
            COEF = cx2.sb([128, 8, 8], F32, "COEF")
            mstg = cx2.sb([128, 4, TPC], BF16, "mstg")
            WT = [cx2.sb([128, 128], BF16, "WT%d" % i) for i in range(R)]
            sm = [cx2.sb([128, 8], F32, "sm%d" % i) for i in range(R)]
            hs = [cx2.sb([128, 128], F32, "hs%d" % i) for i in range(R)]
            hb = [cx2.sb([128, 128], BF16, "hb%d" % i) for i in range(R)]
            stt = S.tok("ST")
            for r in range(8):
                S.dma("sp" if r % 2 == 0 else "act", ST[:, r, :, :],
                      STA.ap()[r * 1024:(r + 1) * 1024, :].rearrange("(c p) x -> p c x", p=128), reads=[S.tok("sta")], writes=[stt])
            cft = S.tok("coef")
            S.op("pool", lambda E: E.memset(ACC[:, :, :], 0.0), writes=[cft])
            for r in range(8):
                for dr in range(2):
                    S.op("dve", lambda E, r=r, dr=dr: E.tensor_scalar(
                        out=MT_[:, r, dr * 4:(dr + 1) * 4], in0=ST[:, r, dr * 4:(dr + 1) * 4, 130], scalar1=MF[:, dr, r:r + 1],
                        scalar2=None, op0=ALU.mult), reads=[stt, S.tok("MF")], writes=[cft])
            for r in range(6, -1, -1):
                S.op("dve", lambda E, r=r: E.tensor_tensor(out=ACC[:, r, 0:4], in0=ACC[:, r + 1, 0:4], in1=MT_[:, r + 1, 0:4], op=ALU.add),
                     reads=[cft], writes=[cft])
            for r in range(1, 8):
                S.op("dve", lambda E, r=r: E.tensor_tensor(out=ACC[:, r, 4:8], in0=ACC[:, r - 1, 4:8], in1=MT_[:, r - 1, 4:8], op=ALU.add),
                     reads=[cft], writes=[cft])
            S.op("act", lambda E: E.activation(out=COEF[:, :, :], in_=ACC[:, :, :], func=AF.Exp, scale=-1.0), reads=[cft], writes=[cft])
            for r in range(8):
                for dr in range(2):
                    S.op("dve", lambda E, r=r, dr=dr: E.tensor_scalar(
                        out=COEF[:, r, dr * 4:(dr + 1) * 4], in0=COEF[:, r, dr * 4:(dr + 1) * 4], scalar1=MF[:, dr, r:r + 1],
                        scalar2=None, op0=ALU.mult), reads=[cft, S.tok("MF")], writes=[cft])
            for ch in range(8):
                S.op("dve", lambda E, ch=ch: E.tensor_scalar(out=Cst[:, ch, 0:130], in0=ST[:, 0, ch, 0:130], scalar1=COEF[:, 0, ch:ch + 1],
                                                             scalar2=None, op0=ALU.mult),
                     reads=[stt, cft, S.tok("Cst", ch), S.tok("sts")], writes=[S.tok("Cst", ch)])
                for r in range(1, 8):
                    S.op("dve", lambda E, ch=ch, r=r: E.scalar_tensor_tensor(
                        out=Cst[:, ch, 0:130], in0=ST[:, r, ch, 0:130], scalar=COEF[:, r, ch:ch + 1], in1=Cst[:, ch, 0:130],
                        op0=ALU.mult, op1=ALU.add), reads=[stt, cft, S.tok("Cst", ch)], writes=[S.tok("Cst", ch)])
                S.op("act", lambda E, ch=ch: E.copy(out=Cbf[:, ch, 0:130], in_=Cst[:, ch, 0:130]),
                     reads=[S.tok("Cst", ch)], writes=[S.tok("Cbf", ch)])
            it = 0
            for dr in range(2):
                for i in range(NT):
                    n = i if dr == 0 else NT - 1 - i
                    tsl = slice(n * 128, (n + 1) * 128)
                    for h in range(4):
                        ch = dr * 4 + h
                        slot = it % R
                        pb = pbank[slot]
                        W_, Wt = WT[slot], S.tok("WT", slot)
                        s_, st = sm[slot], S.tok("sm", slot)
                        S.op("pe", lambda E, pb=pb, h=h, tsl=tsl: E.matmul(pb[:, 0:128], qkb[:, 4 + h, tsl], qkb[:, h, tsl],
                                                                         start=True, stop=True), reads=[qkt], writes=[S.tok("pbs", slot)])
                        S.op("dve", lambda E, pb=pb, W_=W_, dr=dr, ch=ch, n=n: E.scalar_tensor_tensor(
                            out=W_[:, :], in0=pb[:, 0:128], scalar=RS[:, ch, n:n + 1], in1=tri[:, dr, :], op0=ALU.mult, op1=ALU.mult),
                            reads=[S.tok("pbs", slot), gt, ct], writes=[Wt])
                        S.op("pe", lambda E, pb=pb, W_=W_, n=n, h=h: E.matmul(pb[:, 128:258], W_[:, :], vaug[:, n, h, 0:130],
                                                                           start=True, stop=False), reads=[Wt, vt, ct], writes=[S.tok("pbo", slot)])
                        S.op("pe", lambda E, pb=pb, h=h, tsl=tsl, ch=ch: E.matmul(pb[:, 128:258], qkb[:, h, tsl], Cbf[:, ch, 0:130],
                                                                              start=False, stop=True),
                             reads=[qkt, S.tok("Cbf", ch)], writes=[S.tok("pbo", slot)])
                        S.op("act", lambda E, pb=pb, s_=s_, ch=ch, n=n: E.activation(
                            out=s_[:, 0:1], in_=pb[:, 256:257], func=AF.Abs, scale=EB[:, ch, n:n + 1]), reads=[S.tok("pbo", slot), gt], writes=[st])
                        S.op("dve", lambda E, s_=s_: E.tensor_scalar(out=s_[:, 1:2], in0=s_[:, 0:1], scalar1=1.0, scalar2=None, op0=ALU.max),
                             reads=[st], writes=[st])
                        S.op("dve", lambda E, s_=s_: E.reciprocal(out=s_[:, 2:3], in_=s_[:, 1:2]), reads=[st], writes=[st])
                        S.op("dve", lambda E, s_=s_, ch=ch, n=n: E.tensor_tensor(out=s_[:, 3:4], in0=s_[:, 2:3], in1=EB[:, ch, n:n + 1], op=ALU.mult),
                             reads=[st, gt], writes=[st])
                        if dr == 0:
                            S.op("act", lambda E, pb=pb, s_=s_, n=n, h=h: E.activation(
                                out=Hf[:, n, h, :], in_=pb[:, 128:256], func=AF.Copy, scale=s_[:, 3:4]),
                                reads=[S.tok("pbo", slot), st], writes=[S.tok("Hf", n, h)])
                        else:
                            h_, ht = hs[slot], S.tok("hs", slot)
                            hb_, hbt = hb[slot], S.tok("hb", slot)
                            S.op("dve", lambda E, pb=pb, s_=s_, n=n, h=h, h_=h_: E.scalar_tensor_tensor(
                                out=h_[:, :], in0=pb[:, 128:256], scalar=s_[:, 3:4], in1=Hf[:, n, h, :], op0=ALU.mult, op1=ALU.add),
                                reads=[S.tok("pbo", slot), st, S.tok("Hf", n, h)], writes=[ht])
                            S.op("act", lambda E, h_=h_, s_=s_: E.activation(out=sqj[:, :], in_=h_[:, :], func=AF.Square, accum_out=s_[:, 4:5]),
                                 reads=[ht, st], writes=[st, S.tok("sqj")])
                            S.op("act", lambda E, s_=s_: E.activation(out=s_[:, 5:6], in_=s_[:, 4:5], func=AF.Sqrt, scale=1.0 / 128,
                                                                    bias=epsb[:, 0:1]), reads=[st, ct], writes=[st])
                            S.op("dve", lambda E, s_=s_: E.reciprocal(out=s_[:, 6:7], in_=s_[:, 5:6]), reads=[st], writes=[st])
                            S.op("dve", lambda E, h_=h_, s_=s_, h=h: E.scalar_tensor_tensor(
                                out=h_[:, :], in0=h_[:, :], scalar=s_[:, 6:7], in1=hg[:, h * 128:(h + 1) * 128], op0=ALU.mult, op1=ALU.mult),
                                reads=[ht, st, S.tok("hg")], writes=[ht])
                            S.op("dve", lambda E, h_=h_, hb_=hb_, n=n, h=h: E.tensor_tensor(
                                out=hb_[:, :], in0=h_[:, :], in1=SG[:, n, h * 128:(h + 1) * 128], op=ALU.mult),
                                reads=[ht, sgt], writes=[hbt])
                            S.op("pe", lambda E, hb_=hb_, h=h: E.transpose(ptf[:, h, :], hb_[:, :], ident[:, :]),
                                 reads=[hbt, ct], writes=[S.tok("ptf", h)])
                            S.op("act", lambda E, h=h, tsl=tsl: E.copy(out=mstg[:, h, tsl], in_=ptf[:, h, :]),
                                 reads=[S.tok("ptf", h)], writes=[S.tok("mstg", h)])
                        state_steps(ch, n, slot)
                        S.op("act", lambda E, ch=ch: E.copy(out=Cbf[:, ch, 0:130], in_=Cst[:, ch, 0:130]),
                             reads=[S.tok("Cst", ch)], writes=[S.tok("Cbf", ch)])
                        it += 1
            S.dma("sp", MT_out.rearrange("(h p) t -> p h t", p=128), mstg[:, :, :], reads=[S.tok("mstg", h) for h in range(4)],
                  writes=[S.tok("fin")])
            S.end_stage()


def build_fused(n_layers=4, debug_outs=()):
    nc = bass.Bass("TRN2", target_bir_lowering=False)
    dt_ = nc.dram_tensor
    x_in = dt_("x", [D, TPC], F32, kind="ExternalInput").ap()
    NL = n_layers
    NE, NO = (NL + 1) // 2, max(1, NL // 2)
    norm_g = dt_("norm_g", [NL, 3, D], F32, kind="ExternalInput").ap()
    w1 = dt_("ffn_w1", [NL, 2, D, DFF], F32, kind="ExternalInput").ap()
    w3 = dt_("ffn_w3", [NL, 2, D, DFF], F32, kind="ExternalInput").ap()
    w2 = dt_("ffn_w2", [NL, 2, DFF, D], F32, kind="ExternalInput").ap()
    ab_w_in = dt_("ab_w_in", [NE, D, 3600], F32, kind="ExternalInput").ap()
    ab_conv_w = dt_("ab_conv_w", [NE, 5, 1024], F32, kind="ExternalInput").ap()
    ab_conv_b = dt_("ab_conv_b", [NE, 1024], F32, kind="ExternalInput").ap()
    ab_gate_b = dt_("ab_gate_b", [NE, 16], F32, kind="ExternalInput").ap()
    ab_hnorm_g = dt_("ab_hnorm_g", [NE, 512], F32, kind="ExternalInput").ap()
    ab_w_out = dt_("ab_w_out", [NE, D, D], F32, kind="ExternalInput").ap()
    c_w_in = dt_("c_w_in", [NO, D, 1536], F32, kind="ExternalInput").ap()
    c_sink = dt_("c_sink", [NO, 16], F32, kind="ExternalInput").ap()
    c_w_out = dt_("c_w_out", [NO, D, D], F32, kind="ExternalInput").ap()
    final_g = dt_("final_g", [D], F32, kind="ExternalInput").ap()
    sel_in = dt_("sel", [128, 6], F32, kind="ExternalInput").ap()
    kbt_b = dt_("kbt_b", [128, 69], F32, kind="ExternalInput").ap()
    kb_c = dt_("kb_c", [128, 18], F32, kind="ExternalInput").ap()
    mf_in = dt_("mf", [128, 8], F32, kind="ExternalInput").ap()
    mb_in = dt_("mb", [128, 8], F32, kind="ExternalInput").ap()
    y_out = dt_("y", [D, TPC], F32, kind="ExternalOutput").ap()
    X = dt_("X_s", [D, TPC], F32).ap()
    MIXT = dt_("MIXT_s", [D, TPC], BF16).ap()
    QKH = dt_("QKH_s", [1024, TPC + 4], F32).ap()
    VM = dt_("VM_s", [TPC, 512], BF16).ap()
    OM = dt_("OM_s", [TPC, 512], F32).ap()
    GM = dt_("GM_s", [TPC, 16], F32).ap()
    QB = dt_("QB_s", [512, TPC], BF16).ap()
    KBH = dt_("KBH_s", [512, TPC + 2 * B_HALO], BF16).ap()
    VBH = dt_("VBH_s", [TPC + 2 * B_HALO, 512], BF16).ap()
    QC = dt_("QC_s", [1024, TPC], BF16).ap()
    KCH = dt_("KCH_s", [256, TPC + 256], BF16).ap()
    VCH = dt_("VCH_s", [TPC + 256, 256], BF16).ap()
    STS = dt_("STS_s", [8 * 128, 132], F32)
    STA = dt_("STA_s", [64 * 128, 132], F32)
    items_ab = [("fm", KBH, 512, B_HALO, BF16), ("tm", VBH, 512, B_HALO, BF16), ("fm", QKH, 1024, 2, F32)]
    items_c = [("fm", KCH, 256, 128, BF16), ("tm", VCH, 256, 128, BF16)]
    bufs_ab = exch_alloc(nc, items_ab, "ab")
    bufs_c = exch_alloc(nc, items_c, "c")
    dbg = {}
    for name in debug_outs:
        src = {"X": X, "MIXT": MIXT, "KBH": KBH, "VBH": VBH, "QKH": QKH, "KCH": KCH, "VCH": VCH, "QB": QB, "QC": QC}[name]
        dbg[name] = (dt_("dbg_" + name, list(src.shape), src.dtype, kind="ExternalOutput").ap(), src)
    gstack = contextlib.ExitStack()
    with gstack:
        S = Sched(nc, gstack)
        with nc.Block() as block:
            S.block = block
            S.dma("sp", X, x_in, writes=[S.tok("X")])
            S.end_stage()
            import os
            skip = set(os.environ.get("FUSED_SKIP", "").split(","))
            for l in range(n_layers):
                j = l // 2
                if "ffn" not in skip:
                    stage_ffn(S, nc, X, norm_g[l, 0], w1[l, 0], w3[l, 0], w2[l, 0])
                if l % 2 == 0:
                    outs = {"qkT": QKH[:, 2:2 + TPC], "Vm": VM, "Om": OM, "Gm": GM, "qbT": QB,
                            "kbT": KBH[:, B_HALO:B_HALO + TPC], "Vb": VBH[B_HALO:B_HALO + TPC, :]}
                    if "proj" not in skip:
                        stage_proj(S, nc, "AB", X, norm_g[l, 1], ab_w_in[j], outs)
                    if "exch" not in skip:
                        stage_exch(S, nc, items_ab, sel_in, bufs_ab)
                    if "mlstm" not in skip:
                        stage_mlstm(S, nc, QKH, ab_conv_w[j], ab_conv_b[j], VM, OM, GM, ab_gate_b[j], ab_hnorm_g[j],
                                    mf_in, mb_in, STS, STA, MIXT[0:512, :])
                    if "attnb" not in skip:
                        stage_attn_b(S, nc, QB, KBH, VBH, kbt_b, MIXT[512:1024, :])
                    if "out" not in skip:
                        stage_out(S, nc, X, MIXT, ab_w_out[j])
                    if "ffn" in skip:
                        continue
                else:
                    outs = {"qT": QC, "kT": KCH[:, 128:128 + TPC], "V": VCH[128:128 + TPC, :]}
                    stage_proj(S, nc, "C", X, norm_g[l, 1], c_w_in[j], outs)
                    stage_exch(S, nc, items_c, sel_in, bufs_c)
                    stage_attn_c(S, nc, QC, KCH, VCH, kb_c, c_sink[j], MIXT)
                    stage_out(S, nc, X, MIXT, c_w_out[j])
                stage_ffn(S, nc, X, norm_g[l, 2], w1[l, 1], w3[l, 1], w2[l, 1])
            stage_final(S, nc, X, final_g, y_out)
            for name, (dst, src) in dbg.items():
                S.dma("sp", dst, src, writes=[S.tok("dbg", name)])
            S.end_stage()
    return nc


def fused_core_inputs(c):
    qd = c % 4
    sel = np.zeros(6, np.float32)
    if qd > 0:
        sel[qd - 1] = 1.0
    if qd < 3:
        sel[3 + qd] = 1.0
    tab = b_chunks()
    valid = _kvalid(c, B_HALO)
    kbt = np.zeros((128, len(tab)), np.float32)
    for (d, r, m), idx in tab.items():
        u = B_HALO + r + d * (128 * m - 64 + np.arange(128))
        kbt[:, idx] = valid[u]
    kbc = np.ascontiguousarray(_kvalid(c, 128).reshape(18, 128).T)
    mf = np.array([1.0 if (r // 4 == c // 4 and r < c) else 0.0 for r in range(8)], np.float32)
    mb = np.array([1.0 if (r // 4 == c // 4 and r > c) else 0.0 for r in range(8)], np.float32)
    return {"sel": np.ascontiguousarray(np.broadcast_to(sel, (128, 6))), "kbt_b": kbt, "kb_c": kbc,
            "mf": np.ascontiguousarray(np.broadcast_to(mf, (128, 8))), "mb": np.ascontiguousarray(np.broadcast_to(mb, (128, 8)))}


def kernel_fused(x, norm_g, ffn_w1, ffn_w3, ffn_w2, ab_w_in, ab_conv_w, ab_conv_b, ab_gate_b, ab_hnorm_g, ab_w_out,
                 c_w_in, c_sink, c_w_out, final_g, n_layers=4, debug_outs=()):
    nc = _get("fused", build_fused, n_layers, tuple(debug_outs))
    NL = n_layers
    NE, NO = (NL + 1) // 2, max(1, NL // 2)
    shared = {"norm_g": _f32c(norm_g[:NL]), "ffn_w1": _f32c(ffn_w1[:NL]), "ffn_w3": _f32c(ffn_w3[:NL]),
              "ffn_w2": _f32c(ffn_w2[:NL]), "ab_w_in": _f32c(ab_w_in[:NE]), "ab_conv_w": _f32c(ab_conv_w[:NE]),
              "ab_conv_b": _f32c(ab_conv_b[:NE]), "ab_gate_b": _f32c(ab_gate_b).reshape(2, 16)[:NE],
              "ab_hnorm_g": _f32c(ab_hnorm_g[:NE]), "ab_w_out": _f32c(ab_w_out[:NE]), "c_w_in": _f32c(c_w_in[:NO]),
              "c_sink": _f32c(c_sink[:NO]), "c_w_out": _f32c(c_w_out[:NO]), "final_g": _f32c(final_g)}
    x = np.asarray(x, dtype=np.float32)
    in_maps = []
    for c in range(NCORES):
        m = dict(shared)
        m["x"] = np.ascontiguousarray(x[c // 4, (c % 4) * TPC:(c % 4 + 1) * TPC].T)
        m.update(fused_core_inputs(c))
        in_maps.append(m)
    res = run_bass_kernel_spmd(nc, in_maps, core_ids=list(range(NCORES)))
    out = np.empty((BATCH, SEQ, D), np.float32)
    for c in range(NCORES):
        out[c // 4, (c % 4) * TPC:(c % 4 + 1) * TPC] = np.asarray(res.results[c]["y"]).T
    if debug_outs:
        return out, res.results
    return out


def kernel_fused_n(x, norm_g, ffn_w1, ffn_w3, ffn_w2, ab_w_in, ab_conv_w, ab_conv_b, ab_gate_b, ab_hnorm_g, ab_w_out,
                   c_w_in, c_sink, c_w_out, final_g, layers_per_launch=2):
    LP = layers_per_launch
    nc = _get("fused", build_fused, LP, ("X",))
    core_c = [fused_core_inputs(c) for c in range(NCORES)]
    xs = [np.ascontiguousarray(x[c // 4, (c % 4) * TPC:(c % 4 + 1) * TPC].T) for c in range(NCORES)]
    res = None
    for l0 in range(0, 4, LP):
        ls = slice(l0, l0 + LP)
        js = slice(l0 // 2, l0 // 2 + (LP + 1) // 2)
        jo = slice(l0 // 2, l0 // 2 + max(1, LP // 2))
        shared = {"norm_g": _f32c(norm_g[ls]), "ffn_w1": _f32c(ffn_w1[ls]), "ffn_w3": _f32c(ffn_w3[ls]),
                  "ffn_w2": _f32c(ffn_w2[ls]), "ab_w_in": _f32c(ab_w_in[js]), "ab_conv_w": _f32c(ab_conv_w[js]),
                  "ab_conv_b": _f32c(ab_conv_b[js]), "ab_gate_b": _f32c(ab_gate_b).reshape(2, 16)[js],
                  "ab_hnorm_g": _f32c(ab_hnorm_g[js]), "ab_w_out": _f32c(ab_w_out[js]), "c_w_in": _f32c(c_w_in[jo]),
                  "c_sink": _f32c(c_sink[jo]), "c_w_out": _f32c(c_w_out[jo]), "final_g": _f32c(final_g)}
        in_maps = []
        for c in range(NCORES):
            m = dict(shared)
            m["x"] = xs[c]
            m.update(core_c[c])
            in_maps.append(m)
        res = run_bass_kernel_spmd(nc, in_maps, core_ids=list(range(NCORES))).results
        xs = [np.ascontiguousarray(np.asarray(res[c]["dbg_X"])) for c in range(NCORES)]
    out = np.empty((BATCH, SEQ, D), np.float32)
    for c in range(NCORES):
        out[c // 4, (c % 4) * TPC:(c % 4 + 1) * TPC] = np.asarray(res[c]["y"]).T
    return out


def kernel(x, norm_g, ffn_w1, ffn_w3, ffn_w2, ab_w_in, ab_conv_w, ab_conv_b, ab_gate_b, ab_hnorm_g, ab_w_out,
           c_w_in, c_sink, c_w_out, final_g):
    return kernel_unfused(x, norm_g, ffn_w1, ffn_w3, ffn_w2, ab_w_in, ab_conv_w, ab_conv_b, ab_gate_b, ab_hnorm_g,
                          ab_w_out, c_w_in, c_sink, c_w_out, final_g)
```

```python
import contextlib
import numpy as np
import concourse.bass as bass
import concourse.mybir as mybir
from concourse.bass_utils import run_bass_kernel_spmd

F32 = mybir.dt.float32
BF16 = mybir.dt.bfloat16
AF = mybir.ActivationFunctionType
ALU = mybir.AluOpType
AX = mybir.AxisListType

NCORES = 8
D = 1024
SEQ = 8192
BATCH = 2
TPC = 2048
DFF = 2816
EPS = 1e-6


class Tok:
    __slots__ = ("w", "r")

    def __init__(self):
        self.w = None
        self.r = []


class Sched:
    ENG = ("pe", "dve", "act", "pool", "sp")

    def __init__(self, nc, stack, n_dma_sems=6):
        self.nc = nc
        self.eng = {"pe": nc.tensor, "dve": nc.vector, "act": nc.scalar, "pool": nc.gpsimd, "sp": nc.sync}
        self.ops = {e: [] for e in self.ENG}
        self.cnt = {e: 0 for e in self.ENG}
        self.sem = {e: stack.enter_context(nc.semaphore("s_" + e)) for e in self.ENG}
        self.semobj = dict(self.sem)
        self.waited = {}
        self.dma_sems = {}
        self.dma_cnt = {}
        self.dma_rr = {}
        for q in ("sp", "pool", "act"):
            self.dma_sems[q] = []
            for i in range(n_dma_sems):
                key = "d_%s%d" % (q, i)
                self.semobj[key] = stack.enter_context(nc.semaphore(key))
                self.dma_sems[q].append(key)
                self.dma_cnt[key] = 0
            self.dma_rr[q] = 0
        self.toks = {}
        self.semobj["cc"] = stack.enter_context(nc.semaphore("s_cc"))
        self.cc_cnt = 0
        self.block = None

    def cc(self, kind, ins, outs, groups, reads=(), writes=()):
        self._deps("pool", reads, writes)
        self.cc_cnt += 1
        so = self.semobj["cc"]
        self.ops["pool"].append(lambda E, so=so: E.collective_compute(
            kind, ALU.bypass, replica_groups=groups, ins=ins, outs=outs).then_inc(so, 1))
        self._mark(("cc", self.cc_cnt), reads, writes)
        self._need("pool", "cc", self.cc_cnt)

    def barrier(self):
        vals = {e: self.cnt[e] for e in self.ENG}
        vals.update(self.dma_cnt)
        vals["cc"] = self.cc_cnt
        for e in self.ENG:
            for k, v in vals.items():
                if v > 0 and not (e == "pe" and k == "pe"):
                    self._need(e, k, v)
        self.toks = {}

    def flush(self):
        block = self.block
        for e, deco in (("sp", block.sync), ("pe", block.tensor), ("dve", block.vector),
                        ("act", block.scalar), ("pool", block.gpsimd)):
            ops = self.ops[e]
            if not ops:
                continue

            def body(E, ops=ops):
                for o in ops:
                    o(E)
            deco(body)
            self.ops[e] = []

    def end_stage(self):
        self.barrier()
        self.flush()

    def tok(self, *key):
        t = self.toks.get(key)
        if t is None:
            t = self.toks[key] = Tok()
        return t

    def _need(self, eng, semkey, val):
        k = (eng, semkey)
        if self.waited.get(k, 0) >= val:
            return
        self.waited[k] = val
        so = self.semobj[semkey]
        self.ops[eng].append(lambda E, so=so, val=val: E.wait_ge(so, val))

    def _deps(self, eng, reads, writes):
        deps = {}
        def add(d):
            if d is None:
                return
            if deps.get(d[0], 0) < d[1]:
                deps[d[0]] = d[1]
        for t in reads:
            add(t.w)
        for t in writes:
            add(t.w)
            for r in t.r:
                add(r)
        for semkey, val in deps.items():
            if eng == "pe" and semkey == "pe":
                continue
            self._need(eng, semkey, val)

    def _mark(self, stamp, reads, writes):
        for t in reads:
            t.r.append(stamp)
        for t in writes:
            t.w = stamp
            t.r = []

    def op(self, eng, fn, reads=(), writes=()):
        self._deps(eng, reads, writes)
        self.cnt[eng] += 1
        so = self.sem[eng]
        self.ops[eng].append(lambda E, fn=fn, so=so: fn(E).then_inc(so, 1))
        self._mark((eng, self.cnt[eng]), reads, writes)

    def dma(self, q, out, in_, reads=(), writes=(), **kw):
        self._deps(q, reads, writes)
        pool = self.dma_sems[q]
        key = pool[self.dma_rr[q] % len(pool)]
        self.dma_rr[q] += 1
        prev = self.dma_cnt[key]
        if prev:
            self._need(q, key, prev)
        self.dma_cnt[key] = prev + 16
        so = self.semobj[key]
        self.ops[q].append(lambda E, so=so, out=out, in_=in_, kw=kw: E.dma_start(out=out, in_=in_, **kw).then_inc(so, 16))
        self._mark((key, prev + 16), reads, writes)

    def finish(self, final_toks):
        for t in final_toks:
            if t.w is not None:
                self._need("sp", t.w[0], t.w[1])
        nc = self.nc
        with nc.Block() as block:
            for e, deco in (("sp", block.sync), ("pe", block.tensor), ("dve", block.vector),
                            ("act", block.scalar), ("pool", block.gpsimd)):
                ops = self.ops[e]
                if not ops:
                    continue

                def body(E, ops=ops):
                    for o in ops:
                        o(E)
                deco(body)


class Ctx:
    uid = 0

    def __init__(self, nc):
        self.nc = nc
        self.stack = contextlib.ExitStack()
        self.n = 0

    def sb(self, shape, dt, name=None):
        Ctx.uid += 1
        return self.stack.enter_context(self.nc.sbuf_tensor((name or "t") + "_s%d" % Ctx.uid, list(shape), dt))

    def ps(self, shape, dt=F32, name=None):
        Ctx.uid += 1
        return self.stack.enter_context(self.nc.psum_tensor((name or "p") + "_p%d" % Ctx.uid, list(shape), dt))


def emit_rmsnorm(S, cx, xT, xT_tok, ntok0, ntok, gcol, xn, xn_tok, ones_bf, ps_ss, ps_tok, scratch):
    sq, rs = scratch["sq"], scratch["rs"]
    sl = slice(ntok0, ntok0 + ntok)
    for k in range(8):
        S.op("act", lambda E, k=k: E.activation(out=sq[:, k, 0:ntok], in_=xT[:, k, sl], func=AF.Square),
             reads=[xT_tok], writes=[scratch["sq_tok"]])
    for k in range(8):
        S.op("pe", lambda E, k=k: E.matmul(ps_ss[:, 0:ntok], ones_bf[:, :], sq[:, k, 0:ntok],
                                           start=(k == 0), stop=(k == 7)),
             reads=[scratch["sq_tok"]], writes=[ps_tok])
    S.op("act", lambda E: E.activation(out=rs[:, 0:ntok], in_=ps_ss[:, 0:ntok], func=AF.Sqrt,
                                       scale=1.0 / D, bias=scratch["eps"][:, 0:1]),
         reads=[ps_tok], writes=[scratch["rs_tok"]])
    S.op("dve", lambda E: E.reciprocal(out=rs[:, 0:ntok], in_=rs[:, 0:ntok]),
         reads=[scratch["rs_tok"]], writes=[scratch["rs_tok"]])
    for k in range(8):
        S.op("dve", lambda E, k=k: E.scalar_tensor_tensor(out=xn[:, k, 0:ntok], in0=xT[:, k, sl],
                                                          scalar=gcol[:, k:k + 1], in1=rs[:, 0:ntok],
                                                          op0=ALU.mult, op1=ALU.mult),
             reads=[xT_tok, scratch["rs_tok"]], writes=[xn_tok])


def build_ffn(final_norm=False):
    nc = bass.Bass("TRN2", target_bir_lowering=False)
    xin = nc.dram_tensor("xT", [D, TPC], F32, kind="ExternalInput").ap()
    g_in = nc.dram_tensor("g", [D], F32, kind="ExternalInput").ap()
    w1 = nc.dram_tensor("w1", [D, DFF], F32, kind="ExternalInput").ap()
    w3 = nc.dram_tensor("w3", [D, DFF], F32, kind="ExternalInput").ap()
    w2 = nc.dram_tensor("w2", [DFF, D], F32, kind="ExternalInput").ap()
    if final_norm:
        gf_in = nc.dram_tensor("gf", [D], F32, kind="ExternalInput").ap()
    xout = nc.dram_tensor("yT", [D, TPC], F32, kind="ExternalOutput").ap()
    NF = DFF // 128
    TP = 1024
    cx = Ctx(nc)
    with cx.stack:
        S = Sched(nc, cx.stack)
        xT = cx.sb([128, 8, TPC], F32, "xT_sb")
        xn = cx.sb([128, 8, TP], BF16, "xn")
        gb = cx.sb([128, NF, TP], BF16, "gb")
        gcol = cx.sb([128, 8], F32, "gcol")
        ones_bf = cx.sb([128, 128], BF16, "ones")
        epsb = cx.sb([128, 1], F32, "eps")
        sq = cx.sb([128, 8, 512], BF16, "sq")
        rs = cx.sb([128, 512], F32, "rs")
        w13 = [cx.sb([128, 2, 8, 128], BF16, "w13_%d" % i) for i in range(3)]
        w2b = [cx.sb([128, NF, 128], BF16, "w2b_%d" % i) for i in range(2)]
        sil = [cx.sb([128, 512], F32, "sil%d" % i) for i in range(2)]
        ps_h = [[cx.ps([128, 512]) for _ in range(2)] for _ in range(2)]
        ps_y = [cx.ps([128, 512]) for _ in range(2)]
        ps_ss = cx.ps([128, 512])
        scratch = {"sq": sq, "rs": rs, "eps": epsb, "sq_tok": S.tok("sq"), "rs_tok": S.tok("rs")}

        S.op("pool", lambda E: E.memset(ones_bf[:, :], 1.0), writes=[S.tok("ones")])
        S.op("pool", lambda E: E.memset(epsb[:, :], EPS), writes=[S.tok("eps")])
        S.dma("sp", gcol[:, :], g_in.rearrange("(k p) -> p k", p=128), writes=[S.tok("gcol")],
              allow_slow_non_contiguous=True)
        xin_v = xin.rearrange("(k p) t -> p k t", p=128)
        xout_v = xout.rearrange("(k p) t -> p k t", p=128)
        xtoks = [S.tok("x", i) for i in range(4)]
        for i in range(4):
            for k in range(0, 8, 4):
                S.dma("sp", xT[:, k:k + 4, i * 512:(i + 1) * 512], xin_v[:, k:k + 4, i * 512:(i + 1) * 512],
                      writes=[xtoks[i]])
        scratch_r = [S.tok("ones"), S.tok("eps"), S.tok("gcol")]
        w1v = w1.rearrange("(k p) f -> p k f", p=128)
        w3v = w3.rearrange("(k p) f -> p k f", p=128)
        w2v = w2.rearrange("(f p) d -> p f d", p=128)
        wq = ["pool", "act"]
        nload = 0
        for pas in range(TPC // TP):
            for gi in range(TP // 512):
                grp = pas * (TP // 512) + gi
                for t in scratch_r:
                    S._deps("act", [t], [])
                    S._deps("dve", [t], [])
                    S._deps("pe", [t], [])
                emit_rmsnorm(S, cx, xT, xtoks[grp], grp * 512, 512, gcol,
                             xn[:, :, gi * 512:(gi + 1) * 512], S.tok("xn", gi), ones_bf, ps_ss,
                             S.tok("ps_ss"), scratch)
            for f in range(NF):
                wb = w13[f % 3]
                wt = S.tok("w13", f % 3)
                S.dma("pool", wb[:, 0, :, :], w1v[:, :, f * 128:(f + 1) * 128], writes=[wt])
                S.dma("pool", wb[:, 1, :, :], w3v[:, :, f * 128:(f + 1) * 128], writes=[wt])
                for gi in range(TP // 512):
                    pb = ps_h[(f * 2 + gi) % 2]
                    pt = [S.tok("ps_h", (f * 2 + gi) % 2, j) for j in range(2)]
                    tsl = slice(gi * 512, (gi + 1) * 512)
                    for j in range(2):
                        for k in range(8):
                            S.op("pe", lambda E, j=j, k=k, pb=pb, wb=wb, tsl=tsl: E.matmul(
                                pb[j][:, :], wb[:, j, k, :], xn[:, k, tsl], start=(k == 0), stop=(k == 7)),
                                reads=[wt, S.tok("xn", gi)], writes=[pt[j]])
                    sb_ = sil[(f * 2 + gi) % 2]
                    st = S.tok("sil", (f * 2 + gi) % 2)
                    S.op("act", lambda E, pb=pb, sb_=sb_: E.activation(out=sb_[:, :], in_=pb[0][:, :], func=AF.Silu),
                         reads=[pt[0]], writes=[st])
                    S.op("dve", lambda E, pb=pb, sb_=sb_, f=f, tsl=tsl: E.tensor_tensor(
                        out=gb[:, f, tsl], in0=sb_[:, :], in1=pb[1][:, :], op=ALU.mult),
                        reads=[st, pt[1]], writes=[S.tok("gb", gi)])
            for d in range(8):
                wb = w2b[d % 2]
                wt = S.tok("w2b", d % 2)
                S.dma("pool", wb[:, 0:11, :], w2v[:, 0:11, d * 128:(d + 1) * 128], writes=[wt])
                S.dma("pool", wb[:, 11:22, :], w2v[:, 11:22, d * 128:(d + 1) * 128], writes=[wt])
                for gi in range(TP // 512):
                    grp = pas * (TP // 512) + gi
                    py = ps_y[(d * 2 + gi) % 2]
                    pyt = S.tok("ps_y", (d * 2 + gi) % 2)
                    tsl = slice(gi * 512, (gi + 1) * 512)
                    for f in range(NF):
                        S.op("pe", lambda E, f=f, py=py, wb=wb, tsl=tsl: E.matmul(
                            py[:, :], wb[:, f, :], gb[:, f, tsl], start=(f == 0), stop=(f == NF - 1)),
                            reads=[wt, S.tok("gb", gi)], writes=[pyt])
                    xs = xT[:, d, grp * 512:(grp + 1) * 512]
                    S.op("dve", lambda E, py=py, xs=xs: E.scalar_tensor_tensor(
                        out=xs, in0=py[:, :], scalar=0.5, in1=xs, op0=ALU.mult, op1=ALU.add),
                        reads=[pyt, xtoks[grp]], writes=[xtoks[grp]])
        outt = S.tok("out")
        for i in range(4):
            for k in range(0, 8, 4):
                S.dma("sp", xout_v[:, k:k + 4, i * 512:(i + 1) * 512], xT[:, k:k + 4, i * 512:(i + 1) * 512],
                      reads=[xtoks[i]], writes=[outt, S.tok("out", i, k)])
        S.finish([S.tok("out", i, k) for i in range(4) for k in (0, 4)])
    return nc


_CACHE = {}


def _get(name, fn, *a):
    key = (name,) + a
    if key not in _CACHE:
        _CACHE[key] = fn(*a)
    return _CACHE[key]


def run_ffn(xT_shards, g, w1, w3, w2):
    nc = _get("ffn", build_ffn)
    in_maps = [{"xT": xT_shards[c], "g": g, "w1": w1, "w3": w3, "w2": w2} for c in range(NCORES)]
    res = run_bass_kernel_spmd(nc, in_maps, core_ids=list(range(NCORES)))
    return [r["yT"] for r in res.results]


PROJ_SPECS = {
    "C": (1536, [("qT", 0, 1024, "fm64", BF16), ("kT", 1024, 256, "fm64", BF16), ("V", 1280, 256, "tm", BF16)]),
    "AB": (3600, [("qkT", 0, 1024, "fm128", F32), ("Vm", 1024, 512, "tm", BF16), ("Om", 1536, 512, "tm", F32),
                  ("Gm", 2048, 16, "tm", F32), ("qbT", 2064, 512, "fm64", BF16), ("kbT", 2576, 512, "fm64", BF16),
                  ("Vb", 3088, 512, "tm", BF16)]),
}


def build_proj(kind):
    NC_, specs = PROJ_SPECS[kind]
    nc = bass.Bass("TRN2", target_bir_lowering=False)
    xin = nc.dram_tensor("xT", [D, TPC], F32, kind="ExternalInput").ap()
    g_in = nc.dram_tensor("g", [D], F32, kind="ExternalInput").ap()
    w = nc.dram_tensor("w", [D, NC_], F32, kind="ExternalInput").ap()
    outs = {}
    for name, c0, ncol, lay, dt in specs:
        shp = [TPC, ncol] if lay == "tm" else [ncol, TPC]
        outs[name] = nc.dram_tensor(name, shp, dt, kind="ExternalOutput").ap()
    cx = Ctx(nc)
    with cx.stack:
        S = Sched(nc, cx.stack)
        xT = cx.sb([128, 8, TPC], F32, "xT_sb")
        xn = cx.sb([128, 8, TPC], BF16, "xn")
        wsb = cx.sb([128, 8, NC_], BF16, "wsb")
        gcol = cx.sb([128, 8], F32, "gcol")
        ones_bf = cx.sb([128, 128], BF16, "ones")
        epsb = cx.sb([128, 1], F32, "eps")
        sq = cx.sb([128, 8, 512], BF16, "sq")
        rs = cx.sb([128, 512], F32, "rs")
        stg_fm = [cx.sb([128, TPC], F32, "stgfm%d" % i) for i in range(2)]
        stg_tm = [cx.sb([128, 512], F32, "stgtm%d" % i) for i in range(3)]
        ps = [cx.ps([128, 512]) for _ in range(6)]
        ps_ss = cx.ps([128, 512])
        scratch = {"sq": sq, "rs": rs, "eps": epsb, "sq_tok": S.tok("sq"), "rs_tok": S.tok("rs")}
        S.op("pool", lambda E: E.memset(ones_bf[:, :], 1.0), writes=[S.tok("ones")])
        S.op("pool", lambda E: E.memset(epsb[:, :], EPS), writes=[S.tok("eps")])
        S.dma("sp", gcol[:, :], g_in.rearrange("(k p) -> p k", p=128), writes=[S.tok("gcol")],
              allow_slow_non_contiguous=True)
        xin_v = xin.rearrange("(k p) t -> p k t", p=128)
        xtoks = [S.tok("x", i) for i in range(4)]
        for i in range(4):
            for k in range(0, 8, 4):
                S.dma("sp", xT[:, k:k + 4, i * 512:(i + 1) * 512], xin_v[:, k:k + 4, i * 512:(i + 1) * 512],
                      writes=[xtoks[i]])
        wv = w.rearrange("(k p) f -> p k f", p=128)
        wtok = S.tok("w")
        for k in range(8):
            S.dma("pool", wsb[:, k, :], wv[:, k, :], writes=[wtok])
        for t in (S.tok("ones"), S.tok("eps"), S.tok("gcol")):
            for e in ("act", "dve", "pe"):
                S._deps(e, [t], [])
        for gi in range(4):
            emit_rmsnorm(S, cx, xT, xtoks[gi], gi * 512, 512, gcol, xn[:, :, gi * 512:(gi + 1) * 512],
                         S.tok("xn", gi), ones_bf, ps_ss, S.tok("ps_ss"), scratch)
        pi = 0
        ei = 0
        si = 0
        finals = []
        for name, c0, ncol, lay, dt in specs:
            if lay == "tm":
                continue
            M = 64 if lay == "fm64" else 128
            for ch in range(ncol // M):
                stg = stg_fm[si % 2]
                stt = S.tok("stgfm", si % 2)
                si += 1
                for gi in range(4):
                    p = ps[pi % 6]
                    pt = S.tok("ps", pi % 6)
                    pi += 1
                    for k in range(8):
                        S.op("pe", lambda E, p=p, k=k, cc=c0 + ch * M, M=M, gi=gi: E.matmul(
                            p[0:M, :], wsb[:, k, cc:cc + M], xn[:, k, gi * 512:(gi + 1) * 512],
                            start=(k == 0), stop=(k == 7)), reads=[wtok, S.tok("xn", gi)], writes=[pt])
                    dst = _stg_view(stg, dt, M, TPC)[:, gi * 512:(gi + 1) * 512]
                    if ei % 2 == 0:
                        S.op("act", lambda E, p=p, dst=dst, M=M: E.copy(out=dst, in_=p[0:M, :]), reads=[pt], writes=[stt])
                    else:
                        S.op("dve", lambda E, p=p, dst=dst, M=M: E.tensor_copy(out=dst, in_=p[0:M, :]), reads=[pt], writes=[stt])
                    ei += 1
                ft = S.tok("fin", name, ch)
                finals.append(ft)
                S.dma("sp", outs[name][ch * M:(ch + 1) * M, :], _stg_view(stg, dt, M, TPC), reads=[stt], writes=[ft])
        ti = 0
        for tt in range(16):
            for name, c0, ncol, lay, dt in specs:
                if lay != "tm":
                    continue
                p = ps[pi % 6]
                pt = S.tok("ps", pi % 6)
                pi += 1
                for k in range(8):
                    S.op("pe", lambda E, p=p, k=k, c0=c0, ncol=ncol, tt=tt: E.matmul(
                        p[:, 0:ncol], xn[:, k, tt * 128:(tt + 1) * 128], wsb[:, k, c0:c0 + ncol],
                        start=(k == 0), stop=(k == 7)), reads=[wtok, S.tok("xn", tt // 4)], writes=[pt])
                stg = stg_tm[ti % 3]
                stt = S.tok("stgtm", ti % 3)
                ti += 1
                dst = _stg_view(stg, dt, 128, ncol)
                if ei % 2 == 0:
                    S.op("act", lambda E, p=p, dst=dst, ncol=ncol: E.copy(out=dst, in_=p[:, 0:ncol]), reads=[pt], writes=[stt])
                else:
                    S.op("dve", lambda E, p=p, dst=dst, ncol=ncol: E.tensor_copy(out=dst, in_=p[:, 0:ncol]), reads=[pt], writes=[stt])
                ei += 1
                ft = S.tok("fin", name, "t", tt)
                finals.append(ft)
                S.dma("sp", outs[name][tt * 128:(tt + 1) * 128, :], dst, reads=[stt], writes=[ft],
                      allow_slow_non_contiguous=(ncol < 64))
        S.finish(finals)
    return nc


def _stg_view(stg, dt, M, n):
    if dt == F32:
        return stg[0:M, 0:n]
    return stg[0:M, :].bitcast(BF16)[:, 0:n]


def run_proj(kind, xT_shards, g, w):
    nc = _get("proj", build_proj, kind)
    in_maps = [{"xT": xT_shards[c], "g": g, "w": w} for c in range(NCORES)]
    res = run_bass_kernel_spmd(nc, in_maps, core_ids=list(range(NCORES)))
    return res.results


def emit_absrel(S, cx, nchunks, halo, radius, name):
    A = cx.sb([128, nchunks, 128], F32, name)
    Ai = cx.sb([128, nchunks, 128], mybir.dt.int32, name + "_i")
    B = cx.sb([128, nchunks, 128], F32, name + "_b")
    t = S.tok(name)
    for c in range(nchunks):
        S.op("pool", lambda E, c=c: E.iota(Ai[:, c, :], [[-1, 128]], base=c * 128 - halo, channel_multiplier=1),
             writes=[t])
    S.op("dve", lambda E: E.tensor_copy(out=A[:, :, :], in_=Ai[:, :, :]), reads=[t], writes=[t])
    S.op("dve", lambda E: E.tensor_scalar(out=B[:, :, :], in0=A[:, :, :], scalar1=-1.0, scalar2=None, op0=ALU.mult),
         reads=[t], writes=[t])
    S.op("dve", lambda E: E.tensor_tensor(out=A[:, :, :], in0=A[:, :, :], in1=B[:, :, :], op=ALU.max),
         reads=[t], writes=[t])
    S.op("dve", lambda E: E.tensor_scalar(out=B[:, :, :], in0=A[:, :, :], scalar1=float(radius) + 0.5, scalar2=1e9,
                                          op0=ALU.is_gt, op1=ALU.mult), reads=[t], writes=[t])
    S.op("dve", lambda E: E.tensor_tensor(out=A[:, :, :], in0=A[:, :, :], in1=B[:, :, :], op=ALU.add),
         reads=[t], writes=[t])
    return A, t


def alibi_slopes(n):
    return [float(2.0 ** (-8.0 * (i + 1) / n)) for i in range(n)]


def build_attn_c():
    nc = bass.Bass("TRN2", target_bir_lowering=False)
    KH = TPC + 256
    q_in = nc.dram_tensor("qT", [1024, TPC], BF16, kind="ExternalInput").ap()
    k_in = nc.dram_tensor("kT", [256, KH], BF16, kind="ExternalInput").ap()
    v_in = nc.dram_tensor("V", [KH, 256], BF16, kind="ExternalInput").ap()
    kb_in = nc.dram_tensor("kb", [128, 18], F32, kind="ExternalInput").ap()
    sink_in = nc.dram_tensor("sink", [16], F32, kind="ExternalInput").ap()
    o_out = nc.dram_tensor("oT", [1024, TPC], BF16, kind="ExternalOutput").ap()
    cx = Ctx(nc)
    slopes = alibi_slopes(16)
    with cx.stack:
        S = Sched(nc, cx.stack)
        qT = cx.sb([64, 16, TPC], BF16, "qT_sb")
        kT = cx.sb([64, 4, KH], BF16, "kT_sb")
        V = cx.sb([128, 18, 256], BF16, "V_sb")
        kb = cx.sb([128, 18], F32, "kb_sb")
        esink = cx.sb([64, 16], F32, "esink")
        ones_bf = cx.sb([128, 64], BF16, "ones")
        bias = cx.sb([128, 3, 16, 128], F32, "bias")
        tmp = [cx.sb([128, 512], F32, "tmp%d" % i) for i in range(3)]
        P = [cx.sb([128, 512], BF16, "P%d" % i) for i in range(3)]
        rden = [cx.sb([64, 512], F32, "rden%d" % i) for i in range(2)]
        ostg = [cx.sb([64, 4, TPC], BF16, "ostg%d" % i) for i in range(2)]
        ps_s = [cx.ps([128, 512]) for _ in range(4)]
        ps_n = [cx.ps([64, 512]) for _ in range(2)]
        ps_d = [cx.ps([64, 512]) for _ in range(2)]
        S.dma("sp", qT[:, :, :], q_in.rearrange("(h e) t -> e h t", e=64), writes=[S.tok("q")])
        S.dma("sp", kT[:, :, :], k_in.rearrange("(h e) t -> e h t", e=64), writes=[S.tok("k")])
        S.dma("sp", V[:, :, :], v_in.rearrange("(c p) f -> p c f", p=128), writes=[S.tok("v")])
        S.dma("sp", kb[:, :], kb_in, writes=[S.tok("kb")])
        S.dma("sp", esink[:, :], sink_in.partition_broadcast(64), writes=[S.tok("esink")])
        S.op("act", lambda E: E.activation(out=esink[:, :], in_=esink[:, :], func=AF.Exp),
             reads=[S.tok("esink")], writes=[S.tok("esink")])
        S.op("pool", lambda E: E.memset(ones_bf[:, :], 1.0), writes=[S.tok("ones")])
        A, At = emit_absrel(S, cx, 3, 128, 128, "absrel")
        bt = S.tok("bias")
        for h in range(16):
            S.op("dve", lambda E, h=h: E.tensor_scalar(out=bias[:, :, h, :], in0=A[:, :, :], scalar1=-slopes[h],
                                                       scalar2=None, op0=ALU.mult), reads=[At], writes=[bt])
        scale = 64 ** -0.5
        finals = []
        P6 = P + [cx.sb([128, 512], BF16, "P%d" % i) for i in range(3, 6)]
        iters = [(kvh, b) for kvh in range(4) for b in range(16)]

        def emitA(it):
            kvh, b = iters[it]
            for c in range(3):
                j = it * 3 + c
                pss = ps_s[j % 4]
                pst = S.tok("pss", j % 4)
                S.op("pe", lambda E, pss=pss, kvh=kvh, b=b, c=c: E.matmul(
                    pss[:, :], kT[:, kvh, (b + c) * 128:(b + c + 1) * 128],
                    qT[:, kvh * 4:(kvh + 1) * 4, b * 128:(b + 1) * 128], start=True, stop=True),
                    reads=[S.tok("q"), S.tok("k")], writes=[pst])
                tm_, tmt = tmp[j % 3], S.tok("tmp", j % 3)
                S.op("dve", lambda E, pss=pss, tm_=tm_, kvh=kvh, c=c: E.scalar_tensor_tensor(
                    out=tm_[:, :], in0=pss[:, :], scalar=scale, in1=bias[:, c, kvh * 4:(kvh + 1) * 4, :],
                    op0=ALU.mult, op1=ALU.add), reads=[pst, bt], writes=[tmt])
                P_, Pt = P6[j % 6], S.tok("P", j % 6)
                S.op("act", lambda E, tm_=tm_, P_=P_, b=b, c=c: E.activation(
                    out=P_[:, :], in_=tm_[:, :], func=AF.Exp, bias=kb[:, b + c:b + c + 1]),
                    reads=[tmt, S.tok("kb")], writes=[Pt])

        def emitB(it):
            kvh, b = iters[it]
            og = ostg[kvh % 2]
            ogt = S.tok("ostg", kvh % 2)
            pn, pd = ps_n[it % 2], ps_d[it % 2]
            pnt, pdt = S.tok("psn", it % 2), S.tok("psd", it % 2)
            for c in range(3):
                j = it * 3 + c
                P_, Pt = P6[j % 6], S.tok("P", j % 6)
                S.op("pe", lambda E, pn=pn, P_=P_, kvh=kvh, b=b, c=c: E.matmul(
                    pn[:, :], V[:, b + c, kvh * 64:(kvh + 1) * 64], P_[:, :], start=(c == 0), stop=(c == 2)),
                    reads=[Pt, S.tok("v")], writes=[pnt])
            for c in range(3):
                j = it * 3 + c
                P_, Pt = P6[j % 6], S.tok("P", j % 6)
                S.op("pe", lambda E, pd=pd, P_=P_, c=c: E.matmul(
                    pd[:, :], ones_bf[:, :], P_[:, :], start=(c == 0), stop=(c == 2)),
                    reads=[Pt, S.tok("ones")], writes=[pdt])
            rd, rdt = rden[it % 2], S.tok("rden", it % 2)
            for g in range(4):
                h = kvh * 4 + g
                S.op("dve", lambda E, rd=rd, pd=pd, g=g, h=h: E.tensor_scalar(
                    out=rd[:, g * 128:(g + 1) * 128], in0=pd[:, g * 128:(g + 1) * 128],
                    scalar1=esink[:, h:h + 1], scalar2=None, op0=ALU.add),
                    reads=[pdt, S.tok("esink")], writes=[rdt])
            S.op("dve", lambda E, rd=rd: E.reciprocal(out=rd[:, :], in_=rd[:, :]), reads=[rdt], writes=[rdt])
            S.op("dve", lambda E, rd=rd, pn=pn, og=og, b=b: E.tensor_tensor(
                out=og[:, :, b * 128:(b + 1) * 128], in0=pn[:, :].rearrange("p (g q) -> p g q", g=4),
                in1=rd[:, :].rearrange("p (g q) -> p g q", g=4), op=ALU.mult),
                reads=[pnt, rdt], writes=[ogt])
            if b == 15:
                ft = S.tok("fin", kvh)
                finals.append(ft)
                S.dma("sp", o_out[kvh * 256:(kvh + 1) * 256, :].rearrange("(g e) t -> e g t", e=64), og[:, :, :],
                      reads=[ogt], writes=[ft])

        LOOK = 1
        for idx in range(len(iters) + LOOK):
            if idx < len(iters):
                emitA(idx)
            if idx - LOOK >= 0:
                emitB(idx - LOOK)
        S.finish(finals)
    return nc


def run_attn_c(ins):
    nc = _get("attn_c", build_attn_c)
    res = run_bass_kernel_spmd(nc, ins, core_ids=list(range(NCORES)))
    return [r["oT"] for r in res.results]


def _halo(arr, c, halo):
    b, qd = c // 4, c % 4
    t0 = qd * TPC
    out = np.zeros((TPC + 2 * halo, arr.shape[2]), arr.dtype)
    lo, hi = max(0, t0 - halo), min(SEQ, t0 + TPC + halo)
    out[lo - (t0 - halo):hi - (t0 - halo)] = arr[b, lo:hi]
    return out


def _kvalid(c, halo):
    qd = c % 4
    t0 = qd * TPC
    pos = np.arange(t0 - halo, t0 + TPC + halo)
    return np.where((pos >= 0) & (pos < SEQ), 0.0, -30000.0).astype(np.float32)


def prep_attn_c(q, k, v, sink):
    ins = []
    for c in range(NCORES):
        b, qd = c // 4, c % 4
        kh = _halo(k, c, 128)
        vh = _halo(v, c, 128)
        ins.append({"qT": np.ascontiguousarray(q[b, qd * TPC:(qd + 1) * TPC].T),
                    "kT": np.ascontiguousarray(kh.T), "V": vh,
                    "kb": np.ascontiguousarray(_kvalid(c, 128).reshape(18, 128).T), "sink": sink})
    return ins


def build_out():
    nc = bass.Bass("TRN2", target_bir_lowering=False)
    xin = nc.dram_tensor("xT", [D, TPC], F32, kind="ExternalInput").ap()
    m_in = nc.dram_tensor("mT", [D, TPC], BF16, kind="ExternalInput").ap()
    w = nc.dram_tensor("w", [D, D], F32, kind="ExternalInput").ap()
    xout = nc.dram_tensor("yT", [D, TPC], F32, kind="ExternalOutput").ap()
    cx = Ctx(nc)
    with cx.stack:
        S = Sched(nc, cx.stack)
        xT = cx.sb([128, 8, TPC], F32, "xT_sb")
        mT = cx.sb([128, 8, TPC], BF16, "mT_sb")
        wsb = cx.sb([128, 8, D], BF16, "wsb")
        ps = [cx.ps([128, 512]) for _ in range(4)]
        xin_v = xin.rearrange("(k p) t -> p k t", p=128)
        xout_v = xout.rearrange("(k p) t -> p k t", p=128)
        m_v = m_in.rearrange("(k p) t -> p k t", p=128)
        wv = w.rearrange("(k p) f -> p k f", p=128)
        for k in range(8):
            S.dma("pool", wsb[:, k, :], wv[:, k, :], writes=[S.tok("w")])
        for k in range(0, 8, 4):
            S.dma("sp", mT[:, k:k + 4, :], m_v[:, k:k + 4, :], writes=[S.tok("m")])
        xt = [S.tok("x", d) for d in range(8)]
        for d in range(8):
            S.dma("sp", xT[:, d, :], xin_v[:, d, :], writes=[xt[d]])
        finals = []
        i = 0
        for d in range(8):
            for gi in range(4):
                p, pt = ps[i % 4], S.tok("ps", i % 4)
                i += 1
                for k in range(8):
                    S.op("pe", lambda E, p=p, k=k, d=d, gi=gi: E.matmul(
                        p[:, :], wsb[:, k, d * 128:(d + 1) * 128], mT[:, k, gi * 512:(gi + 1) * 512],
                        start=(k == 0), stop=(k == 7)), reads=[S.tok("w"), S.tok("m")], writes=[pt])
                xs = xT[:, d, gi * 512:(gi + 1) * 512]
                S.op("dve", lambda E, p=p, xs=xs: E.tensor_tensor(out=xs, in0=p[:, :], in1=xs, op=ALU.add),
                     reads=[pt, xt[d]], writes=[xt[d]])
            ft = S.tok("fin", d)
            finals.append(ft)
            S.dma("sp", xout_v[:, d, :], xT[:, d, :], reads=[xt[d]], writes=[ft])
        S.finish(finals)
    return nc


def run_out(xT_shards, mT_shards, w):
    nc = _get("out", build_out)
    in_maps = [{"xT": xT_shards[c], "mT": mT_shards[c], "w": w} for c in range(NCORES)]
    res = run_bass_kernel_spmd(nc, in_maps, core_ids=list(range(NCORES)))
    return [r["yT"] for r in res.results]


B_DILS = (1, 4, 16)
B_HALO = 1024


def b_chunks():
    tab = {}
    for d in B_DILS:
        for r in range(d):
            for m in range(16 // d + 1):
                tab[(d, r, m)] = len(tab)
    return tab


def build_attn_b():
    nc = bass.Bass("TRN2", target_bir_lowering=False)
    KH = TPC + 2 * B_HALO
    q_in = nc.dram_tensor("qT", [512, TPC], BF16, kind="ExternalInput").ap()
    k_in = nc.dram_tensor("kT", [512, KH], BF16, kind="ExternalInput").ap()
    v_in = nc.dram_tensor("V", [KH, 512], BF16, kind="ExternalInput").ap()
    kb_in = nc.dram_tensor("kbt", [128, 69], F32, kind="ExternalInput").ap()
    o_out = nc.dram_tensor("oT", [512, TPC], BF16, kind="ExternalOutput").ap()
    cx = Ctx(nc)
    slopes = alibi_slopes(8)
    tab = b_chunks()
    NCH = len(tab)
    with cx.stack:
        S = Sched(nc, cx.stack)
        qT = cx.sb([128, 4, TPC], BF16, "qT_sb")
        kT = cx.sb([128, 4, KH], BF16, "kT_sb")
        V = cx.sb([128, NCH, 512], BF16, "V_sb")
        kb = cx.sb([128, NCH], F32, "kb_sb")
        ones_bf = cx.sb([128, 64], BF16, "ones")
        bias = cx.sb([128, 8, 3, 2, 128], F32, "bias")
        tmp = [cx.sb([128, 2, 128], F32, "tmp%d" % i) for i in range(3)]
        P = [cx.sb([128, 2, 128], BF16, "P%d" % i) for i in range(3)]
        accn = [cx.sb([64, TPC], F32, "accn%d" % i) for i in range(2)]
        accd = [cx.sb([64, TPC], F32, "accd%d" % i) for i in range(2)]
        ostg = [cx.sb([64, TPC], BF16, "ostg%d" % i) for i in range(2)]
        ps_s = [cx.ps([128, 2, 128]) for _ in range(3)]
        ps_n = [cx.ps([64, 128]) for _ in range(2)]
        ps_d = [cx.ps([64, 128]) for _ in range(2)]
        S.dma("sp", qT[:, :, :], q_in.rearrange("(k p) t -> p k t", p=128), writes=[S.tok("q")])
        for k in range(4):
            S.dma("sp", kT[:, k, :], k_in[k * 128:(k + 1) * 128, :], writes=[S.tok("k")])
        S.dma("sp", kb[:, :], kb_in, writes=[S.tok("kb")])
        for d in B_DILS:
            for r in range(d):
                M = 16 // d + 1
                c0 = tab[(d, r, 0)]
                start = B_HALO + r - 64 * d
                src = bass.AP(v_in.tensor, start * 512, [[d * 512, 128], [d * 128 * 512, M], [1, 512]])
                S.dma("sp" if (r % 2 == 0) else "act", V[:, c0:c0 + M, :], src, writes=[S.tok("v")])
        S.op("pool", lambda E: E.memset(ones_bf[:, :], 1.0), writes=[S.tok("ones")])
        A, At = emit_absrel(S, cx, 2, 64, 64, "absrel")
        bt = S.tok("bias")
        for h in range(8):
            for di, d in enumerate(B_DILS):
                S.op("dve", lambda E, h=h, di=di, d=d: E.tensor_scalar(
                    out=bias[:, h, di, :, :], in0=A[:, :, :], scalar1=-slopes[h] * d, scalar2=None, op0=ALU.mult),
                    reads=[At], writes=[bt])
        scale = 64 ** -0.5
        finals = []
        blocks = []
        for h in range(8):
            for di, d in enumerate(B_DILS):
                for r in range(d):
                    for blk in range(16 // d):
                        blocks.append((h, di, d, r, blk))
        LOOK = 2
        NP_ = 4
        P4 = P + [cx.sb([128, 2, 128], BF16, "P3")]

        def emitA(it):
            h, di, d, r, blk = blocks[it]
            pb, hp = (h % 2) * 64, h // 2
            q0 = r + d * 128 * blk
            qsl = slice(q0, q0 + d * 127 + 1, d)
            pss, pst = ps_s[it % 3], S.tok("pss", it % 3)
            for c in range(2):
                u0 = B_HALO + r + d * (128 * (blk + c) - 64)
                S.op("pe", lambda E, pss=pss, c=c, u0=u0, d=d, qsl=qsl, pb=pb, hp=hp: E.matmul(
                    pss[:, c, :], kT[pb:pb + 64, hp, u0:u0 + d * 127 + 1:d], qT[pb:pb + 64, hp, qsl],
                    start=True, stop=True), reads=[S.tok("q"), S.tok("k")], writes=[pst])
            tm_, tmt = tmp[it % 3], S.tok("tmp", it % 3)
            S.op("dve", lambda E, pss=pss, tm_=tm_, h=h, di=di: E.scalar_tensor_tensor(
                out=tm_[:, :, :], in0=pss[:, :, :], scalar=scale, in1=bias[:, h, di, :, :],
                op0=ALU.mult, op1=ALU.add), reads=[pst, bt], writes=[tmt])
            P_, Pt = P4[it % NP_], S.tok("P", it % NP_)
            for c in range(2):
                ci = tab[(d, r, blk + c)]
                S.op("act", lambda E, tm_=tm_, P_=P_, c=c, ci=ci: E.activation(
                    out=P_[:, c, :], in_=tm_[:, c, :], func=AF.Exp, bias=kb[:, ci:ci + 1]),
                    reads=[tmt, S.tok("kb")], writes=[Pt])

        def emitB(it):
            h, di, d, r, blk = blocks[it]
            q0 = r + d * 128 * blk
            qsl = slice(q0, q0 + d * 127 + 1, d)
            an, ad = accn[h % 2], accd[h % 2]
            ant, adt = S.tok("accn", h % 2), S.tok("accd", h % 2)
            P_, Pt = P4[it % NP_], S.tok("P", it % NP_)
            pn, pd = ps_n[it % 2], ps_d[it % 2]
            pnt, pdt = S.tok("psn", it % 2), S.tok("psd", it % 2)
            for c in range(2):
                ci = tab[(d, r, blk + c)]
                S.op("pe", lambda E, pn=pn, P_=P_, c=c, ci=ci, h=h: E.matmul(
                    pn[:, :], V[:, ci, h * 64:(h + 1) * 64], P_[:, c, :], start=(c == 0), stop=(c == 1)),
                    reads=[Pt, S.tok("v")], writes=[pnt])
            for c in range(2):
                S.op("pe", lambda E, pd=pd, P_=P_, c=c: E.matmul(
                    pd[:, :], ones_bf[:, :], P_[:, c, :], start=(c == 0), stop=(c == 1)),
                    reads=[Pt, S.tok("ones")], writes=[pdt])
            if di == 0:
                S.op("dve", lambda E, an=an, pn=pn, qsl=qsl: E.tensor_copy(out=an[:, qsl], in_=pn[:, :]),
                     reads=[pnt], writes=[ant])
                S.op("act", lambda E, ad=ad, pd=pd, qsl=qsl: E.copy(out=ad[:, qsl], in_=pd[:, :]),
                     reads=[pdt], writes=[adt])
            else:
                S.op("dve", lambda E, an=an, pn=pn, qsl=qsl: E.tensor_tensor(
                    out=an[:, qsl], in0=pn[:, :], in1=an[:, qsl], op=ALU.add), reads=[pnt, ant], writes=[ant])
                S.op("dve", lambda E, ad=ad, pd=pd, qsl=qsl: E.tensor_tensor(
                    out=ad[:, qsl], in0=pd[:, :], in1=ad[:, qsl], op=ALU.add), reads=[pdt, adt], writes=[adt])
            last = (it + 1 == len(blocks)) or blocks[it + 1][0] != h
            if last:
                og, ogt = ostg[h % 2], S.tok("ostg", h % 2)
                S.op("dve", lambda E, ad=ad: E.reciprocal(out=ad[:, :], in_=ad[:, :]), reads=[adt], writes=[adt])
                S.op("dve", lambda E, an=an, ad=ad, og=og: E.tensor_tensor(out=og[:, :], in0=an[:, :], in1=ad[:, :], op=ALU.mult),
                     reads=[ant, adt], writes=[ogt])
                ft = S.tok("fin", h)
                finals.append(ft)
                S.dma("sp", o_out[h * 64:(h + 1) * 64, :], og[:, :], reads=[ogt], writes=[ft])

        for idx in range(len(blocks) + LOOK):
            if idx < len(blocks):
                emitA(idx)
            if idx - LOOK >= 0:
                emitB(idx - LOOK)
        S.finish(finals)
    return nc


def prep_attn_b(q, k, v):
    tab = b_chunks()
    ins = []
    for c in range(NCORES):
        b, qd = c // 4, c % 4
        kh = _halo(k, c, B_HALO)
        vh = _halo(v, c, B_HALO)
        valid = _kvalid(c, B_HALO)
        kbt = np.zeros((128, len(tab)), np.float32)
        for (d, r, m), idx in tab.items():
            u = B_HALO + r + d * (128 * m - 64 + np.arange(128))
            kbt[:, idx] = valid[u]
        ins.append({"qT": np.ascontiguousarray(q[b, qd * TPC:(qd + 1) * TPC].T),
                    "kT": np.ascontiguousarray(kh.T), "V": vh, "kbt": kbt})
    return ins


def run_attn_b(ins):
    nc = _get("attn_b", build_attn_b)
    res = run_bass_kernel_spmd(nc, ins, core_ids=list(range(NCORES)))
    return [r["oT"] for r in res.results]


def build_mlstm():
    nc = bass.Bass("TRN2", target_bir_lowering=False)
    NT = SEQ // 128
    qk_in = nc.dram_tensor("qkT", [256, SEQ], F32, kind="ExternalInput").ap()
    cw_in = nc.dram_tensor("cw", [128, 2, 5], F32, kind="ExternalInput").ap()
    cb_in = nc.dram_tensor("cb", [128, 2], F32, kind="ExternalInput").ap()
    v_in = nc.dram_tensor("Vm", [SEQ, 128], BF16, kind="ExternalInput").ap()
    o_in = nc.dram_tensor("Om", [SEQ, 128], F32, kind="ExternalInput").ap()
    g_in = nc.dram_tensor("G", [SEQ, 4], F32, kind="ExternalInput").ap()
    gb_in = nc.dram_tensor("gb", [4], F32, kind="ExternalInput").ap()
    hg_in = nc.dram_tensor("hg", [128], F32, kind="ExternalInput").ap()
    h_out = nc.dram_tensor("hm", [SEQ, 128], BF16, kind="ExternalOutput").ap()
    I32 = mybir.dt.int32
    cx = Ctx(nc)
    with cx.stack:
        S = Sched(nc, cx.stack)
        qkb = cx.sb([128, 2, SEQ], BF16, "qkb")
        vaug = cx.sb([128, NT, 132], BF16, "vaug")
        SG = cx.sb([128, NT, 128], F32, "SG")
        Hf = cx.sb([128, NT, 128], F32, "Hf")
        ostg = cx.sb([128, NT, 128], BF16, "ostg")
        xin = [cx.sb([128, 2, 1028], F32, "xin%d" % i) for i in range(2)]
        cacc = [cx.sb([128, 2, 1024], F32, "cacc%d" % i) for i in range(2)]
        cw = cx.sb([128, 2, 5], F32, "cw")
        cb = cx.sb([128, 2], F32, "cb")
        G = cx.sb([128, NT, 4], F32, "G")
        gb = cx.sb([128, 4], F32, "gb")
        hg = cx.sb([128, 128], F32, "hg")
        L = cx.sb([128, 2, NT], F32, "L")
        CB = cx.sb([128, 2, NT], F32, "CB")
        TOT = cx.sb([128, 2, NT], F32, "TOT")
        EB = cx.sb([128, 2, NT], F32, "EB")
        NEB = cx.sb([128, 2, NT], F32, "NEB")
        RS = cx.sb([128, 2, NT], F32, "RS")
        EA = cx.sb([128, 2, NT], F32, "EA")
        FF = cx.sb([128, 2, NT], F32, "FF")
        tri_i = cx.sb([128, 128], I32, "tri_i")
        tri = cx.sb([128, 2, 128], F32, "tri")
        onesf = cx.sb([128, 128], F32, "onesf")
        ident = cx.sb([128, 128], BF16, "ident")
        one1 = cx.sb([128, 1], F32, "one1")
        lns = cx.sb([128, 1], F32, "lns")
        epsb = cx.sb([128, 1], F32, "epsb")
        Cst = cx.sb([128, 132], F32, "Cst")
        Cbf = cx.sb([128, 132], BF16, "Cbf")
        WT = [cx.sb([128, 128], BF16, "WT%d" % i) for i in range(4)]
        kpp = [cx.sb([128, 128], BF16, "kpp%d" % i) for i in range(4)]
        sm = [cx.sb([128, 8], F32, "sm%d" % i) for i in range(2)]
        hs = [cx.sb([128, 128], F32, "hs%d" % i) for i in range(2)]
        sqj = cx.sb([128, 128], F32, "sqj")
        ps_sL = [cx.ps([128, 128]) for _ in range(2)]
        ps_tL = [cx.ps([128, 128], BF16) for _ in range(2)]
        ps_o = [cx.ps([128, 132]) for _ in range(2)]
        ps_c = [cx.ps([128, 132]) for _ in range(2)]

        ct = S.tok("const")
        S.op("pool", lambda E: E.iota(tri_i[:, :], [[1, 128]], base=0, channel_multiplier=-1), writes=[ct])
        S.op("dve", lambda E: E.tensor_copy(out=tri[:, 0, :], in_=tri_i[:, :]), reads=[ct], writes=[ct])
        S.op("dve", lambda E: E.tensor_scalar(out=sqj[:, :], in0=tri[:, 0, :], scalar1=0.0, scalar2=None, op0=ALU.is_equal),
             reads=[ct], writes=[ct])
        S.op("dve", lambda E: E.tensor_copy(out=ident[:, :], in_=sqj[:, :]), reads=[ct], writes=[ct])
        S.op("dve", lambda E: E.tensor_scalar(out=tri[:, 1, :], in0=tri[:, 0, :], scalar1=0.0, scalar2=None, op0=ALU.is_le),
             reads=[ct], writes=[ct])
        S.op("dve", lambda E: E.tensor_scalar(out=tri[:, 0, :], in0=tri[:, 0, :], scalar1=0.0, scalar2=None, op0=ALU.is_ge),
             reads=[ct], writes=[ct])
        S.op("pool", lambda E: E.memset(onesf[:, :], 1.0), writes=[ct])
        S.op("pool", lambda E: E.memset(one1[:, :], 1.0), writes=[ct])
        S.op("pool", lambda E: E.memset(lns[:, :], float(np.log(128.0 ** -0.5))), writes=[ct])
        S.op("pool", lambda E: E.memset(epsb[:, :], EPS), writes=[ct])
        S.op("pool", lambda E: E.memset(Cst[:, :], 0.0), writes=[S.tok("Cst")])
        S.op("pool", lambda E: E.memset(Cbf[:, :], 0.0), writes=[S.tok("Cbf")])
        S.op("pool", lambda E: E.memset(vaug[:, :, 128:132], 1.0), writes=[S.tok("vaug1")])
        S.dma("sp", cw[:, :, :], cw_in, writes=[S.tok("cw")])
        S.dma("sp", cb[:, :], cb_in, writes=[S.tok("cw")])
        S.dma("sp", G[:, :, :], g_in.rearrange("(n p) g -> p n g", p=128), writes=[S.tok("G")], allow_slow_non_contiguous=True)
        S.dma("sp", gb[:, :], gb_in.partition_broadcast(128), writes=[S.tok("gb")])
        S.dma("sp", hg[:, :], hg_in.partition_broadcast(128), writes=[S.tok("hg")])
        vt = S.tok("vaug")
        for i in range(4):
            S.dma("act", vaug[:, i * 16:(i + 1) * 16, 0:128],
                  v_in[i * 2048:(i + 1) * 2048, :].rearrange("(n p) e -> p n e", p=128), writes=[vt])
        sgt = S.tok("SG")
        for i in range(4):
            S.dma("act", SG[:, i * 16:(i + 1) * 16, :],
                  o_in[i * 2048:(i + 1) * 2048, :].rearrange("(n p) e -> p n e", p=128), writes=[sgt])
        gt = S.tok("gates")
        for j in range(4):
            S.op("dve", lambda E, j=j: E.tensor_scalar(out=G[:, :, j], in0=G[:, :, j], scalar1=gb[:, j:j + 1], scalar2=None,
                                                       op0=ALU.add), reads=[S.tok("G"), S.tok("gb")], writes=[S.tok("G")])
        for dr in range(2):
            S.op("act", lambda E, dr=dr: E.activation(out=L[:, dr, :], in_=G[:, :, 2 + dr], func=AF.Exp, scale=-1.0),
                 reads=[S.tok("G")], writes=[gt])
        S.op("act", lambda E: E.activation(out=L[:, :, :], in_=L[:, :, :], func=AF.Ln, bias=one1[:, 0:1]),
             reads=[gt, ct], writes=[gt])
        pcs = ps_o[0]
        for dr in range(2):
            S.op("pe", lambda E, dr=dr: E.matmul(pcs[:, dr * 64:(dr + 1) * 64], tri[:, dr, :], L[:, dr, :], start=True, stop=True),
                 reads=[gt, ct], writes=[S.tok("pso", 0)])
        S.op("dve", lambda E: E.tensor_copy(out=CB[:, :, :], in_=pcs[:, 0:128].rearrange("p (d n) -> p d n", d=2)),
             reads=[S.tok("pso", 0)], writes=[gt])
        pcs1 = ps_o[1]
        S.op("pe", lambda E: E.matmul(pcs1[:, 0:128], onesf[:, :], L[:, :, :], start=True, stop=True),
             reads=[gt, ct], writes=[S.tok("pso", 1)])
        S.op("dve", lambda E: E.tensor_copy(out=TOT[:, :, :], in_=pcs1[:, 0:128].rearrange("p (d n) -> p d n", d=2)),
             reads=[S.tok("pso", 1)], writes=[gt])
        S.op("act", lambda E: E.activation(out=EB[:, :, :], in_=CB[:, :, :], func=AF.Exp, scale=-1.0), reads=[gt], writes=[gt])
        S.op("act", lambda E: E.activation(out=FF[:, :, :], in_=TOT[:, :, :], func=AF.Exp, scale=-1.0), reads=[gt], writes=[gt])
        for dr in range(2):
            S.op("dve", lambda E, dr=dr: E.tensor_tensor(out=RS[:, dr, :], in0=CB[:, dr, :], in1=G[:, :, dr], op=ALU.add),
                 reads=[gt, S.tok("G")], writes=[gt])
        S.op("dve", lambda E: E.tensor_tensor(out=EA[:, :, :], in0=RS[:, :, :], in1=TOT[:, :, :], op=ALU.subtract),
             reads=[gt], writes=[gt])
        S.op("act", lambda E: E.activation(out=RS[:, :, :], in_=RS[:, :, :], func=AF.Exp, bias=lns[:, 0:1]), reads=[gt], writes=[gt])
        S.op("act", lambda E: E.activation(out=EA[:, :, :], in_=EA[:, :, :], func=AF.Exp, bias=lns[:, 0:1]), reads=[gt], writes=[gt])
        qkt = S.tok("qkb")
        NP = SEQ // 1024
        for pc in range(NP):
            xb, xbt = xin[pc % 2], S.tok("xin", pc % 2)
            ac, act_ = cacc[pc % 2], S.tok("cacc", pc % 2)
            t0 = pc * 1024
            lo, hi = max(0, t0 - 2), min(SEQ, t0 + 1026)
            if pc == 0 or pc == NP - 1:
                S.op("pool", lambda E, xb=xb: E.memset(xb[:, :, :], 0.0), writes=[xbt])
            S.dma("sp", xb[:, :, lo - (t0 - 2):hi - (t0 - 2)],
                  qk_in[:, lo:hi].rearrange("(j p) t -> p j t", p=128), writes=[xbt])
            for j in range(2):
                S.op("dve", lambda E, j=j, xb=xb, ac=ac: E.tensor_scalar(
                    out=ac[:, j, :], in0=xb[:, j, 0:1024], scalar1=cw[:, j, 0:1], scalar2=None, op0=ALU.mult),
                    reads=[xbt, S.tok("cw")], writes=[act_])
                for tap in range(1, 5):
                    S.op("dve", lambda E, j=j, xb=xb, ac=ac, tap=tap: E.scalar_tensor_tensor(
                        out=ac[:, j, :], in0=xb[:, j, tap:tap + 1024], scalar=cw[:, j, tap:tap + 1], in1=ac[:, j, :],
                        op0=ALU.mult, op1=ALU.add), reads=[xbt, act_], writes=[act_])
                S.op("act", lambda E, j=j, ac=ac, t0=t0: E.activation(
                    out=qkb[:, j, t0:t0 + 1024], in_=ac[:, j, :], func=AF.Silu, bias=cb[:, j:j + 1]),
                    reads=[act_, S.tok("cw")], writes=[qkt])
        for i in range(4):
            S.op("act", lambda E, i=i: E.activation(out=SG[:, i * 16:(i + 1) * 16, :], in_=SG[:, i * 16:(i + 1) * 16, :],
                                                   func=AF.Sigmoid), reads=[sgt], writes=[sgt])
        vts = [vt, S.tok("vaug1")]
        steps = [(0, n) for n in range(NT)] + [(1, n) for n in range(NT - 1, -1, -1)]

        def emitP1(it):
            dr, n = steps[it]
            tsl = slice(n * 128, (n + 1) * 128)
            sl = it % 2
            pst, ptk = S.tok("pss", sl), S.tok("pst", sl)
            W_, Wt = WT[sl], S.tok("WT", sl)
            kp, kpt = kpp[sl], S.tok("kpp", sl)
            S.op("pe", lambda E: E.matmul(ps_sL[sl][:, :], qkb[:, 1, tsl], qkb[:, 0, tsl], start=True, stop=True),
                 reads=[qkt], writes=[pst])
            S.op("pe", lambda E: E.transpose(ps_tL[sl][:, :], qkb[:, 1, tsl], ident[:, :]), reads=[qkt, ct], writes=[ptk])
            S.op("dve", lambda E: E.scalar_tensor_tensor(
                out=W_[:, :], in0=ps_sL[sl][:, :], scalar=RS[:, dr, n:n + 1], in1=tri[:, dr, :], op0=ALU.mult, op1=ALU.mult),
                reads=[pst, gt, ct], writes=[Wt])
            S.op("act", lambda E: E.activation(out=kp[:, :], in_=ps_tL[sl][:, :], func=AF.Copy, scale=EA[:, dr, n:n + 1]),
                 reads=[ptk, gt], writes=[kpt])

        def emitP2(it):
            dr, n = steps[it]
            tsl = slice(n * 128, (n + 1) * 128)
            sl = it % 2
            W_, Wt = WT[sl], S.tok("WT", sl)
            kp, kpt = kpp[sl], S.tok("kpp", sl)
            pso, pot = ps_o[it % 2], S.tok("pso", it % 2)
            psc, pct = ps_c[it % 2], S.tok("psc", it % 2)
            s_, st = sm[it % 2], S.tok("sm", it % 2)
            if it == NT:
                S.op("pool", lambda E: E.memset(Cst[:, :], 0.0), reads=[], writes=[S.tok("Cst")])
                S.op("pool", lambda E: E.memset(Cbf[:, :], 0.0), reads=[], writes=[S.tok("Cbf")])
            S.op("pe", lambda E: E.matmul(pso[:, 0:130], W_[:, :], vaug[:, n, 0:130], start=True, stop=False),
                 reads=[Wt] + vts, writes=[pot])
            S.op("pe", lambda E: E.matmul(pso[:, 0:130], qkb[:, 0, tsl], Cbf[:, 0:130], start=False, stop=True),
                 reads=[qkt, S.tok("Cbf")], writes=[pot])
            S.op("pe", lambda E: E.matmul(psc[:, 0:130], kp[:, :], vaug[:, n, 0:130], start=True, stop=True),
                 reads=[kpt] + vts, writes=[pct])
            S.op("dve", lambda E: E.scalar_tensor_tensor(
                out=Cst[:, 0:130], in0=Cst[:, 0:130], scalar=FF[:, dr, n:n + 1], in1=psc[:, 0:130], op0=ALU.mult, op1=ALU.add),
                reads=[pct, gt, S.tok("Cst")], writes=[S.tok("Cst")])
            S.op("act", lambda E: E.copy(out=Cbf[:, 0:130], in_=Cst[:, 0:130]), reads=[S.tok("Cst")], writes=[S.tok("Cbf")])
            S.op("act", lambda E: E.activation(out=s_[:, 0:1], in_=pso[:, 128:129], func=AF.Abs, scale=EB[:, dr, n:n + 1]),
                 reads=[pot, gt], writes=[st])
            S.op("dve", lambda E: E.tensor_scalar(out=s_[:, 1:2], in0=s_[:, 0:1], scalar1=1.0, scalar2=None, op0=ALU.max),
                 reads=[st], writes=[st])
            S.op("dve", lambda E: E.reciprocal(out=s_[:, 2:3], in_=s_[:, 1:2]), reads=[st], writes=[st])
            S.op("dve", lambda E: E.tensor_tensor(out=s_[:, 3:4], in0=s_[:, 2:3], in1=EB[:, dr, n:n + 1], op=ALU.mult),
                 reads=[st, gt], writes=[st])
            if dr == 0:
                S.op("act", lambda E: E.activation(out=Hf[:, n, :], in_=pso[:, 0:128], func=AF.Copy, scale=s_[:, 3:4]),
                     reads=[pot, st], writes=[S.tok("Hf", n)])
            else:
                h_, ht = hs[it % 2], S.tok("hs", it % 2)
                S.op("dve", lambda E: E.scalar_tensor_tensor(
                    out=h_[:, :], in0=pso[:, 0:128], scalar=s_[:, 3:4], in1=Hf[:, n, :], op0=ALU.mult, op1=ALU.add),
                    reads=[pot, st, S.tok("Hf", n)], writes=[ht])
                S.op("act", lambda E: E.activation(out=sqj[:, :], in_=h_[:, :], func=AF.Square, accum_out=s_[:, 4:5]),
                     reads=[ht, st], writes=[st, S.tok("sqj")])
                S.op("act", lambda E: E.activation(out=s_[:, 5:6], in_=s_[:, 4:5], func=AF.Sqrt, scale=1.0 / 128,
                                                   bias=epsb[:, 0:1]), reads=[st, ct], writes=[st])
                S.op("dve", lambda E: E.reciprocal(out=s_[:, 6:7], in_=s_[:, 5:6]), reads=[st], writes=[st])
                S.op("dve", lambda E: E.scalar_tensor_tensor(
                    out=h_[:, :], in0=h_[:, :], scalar=s_[:, 6:7], in1=hg[:, :], op0=ALU.mult, op1=ALU.mult),
                    reads=[ht, st, S.tok("hg")], writes=[ht])
                S.op("dve", lambda E: E.tensor_tensor(out=ostg[:, n, :], in0=h_[:, :], in1=SG[:, n, :], op=ALU.mult),
                     reads=[ht, sgt], writes=[S.tok("ostg")])

        LOOK = 1
        for idx in range(len(steps) + LOOK):
            if idx < len(steps):
                emitP1(idx)
            if idx - LOOK >= 0:
                emitP2(idx - LOOK)
        finals = []
        for i in range(4):
            ft = S.tok("fin", i)
            finals.append(ft)
            S.dma("sp", h_out[i * 2048:(i + 1) * 2048, :].rearrange("(n p) e -> p n e", p=128), ostg[:, i * 16:(i + 1) * 16, :],
                  reads=[S.tok("ostg")], writes=[ft])
        S.finish(finals)
    return nc


def run_mlstm(ins):
    nc = _get("mlstm", build_mlstm)
    res = run_bass_kernel_spmd(nc, ins, core_ids=list(range(NCORES)))
    return [r["hm"] for r in res.results]


def build_final():
    nc = bass.Bass("TRN2", target_bir_lowering=False)
    xin = nc.dram_tensor("xT", [D, TPC], F32, kind="ExternalInput").ap()
    g_in = nc.dram_tensor("g", [D], F32, kind="ExternalInput").ap()
    xout = nc.dram_tensor("yT", [D, TPC], F32, kind="ExternalOutput").ap()
    cx = Ctx(nc)
    with cx.stack:
        S = Sched(nc, cx.stack)
        xT = cx.sb([128, 8, TPC], F32, "xT_sb")
        xo = [cx.sb([128, 8, 512], F32, "xo%d" % i) for i in range(2)]
        gcol = cx.sb([128, 8], F32, "gcol")
        ones_bf = cx.sb([128, 128], BF16, "ones")
        epsb = cx.sb([128, 1], F32, "eps")
        sq = cx.sb([128, 8, 512], BF16, "sq")
        rs = cx.sb([128, 512], F32, "rs")
        ps_ss = cx.ps([128, 512])
        scratch = {"sq": sq, "rs": rs, "eps": epsb, "sq_tok": S.tok("sq"), "rs_tok": S.tok("rs")}
        S.op("pool", lambda E: E.memset(ones_bf[:, :], 1.0), writes=[S.tok("ones")])
        S.op("pool", lambda E: E.memset(epsb[:, :], EPS), writes=[S.tok("eps")])
        S.dma("sp", gcol[:, :], g_in.rearrange("(k p) -> p k", p=128), writes=[S.tok("gcol")],
              allow_slow_non_contiguous=True)
        xin_v = xin.rearrange("(k p) t -> p k t", p=128)
        xout_v = xout.rearrange("(k p) t -> p k t", p=128)
        xtoks = [S.tok("x", i) for i in range(4)]
        for i in range(4):
            for k in range(0, 8, 4):
                S.dma("sp", xT[:, k:k + 4, i * 512:(i + 1) * 512], xin_v[:, k:k + 4, i * 512:(i + 1) * 512],
                      writes=[xtoks[i]])
        for t in (S.tok("ones"), S.tok("eps"), S.tok("gcol")):
            for e in ("act", "dve", "pe"):
                S._deps(e, [t], [])
        finals = []
        for gi in range(4):
            emit_rmsnorm(S, cx, xT, xtoks[gi], gi * 512, 512, gcol, xo[gi % 2], S.tok("xo", gi % 2), ones_bf, ps_ss,
                         S.tok("ps_ss"), scratch)
            ft = S.tok("fin", gi)
            finals.append(ft)
            S.dma("sp", xout_v[:, :, gi * 512:(gi + 1) * 512], xo[gi % 2][:, :, :], reads=[S.tok("xo", gi % 2)], writes=[ft])
        S.finish(finals)
    return nc


def run_final(xT_shards, g):
    nc = _get("final", build_final)
    in_maps = [{"xT": xT_shards[c], "g": g} for c in range(NCORES)]
    res = run_bass_kernel_spmd(nc, in_maps, core_ids=list(range(NCORES)))
    return [r["yT"] for r in res.results]


def _f32c(a):
    return np.ascontiguousarray(np.asarray(a, dtype=np.float32))


def _gather_tm(res, name):
    a = np.stack([np.asarray(res[c][name]) for c in range(NCORES)])
    return a.reshape(BATCH, SEQ, a.shape[-1])


def _gather_fm(res, name):
    a = np.stack([np.asarray(res[c][name]).T for c in range(NCORES)])
    return a.reshape(BATCH, SEQ, a.shape[-1])


def _layer_ab(xT, g, w_in, conv_w, conv_b, gate_b, hnorm_g, w_out):
    res = run_proj("AB", xT, g, w_in)
    qk = _gather_fm(res, "qkT")
    Vm = _gather_tm(res, "Vm")
    Om = _gather_tm(res, "Om")
    Gm = _gather_tm(res, "Gm")
    gate_b = _f32c(gate_b).reshape(16)
    ins = []
    for c in range(NCORES):
        b, h = c // 4, c % 4
        hs = slice(h * 128, (h + 1) * 128)
        ks = slice(512 + h * 128, 512 + (h + 1) * 128)
        qkT = np.ascontiguousarray(np.concatenate([qk[b, :, hs], qk[b, :, ks]], axis=1).T)
        cw = np.ascontiguousarray(np.stack([conv_w[:, hs].T, conv_w[:, ks].T], axis=1))
        cb = np.ascontiguousarray(np.stack([conv_b[hs], conv_b[ks]], axis=1))
        gi = [h, 4 + h, 8 + h, 12 + h]
        ins.append({"qkT": qkT, "cw": _f32c(cw), "cb": _f32c(cb), "Vm": np.ascontiguousarray(Vm[b, :, hs]),
                    "Om": np.ascontiguousarray(Om[b, :, hs]), "G": np.ascontiguousarray(Gm[b][:, gi]),
                    "gb": np.ascontiguousarray(gate_b[gi]), "hg": _f32c(hnorm_g[hs])})
    hm = run_mlstm(ins)
    qb = _gather_fm(res, "qbT")
    kb = _gather_fm(res, "kbT")
    Vb = _gather_tm(res, "Vb")
    ob = run_attn_b(prep_attn_b(qb, kb, Vb))
    mT = []
    for c in range(NCORES):
        b, qd = c // 4, c % 4
        parts = [np.asarray(hm[b * 4 + h])[qd * TPC:(qd + 1) * TPC].T for h in range(4)]
        parts.append(np.asarray(ob[c]))
        mT.append(np.ascontiguousarray(np.concatenate(parts, axis=0)))
    return run_out(xT, mT, w_out)


def _layer_c(xT, g, w_in, sink, w_out):
    res = run_proj("C", xT, g, w_in)
    q = _gather_fm(res, "qT")
    k = _gather_fm(res, "kT")
    v = _gather_tm(res, "V")
    oT = run_attn_c(prep_attn_c(q, k, v, _f32c(sink)))
    return run_out(xT, [np.asarray(o) for o in oT], w_out)


def kernel_unfused(x, norm_g, ffn_w1, ffn_w3, ffn_w2, ab_w_in, ab_conv_w, ab_conv_b, ab_gate_b, ab_hnorm_g, ab_w_out,
                   c_w_in, c_sink, c_w_out, final_g):
    x = np.asarray(x, dtype=np.float32)
    norm_g, ffn_w1, ffn_w3, ffn_w2 = (np.asarray(a, np.float32) for a in (norm_g, ffn_w1, ffn_w3, ffn_w2))
    ab_w_in, ab_conv_w, ab_conv_b, ab_gate_b, ab_hnorm_g, ab_w_out = (
        np.asarray(a, np.float32) for a in (ab_w_in, ab_conv_w, ab_conv_b, ab_gate_b, ab_hnorm_g, ab_w_out))
    c_w_in, c_sink, c_w_out, final_g = (np.asarray(a, np.float32) for a in (c_w_in, c_sink, c_w_out, final_g))
    xT = [np.ascontiguousarray(x[c // 4, (c % 4) * TPC:(c % 4 + 1) * TPC].T) for c in range(NCORES)]
    for l in range(4):
        j = l // 2
        xT = run_ffn(xT, _f32c(norm_g[l, 0]), _f32c(ffn_w1[l, 0]), _f32c(ffn_w3[l, 0]), _f32c(ffn_w2[l, 0]))
        if l % 2 == 0:
            xT = _layer_ab(xT, _f32c(norm_g[l, 1]), _f32c(ab_w_in[j]), ab_conv_w[j], ab_conv_b[j], ab_gate_b[j],
                           ab_hnorm_g[j], _f32c(ab_w_out[j]))
        else:
            xT = _layer_c(xT, _f32c(norm_g[l, 1]), _f32c(c_w_in[j]), c_sink[j], _f32c(c_w_out[j]))
        xT = run_ffn(xT, _f32c(norm_g[l, 2]), _f32c(ffn_w1[l, 1]), _f32c(ffn_w3[l, 1]), _f32c(ffn_w2[l, 1]))
    yT = run_final(xT, _f32c(final_g))
    out = np.empty((BATCH, SEQ, D), np.float32)
    for c in range(NCORES):
        out[c // 4, (c % 4) * TPC:(c % 4 + 1) * TPC] = np.asarray(yT[c]).T
    return out


def stage_ffn(S, nc, X, g_in, w1, w3, w2):
    NF = DFF // 128
    TP = 1024
    cx = Ctx(nc)
    with cx.stack:
        xT = cx.sb([128, 8, TPC], F32, "xT_sb")
        xn = cx.sb([128, 8, TP], BF16, "xn")
        gb = cx.sb([128, NF, TP], BF16, "gb")
        gcol = cx.sb([128, 8], F32, "gcol")
        ones_bf = cx.sb([128, 128], BF16, "ones")
        epsb = cx.sb([128, 1], F32, "eps")
        sq = cx.sb([128, 8, 512], BF16, "sq")
        rs = cx.sb([128, 512], F32, "rs")
        w13 = [cx.sb([128, 2, 8, 128], BF16, "w13_%d" % i) for i in range(3)]
        w2b = [cx.sb([128, NF, 128], BF16, "w2b_%d" % i) for i in range(2)]
        sil = [cx.sb([128, 512], F32, "sil%d" % i) for i in range(2)]
        ps_h = [[cx.ps([128, 512]) for _ in range(2)] for _ in range(2)]
        ps_y = [cx.ps([128, 512]) for _ in range(2)]
        ps_ss = cx.ps([128, 512])
        scratch = {"sq": sq, "rs": rs, "eps": epsb, "sq_tok": S.tok("sq"), "rs_tok": S.tok("rs")}
        S.op("pool", lambda E: E.memset(ones_bf[:, :], 1.0), writes=[S.tok("ones")])
        S.op("pool", lambda E: E.memset(epsb[:, :], EPS), writes=[S.tok("eps")])
        S.dma("sp", gcol[:, :], g_in.rearrange("(k p) -> p k", p=128), writes=[S.tok("gcol")],
              allow_slow_non_contiguous=True)
        xin_v = X.rearrange("(k p) t -> p k t", p=128)
        xtoks = [S.tok("x", i) for i in range(4)]
        for i in range(4):
            for k in range(0, 8, 4):
                S.dma("sp", xT[:, k:k + 4, i * 512:(i + 1) * 512], xin_v[:, k:k + 4, i * 512:(i + 1) * 512],
                      writes=[xtoks[i]])
        w1v = w1.rearrange("(k p) f -> p k f", p=128)
        w3v = w3.rearrange("(k p) f -> p k f", p=128)
        w2v = w2.rearrange("(f p) d -> p f d", p=128)
        for t in (S.tok("ones"), S.tok("eps"), S.tok("gcol")):
            for e in ("act", "dve", "pe"):
                S._deps(e, [t], [])
        for pas in range(TPC // TP):
            for gi in range(TP // 512):
                grp = pas * (TP // 512) + gi
                emit_rmsnorm(S, cx, xT, xtoks[grp], grp * 512, 512, gcol,
                             xn[:, :, gi * 512:(gi + 1) * 512], S.tok("xn", gi), ones_bf, ps_ss,
                             S.tok("ps_ss"), scratch)
            for f in range(NF):
                wb = w13[f % 3]
                wt = S.tok("w13", f % 3)
                S.dma("pool", wb[:, 0, :, :], w1v[:, :, f * 128:(f + 1) * 128], writes=[wt])
                S.dma("pool", wb[:, 1, :, :], w3v[:, :, f * 128:(f + 1) * 128], writes=[wt])
                for gi in range(TP // 512):
                    pb = ps_h[(f * 2 + gi) % 2]
                    pt = [S.tok("ps_h", (f * 2 + gi) % 2, j) for j in range(2)]
                    tsl = slice(gi * 512, (gi + 1) * 512)
                    for j in range(2):
                        for k in range(8):
                            S.op("pe", lambda E, j=j, k=k, pb=pb, wb=wb, tsl=tsl: E.matmul(
                                pb[j][:, :], wb[:, j, k, :], xn[:, k, tsl], start=(k == 0), stop=(k == 7)),
                                reads=[wt, S.tok("xn", gi)], writes=[pt[j]])
                    sb_ = sil[(f * 2 + gi) % 2]
                    st = S.tok("sil", (f * 2 + gi) % 2)
                    S.op("act", lambda E, pb=pb, sb_=sb_: E.activation(out=sb_[:, :], in_=pb[0][:, :], func=AF.Silu),
                         reads=[pt[0]], writes=[st])
                    S.op("dve", lambda E, pb=pb, sb_=sb_, f=f, tsl=tsl: E.tensor_tensor(
                        out=gb[:, f, tsl], in0=sb_[:, :], in1=pb[1][:, :], op=ALU.mult),
                        reads=[st, pt[1]], writes=[S.tok("gb", gi)])
            for d in range(8):
                wb = w2b[d % 2]
                wt = S.tok("w2b", d % 2)
                S.dma("pool", wb[:, 0:11, :], w2v[:, 0:11, d * 128:(d + 1) * 128], writes=[wt])
                S.dma("pool", wb[:, 11:22, :], w2v[:, 11:22, d * 128:(d + 1) * 128], writes=[wt])
                for gi in range(TP // 512):
                    grp = pas * (TP // 512) + gi
                    py = ps_y[(d * 2 + gi) % 2]
                    pyt = S.tok("ps_y", (d * 2 + gi) % 2)
                    tsl = slice(gi * 512, (gi + 1) * 512)
                    for f in range(NF):
                        S.op("pe", lambda E, f=f, py=py, wb=wb, tsl=tsl: E.matmul(
                            py[:, :], wb[:, f, :], gb[:, f, tsl], start=(f == 0), stop=(f == NF - 1)),
                            reads=[wt, S.tok("gb", gi)], writes=[pyt])
                    xs = xT[:, d, grp * 512:(grp + 1) * 512]
                    S.op("dve", lambda E, py=py, xs=xs: E.scalar_tensor_tensor(
                        out=xs, in0=py[:, :], scalar=0.5, in1=xs, op0=ALU.mult, op1=ALU.add),
                        reads=[pyt, xtoks[grp]], writes=[xtoks[grp]])
        for i in range(4):
            for k in range(0, 8, 4):
                S.dma("sp", xin_v[:, k:k + 4, i * 512:(i + 1) * 512], xT[:, k:k + 4, i * 512:(i + 1) * 512],
                      reads=[xtoks[i]], writes=[S.tok("xout", i, k)])
        S.end_stage()


def stage_proj(S, nc, kind, X, g_in, w, outs):
    NC_, specs = PROJ_SPECS[kind]
    cx = Ctx(nc)
    with cx.stack:
        xT = cx.sb([128, 8, TPC], F32, "xT_sb")
        xn = cx.sb([128, 8, TPC], BF16, "xn")
        wsb = cx.sb([128, 8, NC_], BF16, "wsb")
        gcol = cx.sb([128, 8], F32, "gcol")
        ones_bf = cx.sb([128, 128], BF16, "ones")
        epsb = cx.sb([128, 1], F32, "eps")
        sq = cx.sb([128, 8, 512], BF16, "sq")
        rs = cx.sb([128, 512], F32, "rs")
        stg_fm = [cx.sb([128, TPC], F32, "stgfm%d" % i) for i in range(2)]
        stg_tm = [cx.sb([128, 512], F32, "stgtm%d" % i) for i in range(3)]
        ps = [cx.ps([128, 512]) for _ in range(6)]
        ps_ss = cx.ps([128, 512])
        scratch = {"sq": sq, "rs": rs, "eps": epsb, "sq_tok": S.tok("sq"), "rs_tok": S.tok("rs")}
        S.op("pool", lambda E: E.memset(ones_bf[:, :], 1.0), writes=[S.tok("ones")])
        S.op("pool", lambda E: E.memset(epsb[:, :], EPS), writes=[S.tok("eps")])
        S.dma("sp", gcol[:, :], g_in.rearrange("(k p) -> p k", p=128), writes=[S.tok("gcol")],
              allow_slow_non_contiguous=True)
        xin_v = X.rearrange("(k p) t -> p k t", p=128)
        xtoks = [S.tok("x", i) for i in range(4)]
        for i in range(4):
            for k in range(0, 8, 4):
                S.dma("sp", xT[:, k:k + 4, i * 512:(i + 1) * 512], xin_v[:, k:k + 4, i * 512:(i + 1) * 512],
                      writes=[xtoks[i]])
        wv = w.rearrange("(k p) f -> p k f", p=128)
        wtok = S.tok("w")
        for k in range(8):
            S.dma("pool", wsb[:, k, :], wv[:, k, :], writes=[wtok])
        for t in (S.tok("ones"), S.tok("eps"), S.tok("gcol")):
            for e in ("act", "dve", "pe"):
                S._deps(e, [t], [])
        for gi in range(4):
            emit_rmsnorm(S, cx, xT, xtoks[gi], gi * 512, 512, gcol, xn[:, :, gi * 512:(gi + 1) * 512],
                         S.tok("xn", gi), ones_bf, ps_ss, S.tok("ps_ss"), scratch)
        pi = ei = si = ti = 0
        for name, c0, ncol, lay, dt in specs:
            if lay == "tm":
                continue
            M = 64 if lay == "fm64" else 128
            for ch in range(ncol // M):
                stg = stg_fm[si % 2]
                stt = S.tok("stgfm", si % 2)
                si += 1
                for gi in range(4):
                    p = ps[pi % 6]
                    pt = S.tok("ps", pi % 6)
                    pi += 1
                    for k in range(8):
                        S.op("pe", lambda E, p=p, k=k, cc=c0 + ch * M, M=M, gi=gi: E.matmul(
                            p[0:M, :], wsb[:, k, cc:cc + M], xn[:, k, gi * 512:(gi + 1) * 512],
                            start=(k == 0), stop=(k == 7)), reads=[wtok, S.tok("xn", gi)], writes=[pt])
                    dst = _stg_view(stg, dt, M, TPC)[:, gi * 512:(gi + 1) * 512]
                    if ei % 2 == 0:
                        S.op("act", lambda E, p=p, dst=dst, M=M: E.copy(out=dst, in_=p[0:M, :]), reads=[pt], writes=[stt])
                    else:
                        S.op("dve", lambda E, p=p, dst=dst, M=M: E.tensor_copy(out=dst, in_=p[0:M, :]), reads=[pt], writes=[stt])
                    ei += 1
                S.dma("sp", outs[name][ch * M:(ch + 1) * M, :], _stg_view(stg, dt, M, TPC), reads=[stt],
                      writes=[S.tok("fin", name, ch)])
        for tt in range(16):
            for name, c0, ncol, lay, dt in specs:
                if lay != "tm":
                    continue
                p = ps[pi % 6]
                pt = S.tok("ps", pi % 6)
                pi += 1
                for k in range(8):
                    S.op("pe", lambda E, p=p, k=k, c0=c0, ncol=ncol, tt=tt: E.matmul(
                        p[:, 0:ncol], xn[:, k, tt * 128:(tt + 1) * 128], wsb[:, k, c0:c0 + ncol],
                        start=(k == 0), stop=(k == 7)), reads=[wtok, S.tok("xn", tt // 4)], writes=[pt])
                stg = stg_tm[ti % 3]
                stt = S.tok("stgtm", ti % 3)
                ti += 1
                dst = _stg_view(stg, dt, 128, ncol)
                if ei % 2 == 0:
                    S.op("act", lambda E, p=p, dst=dst, ncol=ncol: E.copy(out=dst, in_=p[:, 0:ncol]), reads=[pt], writes=[stt])
                else:
                    S.op("dve", lambda E, p=p, dst=dst, ncol=ncol: E.tensor_copy(out=dst, in_=p[:, 0:ncol]), reads=[pt], writes=[stt])
                ei += 1
                S.dma("sp", outs[name][tt * 128:(tt + 1) * 128, :], dst, reads=[stt], writes=[S.tok("fin", name, "t", tt)],
                      allow_slow_non_contiguous=(ncol < 64))
        S.end_stage()


def stage_out(S, nc, X, MT, w):
    cx = Ctx(nc)
    with cx.stack:
        xT = cx.sb([128, 8, TPC], F32, "xT_sb")
        mT = cx.sb([128, 8, TPC], BF16, "mT_sb")
        wsb = cx.sb([128, 8, D], BF16, "wsb")
        ps = [cx.ps([128, 512]) for _ in range(4)]
        xin_v = X.rearrange("(k p) t -> p k t", p=128)
        m_v = MT.rearrange("(k p) t -> p k t", p=128)
        wv = w.rearrange("(k p) f -> p k f", p=128)
        for k in range(8):
            S.dma("pool", wsb[:, k, :], wv[:, k, :], writes=[S.tok("w")])
        for k in range(0, 8, 4):
            S.dma("sp", mT[:, k:k + 4, :], m_v[:, k:k + 4, :], writes=[S.tok("m")])
        xt = [S.tok("x", d) for d in range(8)]
        for d in range(8):
            S.dma("sp", xT[:, d, :], xin_v[:, d, :], writes=[xt[d]])
        i = 0
        for d in range(8):
            for gi in range(4):
                p, pt = ps[i % 4], S.tok("ps", i % 4)
                i += 1
                for k in range(8):
                    S.op("pe", lambda E, p=p, k=k, d=d, gi=gi: E.matmul(
                        p[:, :], wsb[:, k, d * 128:(d + 1) * 128], mT[:, k, gi * 512:(gi + 1) * 512],
                        start=(k == 0), stop=(k == 7)), reads=[S.tok("w"), S.tok("m")], writes=[pt])
                xs = xT[:, d, gi * 512:(gi + 1) * 512]
                S.op("dve", lambda E, p=p, xs=xs: E.tensor_tensor(out=xs, in0=p[:, :], in1=xs, op=ALU.add),
                     reads=[pt, xt[d]], writes=[xt[d]])
            S.dma("sp", xin_v[:, d, :], xT[:, d, :], reads=[xt[d]], writes=[S.tok("fin", d)])
        S.end_stage()


def stage_final(S, nc, X, g_in, xout):
    cx = Ctx(nc)
    with cx.stack:
        xT = cx.sb([128, 8, TPC], F32, "xT_sb")
        xo = [cx.sb([128, 8, 512], F32, "xo%d" % i) for i in range(2)]
        gcol = cx.sb([128, 8], F32, "gcol")
        ones_bf = cx.sb([128, 128], BF16, "ones")
        epsb = cx.sb([128, 1], F32, "eps")
        sq = cx.sb([128, 8, 512], BF16, "sq")
        rs = cx.sb([128, 512], F32, "rs")
        ps_ss = cx.ps([128, 512])
        scratch = {"sq": sq, "rs": rs, "eps": epsb, "sq_tok": S.tok("sq"), "rs_tok": S.tok("rs")}
        S.op("pool", lambda E: E.memset(ones_bf[:, :], 1.0), writes=[S.tok("ones")])
        S.op("pool", lambda E: E.memset(epsb[:, :], EPS), writes=[S.tok("eps")])
        S.dma("sp", gcol[:, :], g_in.rearrange("(k p) -> p k", p=128), writes=[S.tok("gcol")],
              allow_slow_non_contiguous=True)
        xin_v = X.rearrange("(k p) t -> p k t", p=128)
        xout_v = xout.rearrange("(k p) t -> p k t", p=128)
        xtoks = [S.tok("x", i) for i in range(4)]
        for i in range(4):
            for k in range(0, 8, 4):
                S.dma("sp", xT[:, k:k + 4, i * 512:(i + 1) * 512], xin_v[:, k:k + 4, i * 512:(i + 1) * 512],
                      writes=[xtoks[i]])
        for t in (S.tok("ones"), S.tok("eps"), S.tok("gcol")):
            for e in ("act", "dve", "pe"):
                S._deps(e, [t], [])
        for gi in range(4):
            emit_rmsnorm(S, cx, xT, xtoks[gi], gi * 512, 512, gcol, xo[gi % 2], S.tok("xo", gi % 2), ones_bf, ps_ss,
                         S.tok("ps_ss"), scratch)
            S.dma("sp", xout_v[:, :, gi * 512:(gi + 1) * 512], xo[gi % 2][:, :, :], reads=[S.tok("xo", gi % 2)],
                  writes=[S.tok("fin", gi)])
        S.end_stage()


def stage_attn_c(S, nc, q_in, k_in, v_in, kb_in, sink_in, o_out):
    KH = TPC + 256
    cx = Ctx(nc)
    slopes = alibi_slopes(16)
    with cx.stack:
        qT = cx.sb([64, 16, TPC], BF16, "qT_sb")
        kT = cx.sb([64, 4, KH], BF16, "kT_sb")
        V = cx.sb([128, 18, 256], BF16, "V_sb")
        kb = cx.sb([128, 18], F32, "kb_sb")
        esink = cx.sb([64, 16], F32, "esink")
        ones_bf = cx.sb([128, 64], BF16, "ones")
        bias = cx.sb([128, 3, 16, 128], F32, "bias")
        tmp = [cx.sb([128, 512], F32, "tmp%d" % i) for i in range(3)]
        P = [cx.sb([128, 512], BF16, "P%d" % i) for i in range(3)]
        rden = [cx.sb([64, 512], F32, "rden%d" % i) for i in range(2)]
        ostg = [cx.sb([64, 4, TPC], BF16, "ostg%d" % i) for i in range(2)]
        ps_s = [cx.ps([128, 512]) for _ in range(4)]
        ps_n = [cx.ps([64, 512]) for _ in range(2)]
        ps_d = [cx.ps([64, 512]) for _ in range(2)]
        S.dma("sp", qT[:, :, :], q_in.rearrange("(h e) t -> e h t", e=64), writes=[S.tok("q")])
        S.dma("sp", kT[:, :, :], k_in.rearrange("(h e) t -> e h t", e=64), writes=[S.tok("k")])
        S.dma("sp", V[:, :, :], v_in.rearrange("(c p) f -> p c f", p=128), writes=[S.tok("v")])
        S.dma("sp", kb[:, :], kb_in, writes=[S.tok("kb")])
        S.dma("sp", esink[:, :], sink_in.partition_broadcast(64), writes=[S.tok("esink")])
        S.op("act", lambda E: E.activation(out=esink[:, :], in_=esink[:, :], func=AF.Exp),
             reads=[S.tok("esink")], writes=[S.tok("esink")])
        S.op("pool", lambda E: E.memset(ones_bf[:, :], 1.0), writes=[S.tok("ones")])
        A, At = emit_absrel(S, cx, 3, 128, 128, "absrel")
        bt = S.tok("bias")
        for h in range(16):
            S.op("dve", lambda E, h=h: E.tensor_scalar(out=bias[:, :, h, :], in0=A[:, :, :], scalar1=-slopes[h],
                                                       scalar2=None, op0=ALU.mult), reads=[At], writes=[bt])
        scale = 64 ** -0.5
        it = 0
        for kvh in range(4):
            og = ostg[kvh % 2]
            ogt = S.tok("ostg", kvh % 2)
            for b in range(16):
                pn, pd = ps_n[it % 2], ps_d[it % 2]
                pnt, pdt = S.tok("psn", it % 2), S.tok("psd", it % 2)
                for c in range(3):
                    j = it * 3 + c
                    pss = ps_s[j % 4]
                    pst = S.tok("pss", j % 4)
                    S.op("pe", lambda E, pss=pss, kvh=kvh, b=b, c=c: E.matmul(
                        pss[:, :], kT[:, kvh, (b + c) * 128:(b + c + 1) * 128],
                        qT[:, kvh * 4:(kvh + 1) * 4, b * 128:(b + 1) * 128], start=True, stop=True),
                        reads=[S.tok("q"), S.tok("k")], writes=[pst])
                    tm_, tmt = tmp[j % 3], S.tok("tmp", j % 3)
                    S.op("dve", lambda E, pss=pss, tm_=tm_, kvh=kvh, c=c: E.scalar_tensor_tensor(
                        out=tm_[:, :], in0=pss[:, :], scalar=scale, in1=bias[:, c, kvh * 4:(kvh + 1) * 4, :],
                        op0=ALU.mult, op1=ALU.add), reads=[pst, bt], writes=[tmt])
                    P_, Pt = P[j % 3], S.tok("P", j % 3)
                    S.op("act", lambda E, tm_=tm_, P_=P_, b=b, c=c: E.activation(
                        out=P_[:, :], in_=tm_[:, :], func=AF.Exp, bias=kb[:, b + c:b + c + 1]),
                        reads=[tmt, S.tok("kb")], writes=[Pt])
                    S.op("pe", lambda E, pn=pn, P_=P_, kvh=kvh, b=b, c=c: E.matmul(
                        pn[:, :], V[:, b + c, kvh * 64:(kvh + 1) * 64], P_[:, :], start=(c == 0), stop=(c == 2)),
                        reads=[Pt, S.tok("v")], writes=[pnt])
                    S.op("pe", lambda E, pd=pd, P_=P_, c=c: E.matmul(
                        pd[:, :], ones_bf[:, :], P_[:, :], start=(c == 0), stop=(c == 2)),
                        reads=[Pt, S.tok("ones")], writes=[pdt])
                rd, rdt = rden[it % 2], S.tok("rden", it % 2)
                for g in range(4):
                    h = kvh * 4 + g
                    S.op("dve", lambda E, rd=rd, pd=pd, g=g, h=h: E.tensor_scalar(
                        out=rd[:, g * 128:(g + 1) * 128], in0=pd[:, g * 128:(g + 1) * 128],
                        scalar1=esink[:, h:h + 1], scalar2=None, op0=ALU.add),
                        reads=[pdt, S.tok("esink")], writes=[rdt])
                S.op("dve", lambda E, rd=rd: E.reciprocal(out=rd[:, :], in_=rd[:, :]), reads=[rdt], writes=[rdt])
                S.op("dve", lambda E, rd=rd, pn=pn, og=og, b=b: E.tensor_tensor(
                    out=og[:, :, b * 128:(b + 1) * 128], in0=pn[:, :].rearrange("p (g q) -> p g q", g=4),
                    in1=rd[:, :].rearrange("p (g q) -> p g q", g=4), op=ALU.mult),
                    reads=[pnt, rdt], writes=[ogt])
                it += 1
            S.dma("sp", o_out[kvh * 256:(kvh + 1) * 256, :].rearrange("(g e) t -> e g t", e=64), og[:, :, :],
                  reads=[ogt], writes=[S.tok("fin", kvh)])
        S.end_stage()


def stage_attn_b(S, nc, q_in, k_in, v_in, kb_in, o_out):
    KH = TPC + 2 * B_HALO
    cx = Ctx(nc)
    slopes = alibi_slopes(8)
    tab = b_chunks()
    NCH = len(tab)
    with cx.stack:
        qT = cx.sb([128, 4, TPC], BF16, "qT_sb")
        kT = cx.sb([128, 4, KH], BF16, "kT_sb")
        V = cx.sb([128, NCH, 512], BF16, "V_sb")
        kb = cx.sb([128, NCH], F32, "kb_sb")
        ones_bf = cx.sb([128, 64], BF16, "ones")
        bias = cx.sb([128, 8, 3, 2, 128], F32, "bias")
        tmp = [cx.sb([128, 2, 128], F32, "tmp%d" % i) for i in range(3)]
        P = [cx.sb([128, 2, 128], BF16, "P%d" % i) for i in range(3)]
        accn = [cx.sb([64, TPC], F32, "accn%d" % i) for i in range(2)]
        accd = [cx.sb([64, TPC], F32, "accd%d" % i) for i in range(2)]
        ostg = [cx.sb([64, TPC], BF16, "ostg%d" % i) for i in range(2)]
        ps_s = [cx.ps([128, 2, 128]) for _ in range(3)]
        ps_n = [cx.ps([64, 128]) for _ in range(2)]
        ps_d = [cx.ps([64, 128]) for _ in range(2)]
        S.dma("sp", qT[:, :, :], q_in.rearrange("(k p) t -> p k t", p=128), writes=[S.tok("q")])
        for k in range(4):
            S.dma("sp", kT[:, k, :], k_in[k * 128:(k + 1) * 128, :], writes=[S.tok("k")])
        S.dma("sp", kb[:, :], kb_in, writes=[S.tok("kb")])
        for d in B_DILS:
            for r in range(d):
                M = 16 // d + 1
                c0 = tab[(d, r, 0)]
                start = B_HALO + r - 64 * d
                src = bass.AP(v_in.tensor, v_in.offset + start * 512, [[d * 512, 128], [d * 128 * 512, M], [1, 512]])
                S.dma("sp" if (r % 2 == 0) else "act", V[:, c0:c0 + M, :], src, writes=[S.tok("v")])
        S.op("pool", lambda E: E.memset(ones_bf[:, :], 1.0), writes=[S.tok("ones")])
        A, At = emit_absrel(S, cx, 2, 64, 64, "absrel")
        bt = S.tok("bias")
        for h in range(8):
            for di, d in enumerate(B_DILS):
                S.op("dve", lambda E, h=h, di=di, d=d: E.tensor_scalar(
                    out=bias[:, h, di, :, :], in0=A[:, :, :], scalar1=-slopes[h] * d, scalar2=None, op0=ALU.mult),
                    reads=[At], writes=[bt])
        scale = 64 ** -0.5
        it = 0
        for h in range(8):
            pb, hp = (h % 2) * 64, h // 2
            an, ad = accn[h % 2], accd[h % 2]
            ant, adt = S.tok("accn", h % 2), S.tok("accd", h % 2)
            for di, d in enumerate(B_DILS):
                for r in range(d):
                    for blk in range(16 // d):
                        q0 = r + d * 128 * blk
                        qsl = slice(q0, q0 + d * 127 + 1, d)
                        pss, pst = ps_s[it % 3], S.tok("pss", it % 3)
                        for c in range(2):
                            u0 = B_HALO + r + d * (128 * (blk + c) - 64)
                            S.op("pe", lambda E, pss=pss, c=c, u0=u0, d=d, qsl=qsl, pb=pb, hp=hp: E.matmul(
                                pss[:, c, :], kT[pb:pb + 64, hp, u0:u0 + d * 127 + 1:d], qT[pb:pb + 64, hp, qsl],
                                start=True, stop=True), reads=[S.tok("q"), S.tok("k")], writes=[pst])
                        tm_, tmt = tmp[it % 3], S.tok("tmp", it % 3)
                        S.op("dve", lambda E, pss=pss, tm_=tm_, h=h, di=di: E.scalar_tensor_tensor(
                            out=tm_[:, :, :], in0=pss[:, :, :], scalar=scale, in1=bias[:, h, di, :, :],
                            op0=ALU.mult, op1=ALU.add), reads=[pst, bt], writes=[tmt])
                        P_, Pt = P[it % 3], S.tok("P", it % 3)
                        for c in range(2):
                            ci = tab[(d, r, blk + c)]
                            S.op("act", lambda E, tm_=tm_, P_=P_, c=c, ci=ci: E.activation(
                                out=P_[:, c, :], in_=tm_[:, c, :], func=AF.Exp, bias=kb[:, ci:ci + 1]),
                                reads=[tmt, S.tok("kb")], writes=[Pt])
                        pn, pd = ps_n[it % 2], ps_d[it % 2]
                        pnt, pdt = S.tok("psn", it % 2), S.tok("psd", it % 2)
                        for c in range(2):
                            ci = tab[(d, r, blk + c)]
                            S.op("pe", lambda E, pn=pn, P_=P_, c=c, ci=ci, h=h: E.matmul(
                                pn[:, :], V[:, ci, h * 64:(h + 1) * 64], P_[:, c, :], start=(c == 0), stop=(c == 1)),
                                reads=[Pt, S.tok("v")], writes=[pnt])
                        for c in range(2):
                            S.op("pe", lambda E, pd=pd, P_=P_, c=c: E.matmul(
                                pd[:, :], ones_bf[:, :], P_[:, c, :], start=(c == 0), stop=(c == 1)),
                                reads=[Pt, S.tok("ones")], writes=[pdt])
                        if di == 0:
                            S.op("dve", lambda E, an=an, pn=pn, qsl=qsl: E.tensor_copy(out=an[:, qsl], in_=pn[:, :]),
                                 reads=[pnt], writes=[ant])
                            S.op("act", lambda E, ad=ad, pd=pd, qsl=qsl: E.copy(out=ad[:, qsl], in_=pd[:, :]),
                                 reads=[pdt], writes=[adt])
                        else:
                            S.op("dve", lambda E, an=an, pn=pn, qsl=qsl: E.tensor_tensor(
                                out=an[:, qsl], in0=pn[:, :], in1=an[:, qsl], op=ALU.add), reads=[pnt, ant], writes=[ant])
                            S.op("dve", lambda E, ad=ad, pd=pd, qsl=qsl: E.tensor_tensor(
                                out=ad[:, qsl], in0=pd[:, :], in1=ad[:, qsl], op=ALU.add), reads=[pdt, adt], writes=[adt])
                        it += 1
            og, ogt = ostg[h % 2], S.tok("ostg", h % 2)
            S.op("dve", lambda E, ad=ad: E.reciprocal(out=ad[:, :], in_=ad[:, :]), reads=[adt], writes=[adt])
            S.op("dve", lambda E, an=an, ad=ad, og=og: E.tensor_tensor(out=og[:, :], in0=an[:, :], in1=ad[:, :], op=ALU.mult),
                 reads=[ant, adt], writes=[ogt])
            S.dma("sp", o_out[h * 64:(h + 1) * 64, :], og[:, :], reads=[ogt], writes=[S.tok("fin", h)])
        S.end_stage()


EX_GROUPS = [[0, 1, 2, 3], [4, 5, 6, 7]]


def exch_rows(items):
    r = 0
    for lay, DST, F, H, dt in items:
        nb = F * H * (4 if dt == F32 else 2)
        r += 2 * ((nb // 4 + 511) // 512)
    return r


def _blockview(rows_ap, lay, F, H, dt):
    v = rows_ap if dt == F32 else rows_ap.bitcast(BF16)
    if lay == "fm":
        return v.rearrange("r (a h) -> (r a) h", h=H)
    return v.rearrange("r (a f) -> (r a) f", f=F)


def exch_alloc(nc, items, tag):
    bufs = []
    for n, (lay, DST, F, H, dt) in enumerate(items):
        rows = (F * H * (4 if dt == F32 else 2) // 4 + 511) // 512
        assert rows <= 512
        for side in range(2):
            bufs.append((nc.dram_tensor("EXS_%s_%d_%d" % (tag, n, side), [rows, 512], F32),
                         nc.dram_tensor("EXO_%s_%d_%d" % (tag, n, side), [4 * rows, 512], F32), rows))
    return bufs


def stage_exch(S, nc, items, sel_in, bufs):
    cx = Ctx(nc)
    with cx.stack:
        sel = cx.sb([128, 6], F32, "sel")
        S.dma("sp", sel[:, :], sel_in, writes=[S.tok("sel")])
        bi = 0
        for lay, DST, F, H, dt in items:
            for side in range(2):
                EXS, EXO, rows = bufs[bi]
                t0 = H if side == 0 else TPC
                src = DST[:, t0:t0 + H] if lay == "fm" else DST[t0:t0 + H, :]
                S.dma("sp" if side == 0 else "act", _blockview(EXS.ap(), lay, F, H, dt), src,
                      writes=[S.tok("exs", bi)], allow_slow_non_contiguous=(lay == "fm" and H < 64))
                bi += 1
        for b in range(len(bufs)):
            EXS, EXO, rows = bufs[b]
            S.cc("AllGather", [EXS.ap().opt()], [EXO.ap().opt()], EX_GROUPS, reads=[S.tok("exs", b)], writes=[S.tok("exo", b)])
        bi = 0
        n = 0
        for lay, DST, F, H, dt in items:
            K = (F if lay == "fm" else H) // 128
            W = H if lay == "fm" else F
            pat = "(k p) h -> p k h" if lay == "fm" else "(c p) f -> p c f"
            for side in range(2):
                b = bi + (1 if side == 0 else 0)
                EXS, EXO, rows = bufs[b]
                exo = EXO.ap()
                cands = [cx.sb([128, K, W], dt, "exc%d_%d_%d" % (n, side, j)) for j in range(3)]
                o = cx.sb([128, K, W], dt, "exo%d_%d" % (n, side))
                tt = S.tok("ext", n, side)
                for j in range(3):
                    slot = j + side
                    S.dma(("sp", "act", "sp")[j], cands[j][:, :, :],
                          _blockview(exo[slot * rows:(slot + 1) * rows, :], lay, F, H, dt).rearrange(pat, p=128),
                          reads=[S.tok("exo", b)], writes=[tt], allow_slow_non_contiguous=(W < 64))
                S.op("dve", lambda E, a=cands[0], o=o, side=side: E.tensor_scalar(
                    out=o[:, :, :], in0=a[:, :, :], scalar1=sel[:, 3 * side:3 * side + 1], scalar2=None, op0=ALU.mult),
                    reads=[tt, S.tok("sel")], writes=[tt])
                for j in (1, 2):
                    S.op("dve", lambda E, b_=cands[j], o=o, side=side, j=j: E.scalar_tensor_tensor(
                        out=o[:, :, :], in0=b_[:, :, :], scalar=sel[:, 3 * side + j:3 * side + j + 1], in1=o[:, :, :],
                        op0=ALU.mult, op1=ALU.add), reads=[tt, S.tok("sel")], writes=[tt])
                t0 = 0 if side == 0 else H + TPC
                dst = DST[:, t0:t0 + H] if lay == "fm" else DST[t0:t0 + H, :]
                S.dma("sp", dst.rearrange(pat, p=128), o[:, :, :], reads=[tt], writes=[S.tok("exd", n, side)],
                      allow_slow_non_contiguous=(W < 64))
            bi += 2
            n += 1
        S.end_stage()


def stage_mlstm(S, nc, QKH, cw_in, cb_in, VM, OM, GM, gb_in, hg_in, mf_in, mb_in, STS, STA, MT_out):
    NT = TPC // 128
    I32 = mybir.dt.int32
    cx = Ctx(nc)
    with cx.stack:
        qkb = cx.sb([128, 8, TPC], BF16, "qkb")
        vaug = cx.sb([128, NT, 4, 132], BF16, "vaug")
        SG = cx.sb([128, NT, 512], BF16, "SG")
        Hf = cx.sb([128, NT, 4, 128], F32, "Hf")
        cw = cx.sb([128, 8, 5], F32, "cw")
        cb = cx.sb([128, 8], F32, "cb")
        G = cx.sb([128, NT, 16], F32, "G")
        gb = cx.sb([128, 16], F32, "gb")
        hg = cx.sb([128, 512], F32, "hg")
        MF = cx.sb([128, 2, 8], F32, "MF")
        L = cx.sb([128, 8, NT], F32, "L")
        CB = cx.sb([128, 8, NT], F32, "CB")
        TOT = cx.sb([128, 8, NT], F32, "TOT")
        EB = cx.sb([128, 8, NT], F32, "EB")
        RS = cx.sb([128, 8, NT], F32, "RS")
        EA = cx.sb([128, 8, NT], F32, "EA")
        FF = cx.sb([128, 8, NT], F32, "FF")
        tri_i = cx.sb([128, 128], I32, "tri_i")
        tri = cx.sb([128, 2, 128], F32, "tri")
        onesf = cx.sb([128, 128], F32, "onesf")
        ident = cx.sb([128, 128], BF16, "ident")
        one1 = cx.sb([128, 1], F32, "one1")
        lns = cx.sb([128, 1], F32, "lns")
        epsb = cx.sb([128, 1], F32, "epsb")
        Cst = cx.sb([128, 8, 132], F32, "Cst")
        Cbf = cx.sb([128, 8, 132], BF16, "Cbf")
        sqj = cx.sb([128, 128], F32, "sqj")
        R = 4
        kpp = [cx.sb([128, 128], BF16, "kpp%d" % i) for i in range(R)]
        pbank = [cx.ps([128, 512]) for _ in range(R)]
        ptb = cx.ps([128, R, 128], BF16)
        ptf = cx.ps([128, 4, 128], BF16)

        def consts():
            ct = S.tok("const")
            S.op("pool", lambda E: E.iota(tri_i[:, :], [[1, 128]], base=0, channel_multiplier=-1), writes=[ct])
            S.op("dve", lambda E: E.tensor_copy(out=tri[:, 0, :], in_=tri_i[:, :]), reads=[ct], writes=[ct])
            S.op("dve", lambda E: E.tensor_scalar(out=sqj[:, :], in0=tri[:, 0, :], scalar1=0.0, scalar2=None, op0=ALU.is_equal),
                 reads=[ct], writes=[ct])
            S.op("dve", lambda E: E.tensor_copy(out=ident[:, :], in_=sqj[:, :]), reads=[ct], writes=[ct])
            S.op("dve", lambda E: E.tensor_scalar(out=tri[:, 1, :], in0=tri[:, 0, :], scalar1=0.0, scalar2=None, op0=ALU.is_le),
                 reads=[ct], writes=[ct])
            S.op("dve", lambda E: E.tensor_scalar(out=tri[:, 0, :], in0=tri[:, 0, :], scalar1=0.0, scalar2=None, op0=ALU.is_ge),
                 reads=[ct], writes=[ct])
            S.op("pool", lambda E: E.memset(onesf[:, :], 1.0), writes=[ct])
            S.op("pool", lambda E: E.memset(one1[:, :], 1.0), writes=[ct])
            S.op("pool", lambda E: E.memset(lns[:, :], float(np.log(128.0 ** -0.5))), writes=[ct])
            S.op("pool", lambda E: E.memset(epsb[:, :], EPS), writes=[ct])
            S.op("pool", lambda E: E.memset(Cst[:, :, :], 0.0), writes=[ct])
            S.op("pool", lambda E: E.memset(vaug[:, :, :, 128:132], 1.0), writes=[ct])
            return ct

        ct = consts()
        for tap in range(5):
            S.dma("sp", cw[:, :, tap], cw_in[tap].rearrange("(k p) -> p k", p=128), writes=[S.tok("cw")],
                  allow_slow_non_contiguous=True)
        S.dma("sp", cb[:, :], cb_in.rearrange("(k p) -> p k", p=128), writes=[S.tok("cw")], allow_slow_non_contiguous=True)
        S.dma("sp", G[:, :, :], GM.rearrange("(n p) g -> p n g", p=128), writes=[S.tok("G")])
        S.dma("sp", gb[:, :], gb_in.partition_broadcast(128), writes=[S.tok("gb")])
        S.dma("sp", hg[:, :], hg_in.partition_broadcast(128), writes=[S.tok("hg")])
        S.dma("sp", MF[:, 0, :], mf_in, writes=[S.tok("MF")])
        S.dma("sp", MF[:, 1, :], mb_in, writes=[S.tok("MF")])
        vt = S.tok("vaug")
        for h in range(4):
            S.dma("act", vaug[:, :, h, 0:128], VM[:, h * 128:(h + 1) * 128].rearrange("(n p) e -> p n e", p=128), writes=[vt])
        gt = S.tok("gates")
        for j in range(16):
            S.op("dve", lambda E, j=j: E.tensor_scalar(out=G[:, :, j], in0=G[:, :, j], scalar1=gb[:, j:j + 1], scalar2=None,
                                                       op0=ALU.add), reads=[S.tok("G"), S.tok("gb")], writes=[S.tok("G")])
        S.op("act", lambda E: E.activation(out=L[:, :, :].rearrange("p c n -> p n c"), in_=G[:, :, 8:16], func=AF.Exp, scale=-1.0),
             reads=[S.tok("G")], writes=[gt])
        S.op("act", lambda E: E.activation(out=L[:, :, :], in_=L[:, :, :], func=AF.Ln, bias=one1[:, 0:1]),
             reads=[gt, ct], writes=[gt])
        pcs = pbank[0]
        for dr in range(2):
            S.op("pe", lambda E, dr=dr: E.matmul(pcs[:, dr * 64:(dr + 1) * 64], tri[:, dr, :], L[:, dr * 4:(dr + 1) * 4, :],
                                                 start=True, stop=True), reads=[gt, ct], writes=[S.tok("pb", 0)])
        S.op("dve", lambda E: E.tensor_copy(out=CB[:, :, :], in_=pcs[:, 0:128].rearrange("p (c n) -> p c n", c=8)),
             reads=[S.tok("pb", 0)], writes=[gt])
        pcs1 = pbank[1]
        S.op("pe", lambda E: E.matmul(pcs1[:, 0:128], onesf[:, :], L[:, :, :], start=True, stop=True),
             reads=[gt, ct], writes=[S.tok("pb", 1)])
        S.op("dve", lambda E: E.tensor_copy(out=TOT[:, :, :], in_=pcs1[:, 0:128].rearrange("p (c n) -> p c n", c=8)),
             reads=[S.tok("pb", 1)], writes=[gt])
        S.op("act", lambda E: E.activation(out=EB[:, :, :], in_=CB[:, :, :], func=AF.Exp, scale=-1.0), reads=[gt], writes=[gt])
        S.op("act", lambda E: E.activation(out=FF[:, :, :], in_=TOT[:, :, :], func=AF.Exp, scale=-1.0), reads=[gt], writes=[gt])
        S.op("dve", lambda E: E.tensor_tensor(out=RS[:, :, :], in0=CB[:, :, :], in1=G[:, :, 0:8].rearrange("p n c -> p c n"), op=ALU.add),
             reads=[gt, S.tok("G")], writes=[gt])
        S.op("dve", lambda E: E.tensor_tensor(out=EA[:, :, :], in0=RS[:, :, :], in1=TOT[:, :, :], op=ALU.subtract),
             reads=[gt], writes=[gt])
        S.op("act", lambda E: E.activation(out=RS[:, :, :], in_=RS[:, :, :], func=AF.Exp, bias=lns[:, 0:1]), reads=[gt, ct], writes=[gt])
        S.op("act", lambda E: E.activation(out=EA[:, :, :], in_=EA[:, :, :], func=AF.Exp, bias=lns[:, 0:1]), reads=[gt, ct], writes=[gt])
        qkt = S.tok("qkb")
        sgt = S.tok("SG")
        cx1 = Ctx(nc)
        with cx1.stack:
            xin = [cx1.sb([128, TPC + 4], F32, "xin%d" % i) for i in range(2)]
            cacc = [cx1.sb([128, TPC], F32, "cacc%d" % i) for i in range(2)]
            ostage = [cx1.sb([128, 4, 512], F32, "ostage%d" % i) for i in range(2)]
            qk_v = QKH.rearrange("(k p) t -> p k t", p=128)
            for k in range(8):
                xb, xbt = xin[k % 2], S.tok("xin", k % 2)
                ac, act_ = cacc[k % 2], S.tok("cacc", k % 2)
                S.dma("sp", xb[:, :], qk_v[:, k, :], writes=[xbt])
                S.op("dve", lambda E, k=k, xb=xb, ac=ac: E.tensor_scalar(
                    out=ac[:, :], in0=xb[:, 0:TPC], scalar1=cw[:, k, 0:1], scalar2=None, op0=ALU.mult),
                    reads=[xbt, S.tok("cw")], writes=[act_])
                for tap in range(1, 5):
                    S.op("dve", lambda E, k=k, xb=xb, ac=ac, tap=tap: E.scalar_tensor_tensor(
                        out=ac[:, :], in0=xb[:, tap:tap + TPC], scalar=cw[:, k, tap:tap + 1], in1=ac[:, :],
                        op0=ALU.mult, op1=ALU.add), reads=[xbt, act_], writes=[act_])
                S.op("act", lambda E, k=k, ac=ac: E.activation(out=qkb[:, k, :], in_=ac[:, :], func=AF.Silu, bias=cb[:, k:k + 1]),
                     reads=[act_, S.tok("cw")], writes=[qkt])
            for i in range(4):
                ob, obt = ostage[i % 2], S.tok("ostage", i % 2)
                S.dma("act", ob[:, :, :], OM[i * 512:(i + 1) * 512, :].rearrange("(n p) e -> p n e", p=128), writes=[obt])
                S.op("act", lambda E, i=i, ob=ob: E.activation(out=SG[:, i * 4:(i + 1) * 4, :], in_=ob[:, :, :], func=AF.Sigmoid),
                     reads=[obt], writes=[sgt])
            S.end_stage()
        ct = S.tok("const")
        gt = S.tok("gates")
        qkt = S.tok("qkb")
        sgt = S.tok("SG")
        vt = S.tok("vaug")

        def state_steps(ch, n, slot):
            h = ch % 4
            tsl = slice(n * 128, (n + 1) * 128)
            pb, pbt = pbank[slot], S.tok("pb", slot)
            kp, kpt = kpp[slot], S.tok("kpp", slot)
            S.op("pe", lambda E: E.transpose(ptb[:, slot, :], qkb[:, 4 + h, tsl], ident[:, :]),
                 reads=[qkt, ct], writes=[S.tok("ptb", slot)])
            S.op("act", lambda E: E.activation(out=kp[:, :], in_=ptb[:, slot, :], func=AF.Copy, scale=EA[:, ch, n:n + 1]),
                 reads=[S.tok("ptb", slot), gt], writes=[kpt])
            S.op("pe", lambda E: E.matmul(pb[:, 260:390], kp[:, :], vaug[:, n, h, 0:130], start=True, stop=True),
                 reads=[kpt, vt, ct], writes=[S.tok("pbc", slot)])
            S.op("dve", lambda E: E.scalar_tensor_tensor(
                out=Cst[:, ch, 0:130], in0=Cst[:, ch, 0:130], scalar=FF[:, ch, n:n + 1], in1=pb[:, 260:390],
                op0=ALU.mult, op1=ALU.add), reads=[S.tok("pbc", slot), gt, S.tok("Cst", ch)], writes=[S.tok("Cst", ch)])

        it = 0
        for i in range(NT):
            for ch in range(8):
                n = i if ch < 4 else NT - 1 - i
                state_steps(ch, n, it % R)
                it += 1
        TS = cx.sb([128, 8], F32, "TS")
        S.op("dve", lambda E: E.tensor_reduce(out=TS[:, :], in_=TOT[:, :, :], axis=AX.X, op=ALU.add), reads=[gt], writes=[S.tok("TS")])
        for ch in range(8):
            S.op("dve", lambda E, ch=ch: E.tensor_copy(out=Cst[:, ch, 130:131], in_=TS[:, ch:ch + 1]),
                 reads=[S.tok("TS"), S.tok("Cst", ch)], writes=[S.tok("Cst", ch)])
        S.dma("sp", STS.ap().rearrange("(c p) x -> p c x", p=128), Cst[:, :, :],
              reads=[S.tok("Cst", ch) for ch in range(8)], writes=[S.tok("sts")])
        S.cc("AllGather", [STS.ap().opt()], [STA.ap().opt()], [list(range(8))], reads=[S.tok("sts")], writes=[S.tok("sta")])
        cx2 = Ctx(nc)
        with cx2.stack:
            ST = cx2.sb([128, 8, 8, 132], F32, "ST")
            MT_ = cx2.sb([128, 8, 8], F32, "MT")
            ACC = cx2.sb([128, 8, 8], F32, "ACC")
            COEF = cx2.sb([128, 8, 8], F32, "COEF")
            mstg = cx2.sb([128, 4, TPC], BF16, "mstg")
            WT = [cx2.sb([128, 128], BF16, "WT%d" % i) for i in range(R)]
            sm = [cx2.sb([128, 8], F32, "sm%d" % i) for i in range(R)]
            hs = [cx2.sb([128, 128], F32, "hs%d" % i) for i in range(R)]
            hb = [cx2.sb([128, 128], BF16, "hb%d" % i) for i in range(R)]
            stt = S.tok("ST")
            for r in range(8):
                S.dma("sp" if r % 2 == 0 else "act", ST[:, r, :, :],
                      STA.ap()[r * 1024:(r + 1) * 1024, :].rearrange("(c p) x -> p c x", p=128), reads=[S.tok("sta")], writes=[stt])
            cft = S.tok("coef")
            S.op("pool", lambda E: E.memset(ACC[:, :, :], 0.0), writes=[cft])
            for r in range(8):
                for dr in range(2):
                    S.op("dve", lambda E, r=r, dr=dr: E.tensor_scalar(
                        out=MT_[:, r, dr * 4:(dr + 1) * 4], in0=ST[:, r, dr * 4:(dr + 1) * 4, 130], scalar1=MF[:, dr, r:r + 1],
                        scalar2=None, op0=ALU.mult), reads=[stt, S.tok("MF")], writes=[cft])
            for r in range(6, -1, -1):
                S.op("dve", lambda E, r=r: E.tensor_tensor(out=ACC[:, r, 0:4], in0=ACC[:, r + 1, 0:4], in1=MT_[:, r + 1, 0:4], op=ALU.add),
                     reads=[cft], writes=[cft])
            for r in range(1, 8):
                S.op("dve", lambda E, r=r: E.tensor_tensor(out=ACC[:, r, 4:8], in0=ACC[:, r - 1, 4:8], in1=MT_[:, r - 1, 4:8], op=ALU.add),
                     reads=[cft], writes=[cft])
            S.op("act", lambda E: E.activation(out=COEF[:, :, :], in_=ACC[:, :, :], func=AF.Exp, scale=-1.0), reads=[cft], writes=[cft])
            for r in range(8):
                for dr in range(2):
                    S.op("dve", lambda E, r=r, dr=dr: E.tensor_scalar(
                        out=COEF[:, r, dr * 4:(dr + 1) * 4], in0=COEF[:, r, dr * 4:(dr + 1) * 4], scalar1=MF[:, dr, r:r + 1],
                        scalar2=None, op0=ALU.mult), reads=[cft, S.tok("MF")], writes=[cft])
            for ch in range(8):
                S.op("dve", lambda E, ch=ch: E.tensor_scalar(out=Cst[:, ch, 0:130], in0=ST[:, 0, ch, 0:130], scalar1=COEF[:, 0, ch:ch + 1],
                                                             scalar2=None, op0=ALU.mult),
                     reads=[stt, cft, S.tok("Cst", ch), S.tok("sts")], writes=[S.tok("Cst", ch)])
                for r in range(1, 8):
                    S.op("dve", lambda E, ch=ch, r=r: E.scalar_tensor_tensor(
                        out=Cst[:, ch, 0:130], in0=ST[:, r, ch, 0:130], scalar=COEF[:, r, ch:ch + 1], in1=Cst[:, ch, 0:130],
                        op0=ALU.mult, op1=ALU.add), reads=[stt, cft, S.tok("Cst", ch)], writes=[S.tok("Cst", ch)])
                S.op("act", lambda E, ch=ch: E.copy(out=Cbf[:, ch, 0:130], in_=Cst[:, ch, 0:130]),
                     reads=[S.tok("Cst", ch)], writes=[S.tok("Cbf", ch)])
            it = 0
            for dr in range(2):
                for i in range(NT):
                    n = i if dr == 0 else NT - 1 - i
                    tsl = slice(n * 128, (n + 1) * 128)
                    for h in range(4):
                        ch = dr * 4 + h
                        slot = it % R
                        pb = pbank[slot]
                        W_, Wt = WT[slot], S.tok("WT", slot)
                        s_, st = sm[slot], S.tok("sm", slot)
                        S.op("pe", lambda E, pb=pb, h=h, tsl=tsl: E.matmul(pb[:, 0:128], qkb[:, 4 + h, tsl], qkb[:, h, tsl],
                                                                         start=True, stop=True), reads=[qkt], writes=[S.tok("pbs", slot)])
                        S.op("dve", lambda E, pb=pb, W_=W_, dr=dr, ch=ch, n=n: E.scalar_tensor_tensor(
                            out=W_[:, :], in0=pb[:, 0:128], scalar=RS[:, ch, n:n + 1], in1=tri[:, dr, :], op0=ALU.mult, op1=ALU.mult),
                            reads=[S.tok("pbs", slot), gt, ct], writes=[Wt])
                        S.op("pe", lambda E, pb=pb, W_=W_, n=n, h=h: E.matmul(pb[:, 128:258], W_[:, :], vaug[:, n, h, 0:130],
                                                                           start=True, stop=False), reads=[Wt, vt, ct], writes=[S.tok("pbo", slot)])
                        S.op("pe", lambda E, pb=pb, h=h, tsl=tsl, ch=ch: E.matmul(pb[:, 128:258], qkb[:, h, tsl], Cbf[:, ch, 0:130],
                                                                              start=False, stop=True),
                             reads=[qkt, S.tok("Cbf", ch)], writes=[S.tok("pbo", slot)])
                        S.op("act", lambda E, pb=pb, s_=s_, ch=ch, n=n: E.activation(
                            out=s_[:, 0:1], in_=pb[:, 256:257], func=AF.Abs, scale=EB[:, ch, n:n + 1]), reads=[S.tok("pbo", slot), gt], writes=[st])
                        S.op("dve", lambda E, s_=s_: E.tensor_scalar(out=s_[:, 1:2], in0=s_[:, 0:1], scalar1=1.0, scalar2=None, op0=ALU.max),
                             reads=[st], writes=[st])
                        S.op("dve", lambda E, s_=s_: E.reciprocal(out=s_[:, 2:3], in_=s_[:, 1:2]), reads=[st], writes=[st])
                        S.op("dve", lambda E, s_=s_, ch=ch, n=n: E.tensor_tensor(out=s_[:, 3:4], in0=s_[:, 2:3], in1=EB[:, ch, n:n + 1], op=ALU.mult),
                             reads=[st, gt], writes=[st])
                        if dr == 0:
                            S.op("act", lambda E, pb=pb, s_=s_, n=n, h=h: E.activation(
                                out=Hf[:, n, h, :], in_=pb[:, 128:256], func=AF.Copy, scale=s_[:, 3:4]),
                                reads=[S.tok("pbo", slot), st], writes=[S.tok("Hf", n, h)])
                        else:
                            h_, ht = hs[slot], S.tok("hs", slot)
                            hb_, hbt = hb[slot], S.tok("hb", slot)
                            S.op("dve", lambda E, pb=pb, s_=s_, n=n, h=h, h_=h_: E.scalar_tensor_tensor(
                                out=h_[:, :], in0=pb[:, 128:256], scalar=s_[:, 3:4], in1=Hf[:, n, h, :], op0=ALU.mult, op1=ALU.add),
                                reads=[S.tok("pbo", slot), st, S.tok("Hf", n, h)], writes=[ht])
                            S.op("act", lambda E, h_=h_, s_=s_: E.activation(out=sqj[:, :], in_=h_[:, :], func=AF.Square, accum_out=s_[:, 4:5]),
                                 reads=[ht, st], writes=[st, S.tok("sqj")])
                            S.op("act", lambda E, s_=s_: E.activation(out=s_[:, 5:6], in_=s_[:, 4:5], func=AF.Sqrt, scale=1.0 / 128,
                                                                    bias=epsb[:, 0:1]), reads=[st, ct], writes=[st])
                            S.op("dve", lambda E, s_=s_: E.reciprocal(out=s_[:, 6:7], in_=s_[:, 5:6]), reads=[st], writes=[st])
                            S.op("dve", lambda E, h_=h_, s_=s_, h=h: E.scalar_tensor_tensor(
                                out=h_[:, :], in0=h_[:, :], scalar=s_[:, 6:7], in1=hg[:, h * 128:(h + 1) * 128], op0=ALU.mult, op1=ALU.mult),
                                reads=[ht, st, S.tok("hg")], writes=[ht])
                            S.op("dve", lambda E, h_=h_, hb_=hb_, n=n, h=h: E.tensor_tensor(
                                out=hb_[:, :], in0=h_[:, :], in1=SG[:, n, h * 128:(h + 1) * 128], op=ALU.mult),
                                reads=[ht, sgt], writes=[hbt])
                            S.op("pe", lambda E, hb_=hb_, h=h: E.transpose(ptf[:, h, :], hb_[:, :], ident[:, :]),
                                 reads=[hbt, ct], writes=[S.tok("ptf", h)])
                            S.op("act", lambda E, h=h, tsl=tsl: E.copy(out=mstg[:, h, tsl], in_=ptf[:, h, :]),
                                 reads=[S.tok("ptf", h)], writes=[S.tok("mstg", h)])
                        state_steps(ch, n, slot)
                        S.op("act", lambda E, ch=ch: E.copy(out=Cbf[:, ch, 0:130], in_=Cst[:, ch, 0:130]),
                             reads=[S.tok("Cst", ch)], writes=[S.tok("Cbf", ch)])
                        it += 1
            S.dma("sp", MT_out.rearrange("(h p) t -> p h t", p=128), mstg[:, :, :], reads=[S.tok("mstg", h) for h in range(4)],
                  writes=[S.tok("fin")])
            S.end_stage()


def build_fused(n_layers=4, debug_outs=()):
    nc = bass.Bass("TRN2", target_bir_lowering=False)
    dt_ = nc.dram_tensor
    x_in = dt_("x", [D, TPC], F32, kind="ExternalInput").ap()
    NL = n_layers
    NE, NO = (NL + 1) // 2, max(1, NL // 2)
    norm_g = dt_("norm_g", [NL, 3, D], F32, kind="ExternalInput").ap()
    w1 = dt_("ffn_w1", [NL, 2, D, DFF], F32, kind="ExternalInput").ap()
    w3 = dt_("ffn_w3", [NL, 2, D, DFF], F32, kind="ExternalInput").ap()
    w2 = dt_("ffn_w2", [NL, 2, DFF, D], F32, kind="ExternalInput").ap()
    ab_w_in = dt_("ab_w_in", [NE, D, 3600], F32, kind="ExternalInput").ap()
    ab_conv_w = dt_("ab_conv_w", [NE, 5, 1024], F32, kind="ExternalInput").ap()
    ab_conv_b = dt_("ab_conv_b", [NE, 1024], F32, kind="ExternalInput").ap()
    ab_gate_b = dt_("ab_gate_b", [NE, 16], F32, kind="ExternalInput").ap()
    ab_hnorm_g = dt_("ab_hnorm_g", [NE, 512], F32, kind="ExternalInput").ap()
    ab_w_out = dt_("ab_w_out", [NE, D, D], F32, kind="ExternalInput").ap()
    c_w_in = dt_("c_w_in", [NO, D, 1536], F32, kind="ExternalInput").ap()
    c_sink = dt_("c_sink", [NO, 16], F32, kind="ExternalInput").ap()
    c_w_out = dt_("c_w_out", [NO, D, D], F32, kind="ExternalInput").ap()
    final_g = dt_("final_g", [D], F32, kind="ExternalInput").ap()
    sel_in = dt_("sel", [128, 6], F32, kind="ExternalInput").ap()
    kbt_b = dt_("kbt_b", [128, 69], F32, kind="ExternalInput").ap()
    kb_c = dt_("kb_c", [128, 18], F32, kind="ExternalInput").ap()
    mf_in = dt_("mf", [128, 8], F32, kind="ExternalInput").ap()
    mb_in = dt_("mb", [128, 8], F32, kind="ExternalInput").ap()
    y_out = dt_("y", [D, TPC], F32, kind="ExternalOutput").ap()
    X = dt_("X_s", [D, TPC], F32).ap()
    MIXT = dt_("MIXT_s", [D, TPC], BF16).ap()
    QKH = dt_("QKH_s", [1024, TPC + 4], F32).ap()
    VM = dt_("VM_s", [TPC, 512], BF16).ap()
    OM = dt_("OM_s", [TPC, 512], F32).ap()
    GM = dt_("GM_s", [TPC, 16], F32).ap()
    QB = dt_("QB_s", [512, TPC], BF16).ap()
    KBH = dt_("KBH_s", [512, TPC + 2 * B_HALO], BF16).ap()
    VBH = dt_("VBH_s", [TPC + 2 * B_HALO, 512], BF16).ap()
    QC = dt_("QC_s", [1024, TPC], BF16).ap()
    KCH = dt_("KCH_s", [256, TPC + 256], BF16).ap()
    VCH = dt_("VCH_s", [TPC + 256, 256], BF16).ap()
    STS = dt_("STS_s", [8 * 128, 132], F32)
    STA = dt_("STA_s", [64 * 128, 132], F32)
    items_ab = [("fm", KBH, 512, B_HALO, BF16), ("tm", VBH, 512, B_HALO, BF16), ("fm", QKH, 1024, 2, F32)]
    items_c = [("fm", KCH, 256, 128, BF16), ("tm", VCH, 256, 128, BF16)]
    bufs_ab = exch_alloc(nc, items_ab, "ab")
    bufs_c = exch_alloc(nc, items_c, "c")
    dbg = {}
    for name in debug_outs:
        src = {"X": X, "MIXT": MIXT, "KBH": KBH, "VBH": VBH, "QKH": QKH, "KCH": KCH, "VCH": VCH, "QB": QB, "QC": QC}[name]
        dbg[name] = (dt_("dbg_" + name, list(src.shape), src.dtype, kind="ExternalOutput").ap(), src)
    gstack = contextlib.ExitStack()
    with gstack:
        S = Sched(nc, gstack)
        with nc.Block() as block:
            S.block = block
            S.dma("sp", X, x_in, writes=[S.tok("X")])
            S.end_stage()
            import os
            skip = set(os.environ.get("FUSED_SKIP", "").split(","))
            for l in range(n_layers):
                j = l // 2
                if "ffn" not in skip:
                    stage_ffn(S, nc, X, norm_g[l, 0], w1[l, 0], w3[l, 0], w2[l, 0])
                if l % 2 == 0:
                    outs = {"qkT": QKH[:, 2:2 + TPC], "Vm": VM, "Om": OM, "Gm": GM, "qbT": QB,
                            "kbT": KBH[:, B_HALO:B_HALO + TPC], "Vb": VBH[B_HALO:B_HALO + TPC, :]}
                    if "proj" not in skip:
                        stage_proj(S, nc, "AB", X, norm_g[l, 1], ab_w_in[j], outs)
                    if "exch" not in skip:
                        stage_exch(S, nc, items_ab, sel_in, bufs_ab)
                    if "mlstm" not in skip:
                        stage_mlstm(S, nc, QKH, ab_conv_w[j], ab_conv_b[j], VM, OM, GM, ab_gate_b[j], ab_hnorm_g[j],
                                    mf_in, mb_in, STS, STA, MIXT[0:512, :])
                    if "attnb" not in skip:
                        stage_attn_b(S, nc, QB, KBH, VBH, kbt_b, MIXT[512:1024, :])
                    if "out" not in skip:
                        stage_out(S, nc, X, MIXT, ab_w_out[j])
                    if "ffn" in skip:
                        continue
                else:
                    outs = {"qT": QC, "kT": KCH[:, 128:128 + TPC], "V": VCH[128:128 + TPC, :]}
                    stage_proj(S, nc, "C", X, norm_g[l, 1], c_w_in[j], outs)
                    stage_exch(S, nc, items_c, sel_in, bufs_c)
                    stage_attn_c(S, nc, QC, KCH, VCH, kb_c, c_sink[j], MIXT)
                    stage_out(S, nc, X, MIXT, c_w_out[j])
                stage_ffn(S, nc, X, norm_g[l, 2], w1[l, 1], w3[l, 1], w2[l, 1])
            stage_final(S, nc, X, final_g, y_out)
            for name, (dst, src) in dbg.items():
                S.dma("sp", dst, src, writes=[S.tok("dbg", name)])
            S.end_stage()
    return nc


def fused_core_inputs(c):
    qd = c % 4
    sel = np.zeros(6, np.float32)
    if qd > 0:
        sel[qd - 1] = 1.0
    if qd < 3:
        sel[3 + qd] = 1.0
    tab = b_chunks()
    valid = _kvalid(c, B_HALO)
    kbt = np.zeros((128, len(tab)), np.float32)
    for (d, r, m), idx in tab.items():
        u = B_HALO + r + d * (128 * m - 64 + np.arange(128))
        kbt[:, idx] = valid[u]
    kbc = np.ascontiguousarray(_kvalid(c, 128).reshape(18, 128).T)
    mf = np.array([1.0 if (r // 4 == c // 4 and r < c) else 0.0 for r in range(8)], np.float32)
    mb = np.array([1.0 if (r // 4 == c // 4 and r > c) else 0.0 for r in range(8)], np.float32)
    return {"sel": np.ascontiguousarray(np.broadcast_to(sel, (128, 6))), "kbt_b": kbt, "kb_c": kbc,
            "mf": np.ascontiguousarray(np.broadcast_to(mf, (128, 8))), "mb": np.ascontiguousarray(np.broadcast_to(mb, (128, 8)))}


def kernel_fused(x, norm_g, ffn_w1, ffn_w3, ffn_w2, ab_w_in, ab_conv_w, ab_conv_b, ab_gate_b, ab_hnorm_g, ab_w_out,
                 c_w_in, c_sink, c_w_out, final_g, n_layers=4, debug_outs=()):
    nc = _get("fused", build_fused, n_layers, tuple(debug_outs))
    NL = n_layers
    NE, NO = (NL + 1) // 2, max(1, NL // 2)
    shared = {"norm_g": _f32c(norm_g[:NL]), "ffn_w1": _f32c(ffn_w1[:NL]), "ffn_w3": _f32c(ffn_w3[:NL]),
              "ffn_w2": _f32c(ffn_w2[:NL]), "ab_w_in": _f32c(ab_w_in[:NE]), "ab_conv_w": _f32c(ab_conv_w[:NE]),
              "ab_conv_b": _f32c(ab_conv_b[:NE]), "ab_gate_b": _f32c(ab_gate_b).reshape(2, 16)[:NE],
              "ab_hnorm_g": _f32c(ab_hnorm_g[:NE]), "ab_w_out": _f32c(ab_w_out[:NE]), "c_w_in": _f32c(c_w_in[:NO]),
              "c_sink": _f32c(c_sink[:NO]), "c_w_out": _f32c(c_w_out[:NO]), "final_g": _f32c(final_g)}
    x = np.asarray(x, dtype=np.float32)
    in_maps = []
    for c in range(NCORES):
        m = dict(shared)
        m["x"] = np.ascontiguousarray(x[c // 4, (c % 4) * TPC:(c % 4 + 1) * TPC].T)
        m.update(fused_core_inputs(c))
        in_maps.append(m)
    res = run_bass_kernel_spmd(nc, in_maps, core_ids=list(range(NCORES)))
    out = np.empty((BATCH, SEQ, D), np.float32)
    for c in range(NCORES):
        out[c // 4, (c % 4) * TPC:(c % 4 + 1) * TPC] = np.asarray(res.results[c]["y"]).T
    if debug_outs:
        return out, res.results
    return out


def kernel_fused_n(x, norm_g, ffn_w1, ffn_w3, ffn_w2, ab_w_in, ab_conv_w, ab_conv_b, ab_gate_b, ab_hnorm_g, ab_w_out,
                   c_w_in, c_sink, c_w_out, final_g, layers_per_launch=2):
    LP = layers_per_launch
    nc = _get("fused", build_fused, LP, ("X",))
    core_c = [fused_core_inputs(c) for c in range(NCORES)]
    xs = [np.ascontiguousarray(x[c // 4, (c % 4) * TPC:(c % 4 + 1) * TPC].T) for c in range(NCORES)]
    res = None
    for l0 in range(0, 4, LP):
        ls = slice(l0, l0 + LP)
        js = slice(l0 // 2, l0 // 2 + (LP + 1) // 2)
        jo = slice(l0 // 2, l0 // 2 + max(1, LP // 2))
        shared = {"norm_g": _f32c(norm_g[ls]), "ffn_w1": _f32c(ffn_w1[ls]), "ffn_w3": _f32c(ffn_w3[ls]),
                  "ffn_w2": _f32c(ffn_w2[ls]), "ab_w_in": _f32c(ab_w_in[js]), "ab_conv_w": _f32c(ab_conv_w[js]),
                  "ab_conv_b": _f32c(ab_conv_b[js]), "ab_gate_b": _f32c(ab_gate_b).reshape(2, 16)[js],
                  "ab_hnorm_g": _f32c(ab_hnorm_g[js]), "ab_w_out": _f32c(ab_w_out[js]), "c_w_in": _f32c(c_w_in[jo]),
                  "c_sink": _f32c(c_sink[jo]), "c_w_out": _f32c(c_w_out[jo]), "final_g": _f32c(final_g)}
        in_maps = []
        for c in range(NCORES):
            m = dict(shared)
            m["x"] = xs[c]
            m.update(core_c[c])
            in_maps.append(m)
        res = run_bass_kernel_spmd(nc, in_maps, core_ids=list(range(NCORES))).results
        xs = [np.ascontiguousarray(np.asarray(res[c]["dbg_X"])) for c in range(NCORES)]
    out = np.empty((BATCH, SEQ, D), np.float32)
    for c in range(NCORES):
        out[c // 4, (c % 4) * TPC:(c % 4 + 1) * TPC] = np.asarray(res[c]["y"]).T
    return out


def kernel(x, norm_g, ffn_w1, ffn_w3, ffn_w2, ab_w_in, ab_conv_w, ab_conv_b, ab_gate_b, ab_hnorm_g, ab_w_out,
           c_w_in, c_sink, c_w_out, final_g):
    return kernel_unfused(x, norm_g, ffn_w1, ffn_w3, ffn_w2, ab_w_in, ab_conv_w, ab_conv_b, ab_gate_b, ab_hnorm_g,
                          ab_w_out, c_w_in, c_sink, c_w_out, final_g)
```
